# Optimizing a Trainium2 kernel written in Bass

```python
import math
import jax, jax.numpy as jnp
from jax import lax
import numpy as np

D_MODEL = 1024
BATCH = 8
SEQ = 4096
DEPTH = 4

BLOCK = 128
EPS = 1e-6
NEG = -1e30

MLA_HEADS = 8
MLA_Q_RANK = 256
MLA_KV_RANK = 128
MLA_NOPE = 64
MLA_ROPE = 32
MLA_V = 64
MLA_WIDTH = MLA_HEADS * MLA_V
ROPE_THETA = 10000.0

SWA_HEADS = 8
SWA_KV_HEADS = 2
SWA_GROUP = SWA_HEADS // SWA_KV_HEADS
SWA_HEAD_DIM = 64
SWA_WINDOW = 128
SWA_WIDTH = SWA_HEADS * SWA_HEAD_DIM
SWA_KV_WIDTH = SWA_KV_HEADS * SWA_HEAD_DIM

DIFF_HEADS = 4
DIFF_HEAD_DIM = 64
DIFF_WIDTH = DIFF_HEADS * 2 * DIFF_HEAD_DIM

REL_BUCKETS = 32
REL_MAX_DIST = 128
REL_HEADS = SWA_HEADS + DIFF_HEADS

N_BRANCH = 3

IN_SPLITS = (MLA_Q_RANK, MLA_KV_RANK, MLA_ROPE, MLA_WIDTH,
             SWA_WIDTH, SWA_KV_WIDTH, SWA_KV_WIDTH, SWA_WIDTH,
             DIFF_WIDTH, DIFF_WIDTH, DIFF_WIDTH, DIFF_WIDTH,
             N_BRANCH * D_MODEL)
IN_COLS = sum(IN_SPLITS)

kernel_name = "hybrid_mla_swa_diff_encoder"


def rms_norm(x, g):
    xf = x.astype(jnp.float32)
    y = xf * lax.rsqrt(jnp.mean(xf * xf, axis=-1, keepdims=True) + EPS)
    return (y * g.astype(jnp.float32)).astype(x.dtype)


def split_cols(t, sizes):
    out, o = [], 0
    for n in sizes:
        out.append(t[..., o:o + n])
        o += n
    return out


def rope(x, cos, sin):
    half = x.shape[-1] // 2
    x1, x2 = x[..., :half], x[..., half:]
    cos = cos.astype(x.dtype)
    sin = sin.astype(x.dtype)
    return jnp.concatenate([x1 * cos - x2 * sin, x2 * cos + x1 * sin], axis=-1)


def t5_bucket(rel):
    nb = REL_BUCKETS // 2
    max_exact = nb // 2
    base = jnp.where(rel > 0, nb, 0)
    n = jnp.abs(rel)
    nf = jnp.maximum(n, 1).astype(jnp.float32)
    large = max_exact + (jnp.log(nf / max_exact) / math.log(REL_MAX_DIST / max_exact)
                         * (nb - max_exact)).astype(jnp.int32)
    large = jnp.minimum(large, nb - 1)
    return base + jnp.where(n < max_exact, n, large)


def mla_branch(c_q, c_kv, k_r, g_q, w_uq, g_kv, w_ukv, cos, sin):
    B, S, _ = c_q.shape
    q = (rms_norm(c_q, g_q) @ w_uq).reshape(B, S, MLA_HEADS, MLA_NOPE + MLA_ROPE)
    q = jnp.concatenate([q[..., :MLA_NOPE],
                         rope(q[..., MLA_NOPE:], cos[None, :, None, :], sin[None, :, None, :])], axis=-1)
    kv = (rms_norm(c_kv, g_kv) @ w_ukv).reshape(B, S, MLA_HEADS, MLA_NOPE + MLA_V)
    k_pe = rope(k_r, cos[None], sin[None])
    k = jnp.concatenate([kv[..., :MLA_NOPE],
                         jnp.broadcast_to(k_pe[:, :, None, :], (B, S, MLA_HEADS, MLA_ROPE))], axis=-1)
    v = kv[..., MLA_NOPE:]
    scale = (MLA_NOPE + MLA_ROPE) ** -0.5

    def attend(qb):
        s = jnp.einsum('bqhd,bkhd->bhqk', qb, k).astype(jnp.float32) * scale
        p = jax.nn.softmax(s, axis=-1).astype(v.dtype)
        return jnp.einsum('bhqk,bkhd->bqhd', p, v)

    nq = S // BLOCK
    qb = q.reshape(B, nq, BLOCK, MLA_HEADS, MLA_NOPE + MLA_ROPE).swapaxes(0, 1)
    o = lax.map(attend, qb)
    return o.swapaxes(0, 1).reshape(B, S, MLA_WIDTH)


def swa_branch(q, k, v, sink, bias_band):
    B, S, _ = q.shape
    nb = S // BLOCK
    q = q.reshape(B, nb, BLOCK, SWA_KV_HEADS, SWA_GROUP, SWA_HEAD_DIM)

    def band(t):
        t = t.reshape(B, S, SWA_KV_HEADS, SWA_HEAD_DIM)
        tp = jnp.pad(t, ((0, 0), (BLOCK, BLOCK), (0, 0), (0, 0)))
        tp = tp.reshape(B, nb + 2, BLOCK, SWA_KV_HEADS, SWA_HEAD_DIM)
        return jnp.concatenate([tp[:, :-2], tp[:, 1:-1], tp[:, 2:]], axis=2)

    kb, vb = band(k), band(v)
    s = jnp.einsum('bnqgrd,bnkgd->bngrqk', q, kb).astype(jnp.float32) * SWA_HEAD_DIM ** -0.5
    s = s + bias_band.reshape(SWA_KV_HEADS, SWA_GROUP, BLOCK, 3 * BLOCK)[None, None]
    t = jnp.arange(BLOCK)
    j = jnp.arange(3 * BLOCK)
    blk = jnp.arange(nb)
    rel = j[None, :] - BLOCK - t[:, None]
    kpos = blk[:, None] * BLOCK - BLOCK + j[None, :]
    valid = (jnp.abs(rel) <= SWA_WINDOW)[None] & ((kpos >= 0) & (kpos < S))[:, None, :]
    s = jnp.where(valid[None, :, None, None], s, NEG)
    sink_l = sink.astype(jnp.float32).reshape(SWA_KV_HEADS, SWA_GROUP)[None, None, :, :, None, None]
    m = jnp.maximum(jnp.max(s, axis=-1, keepdims=True), sink_l)
    p = jnp.exp(s - m)
    p = p / (jnp.sum(p, axis=-1, keepdims=True) + jnp.exp(sink_l - m))
    o = jnp.einsum('bngrqk,bnkgd->bnqgrd', p.astype(v.dtype), vb)
    return o.reshape(B, S, SWA_WIDTH)


def diff_branch(q, k, v, lam_q1, lam_k1, lam_q2, lam_k2, g_sub, rel_bias, lam_init):
    B, S, _ = q.shape
    H, d = DIFF_HEADS, DIFF_HEAD_DIM
    q = q.reshape(B, S, H, 2, d)
    k = k.reshape(B, S, H, 2, d)
    v = v.reshape(B, S, H, 2 * d)
    f32 = jnp.float32
    lam = (jnp.exp(jnp.sum(lam_q1.astype(f32) * lam_k1.astype(f32)))
           - jnp.exp(jnp.sum(lam_q2.astype(f32) * lam_k2.astype(f32))) + lam_init)
    table = rel_bias[:, SWA_HEADS:].astype(f32)
    kpos = jnp.arange(S)

    def attend(args):
        i, qb = args
        qpos = i * BLOCK + jnp.arange(BLOCK)
        bias = table[t5_bucket(kpos[None, :] - qpos[:, None])].transpose(2, 0, 1)
        s = jnp.einsum('bqhcd,bkhcd->bchqk', qb, k).astype(f32) * d ** -0.5 + bias[None, None]
        p = jax.nn.softmax(s, axis=-1)
        a = p[:, 0] - lam * p[:, 1]
        return jnp.einsum('bhqk,bkhe->bqhe', a.astype(v.dtype), v)

    nq = S // BLOCK
    qb = q.reshape(B, nq, BLOCK, H, 2, d).swapaxes(0, 1)
    o = lax.map(attend, (jnp.arange(nq), qb))
    o = o.swapaxes(0, 1).reshape(B, S, H, 2 * d)
    o = rms_norm(o, g_sub) * (1.0 - lam_init)
    return o.reshape(B, S, DIFF_WIDTH)


def setup_inputs(seed: int = 0) -> dict:
    key = jax.random.key(seed)
    ks = jax.random.split(key, 24)
    D, L = D_MODEL, DEPTH
    nrm = jax.random.normal
    f32 = jnp.float32
    return {
        "x": nrm(ks[0], (BATCH, SEQ, D), f32),
        "c": nrm(ks[1], (BATCH, D), f32),
        "w_ada": nrm(ks[2], (L, D, 3 * D), f32) * (0.5 * D ** -0.5),
        "b_ada": nrm(ks[3], (L, 3 * D), f32) * 0.01,
        "g_pre": 1.0 + 0.1 * nrm(ks[4], (L, D), f32),
        "g_post": 1.0 + 0.1 * nrm(ks[5], (L, D), f32),
        "w_in": nrm(ks[6], (L, D, IN_COLS), f32) * D ** -0.5,
        "g_q": 1.0 + 0.1 * nrm(ks[7], (L, MLA_Q_RANK), f32),
        "w_uq": nrm(ks[8], (L, MLA_Q_RANK, MLA_HEADS * (MLA_NOPE + MLA_ROPE)), f32) * MLA_Q_RANK ** -0.5,
        "g_kv": 1.0 + 0.1 * nrm(ks[9], (L, MLA_KV_RANK), f32),
        "w_ukv": nrm(ks[10], (L, MLA_KV_RANK, MLA_HEADS * (MLA_NOPE + MLA_V)), f32) * MLA_KV_RANK ** -0.5,
        "sink": nrm(ks[11], (L, SWA_HEADS), f32) * 0.5,
        "lam_q1": nrm(ks[12], (L, DIFF_HEAD_DIM), f32) * 0.1,
        "lam_k1": nrm(ks[13], (L, DIFF_HEAD_DIM), f32) * 0.1,
        "lam_q2": nrm(ks[14], (L, DIFF_HEAD_DIM), f32) * 0.1,
        "lam_k2": nrm(ks[15], (L, DIFF_HEAD_DIM), f32) * 0.1,
        "g_sub": 1.0 + 0.1 * nrm(ks[16], (L, 2 * DIFF_HEAD_DIM), f32),
        "w_o_mla": nrm(ks[17], (L, MLA_WIDTH, D), f32) * MLA_WIDTH ** -0.5,
        "w_o_swa": nrm(ks[18], (L, SWA_WIDTH, D), f32) * SWA_WIDTH ** -0.5,
        "w_o_diff": nrm(ks[19], (L, DIFF_WIDTH, D), f32) * DIFF_WIDTH ** -0.5,
        "w_out": nrm(ks[20], (L, D, D), f32) * D ** -0.5,
        "rel_bias": nrm(ks[21], (REL_BUCKETS, REL_HEADS), f32) * 0.5,
    }


def reference(x, c, w_ada, b_ada, g_pre, g_post, w_in, g_q, w_uq, g_kv, w_ukv, sink,
              lam_q1, lam_k1, lam_q2, lam_k2, g_sub, w_o_mla, w_o_swa, w_o_diff, w_out, rel_bias):
    B, S, D = x.shape
    pos = jnp.arange(S, dtype=jnp.float32)
    inv = ROPE_THETA ** (-jnp.arange(0, MLA_ROPE, 2, dtype=jnp.float32) / MLA_ROPE)
    ang = pos[:, None] * inv[None, :]
    cos, sin = jnp.cos(ang), jnp.sin(ang)
    rel_band = jnp.arange(3 * BLOCK)[None, :] - BLOCK - jnp.arange(BLOCK)[:, None]
    swa_bias = rel_bias[t5_bucket(rel_band)][..., :SWA_HEADS].astype(jnp.float32).transpose(2, 0, 1)
    c_act = jax.nn.silu(c)

    for l in range(DEPTH):
        mod = c_act @ w_ada[l] + b_ada[l]
        shift, scale, gate = [m[:, None, :] for m in split_cols(mod, (D, D, D))]
        h = rms_norm(x, g_pre[l]) * (1.0 + scale) + shift
        (cq, ckv, kr, za, qb_, kb_, vb_, zb, qc, kc, vc, zc, gcols) = split_cols(h @ w_in[l], IN_SPLITS)

        o_a = (mla_branch(cq, ckv, kr, g_q[l], w_uq[l], g_kv[l], w_ukv[l], cos, sin)
               * jax.nn.silu(za)) @ w_o_mla[l]
        o_b = (swa_branch(qb_, kb_, vb_, sink[l], swa_bias) * jax.nn.silu(zb)) @ w_o_swa[l]
        lam_init = 0.8 - 0.6 * math.exp(-0.3 * l)
        o_c = (diff_branch(qc, kc, vc, lam_q1[l], lam_k1[l], lam_q2[l], lam_k2[l], g_sub[l],
                           rel_bias, lam_init) * jax.nn.silu(zc)) @ w_o_diff[l]

        g = jax.nn.sigmoid(gcols).reshape(B, S, N_BRANCH, D)
        y = (g[:, :, 0] * o_a + g[:, :, 1] * o_b + g[:, :, 2] * o_c) @ w_out[l]
        x = x + gate * rms_norm(y, g_post[l])
    return x
```

```python
import contextlib
import math
import numpy as np
import concourse.bass as bass
import concourse.mybir as mybir
from concourse.bass_utils import run_bass_kernel_spmd

F32 = mybir.dt.float32
BF16 = mybir.dt.bfloat16
AF = mybir.ActivationFunctionType
ALU = mybir.AluOpType

S = 4096
D = 1024
DEPTH = 4
NTB = 32
NTC = 8
EPS = 1e-6
NCORES = 8
WA_COLS = 1056
WB_COLS = 6272
MLA_SCALE = 96 ** -0.5

ENGS = ("pe", "act", "dve", "pool", "sp")
SAME_ENGINE_RAW = ("act", "dve", "pool")


class Res:
    __slots__ = ("name", "w", "r", "persistent")

    def __init__(self, name, persistent=False):
        self.name = name
        self.w = None
        self.r = {}
        self.persistent = persistent


class DSem:
    def __init__(self, sem, persistent=False):
        self.sem = sem
        self.count = 0
        self.persistent = persistent


class Prog:
    def __init__(self, nc, stack, n_eng_sems=30, n_dma_sems=70):
        self.nc = nc
        self.streams = {e: [] for e in ENGS}
        self.free_sems = [stack.enter_context(nc.semaphore(f"s{i}")) for i in range(n_eng_sems)]
        self.free_dsems = [stack.enter_context(nc.semaphore(f"d{i}")) for i in range(n_dma_sems)]
        self.dsems = []
        self.sem, self.cnt, self.last_sig = {}, {}, {}
        for e in ENGS:
            self._fresh(e)
        self.waited = {e: {} for e in ENGS}
        self.pending = {e: [] for e in ENGS}
        self.all_res = []

    def _fresh(self, e):
        self.sem[e] = self.free_sems.pop(0)
        self.cnt[e] = 0
        self.last_sig[e] = None

    def res(self, name, persistent=False):
        r = Res(name, persistent)
        self.all_res.append(r)
        return r

    def dsem(self, persistent=False):
        d = DSem(self.free_dsems.pop(0), persistent)
        self.dsems.append(d)
        return d

    def _deps(self, reads, writes):
        d = []
        for r in reads:
            if r.w is not None:
                d.append(r.w)
        for w in writes:
            if w.w is not None:
                d.append(w.w)
            d.extend(w.r.values())
        return d

    def _commit(self, ev, reads, writes):
        for r in reads:
            r.r[id(ev[0])] = ev
        for w in writes:
            w.w = ev
            w.r = {}

    def _waits(self, eng, deps, raw):
        waits = []
        for ev in deps:
            sem, val, src = ev
            if src == eng and (eng not in SAME_ENGINE_RAW or not any(ev is x for x in raw)):
                continue
            key = id(sem)
            if self.waited[eng].get(key, 0) >= val:
                continue
            self.waited[eng][key] = val
            waits.append((sem, val))
        return waits

    def op(self, eng, fn, reads=(), writes=(), signal=True):
        raw = [r.w for r in reads if r.w is not None]
        alld = self._deps(reads, writes) + self.pending[eng]
        self.pending[eng] = []
        waits = self._waits(eng, alld, raw)
        sem = self.sem[eng]
        if signal:
            self.cnt[eng] += 1
            ev = (sem, self.cnt[eng], eng)
            self.last_sig[eng] = ev
        else:
            ev = (sem, self.cnt[eng] + 1, eng)

        def closure(e, fn=fn, waits=waits, sem=sem, signal=signal):
            for s, v in waits:
                e.wait_ge(s, v)
            ins = fn(e)
            if signal:
                ins.then_inc(sem, 1)
        self.streams[eng].append(closure)
        self._commit(ev, reads, writes)
        return ev

    def dma(self, eng, out, in_, dsem, reads=(), writes=(), **kw):
        alld = [d for d in self._deps(reads, writes) if d[0] is not dsem.sem] + self.pending[eng]
        self.pending[eng] = []
        waits = self._waits(eng, alld, [d for d in alld if d[2] == eng])
        dsem.count += 16
        ev = (dsem.sem, dsem.count, "dma")

        def closure(e, out=out, in_=in_, waits=waits, sem=dsem.sem, kw=kw):
            for s, v in waits:
                e.wait_ge(s, v)
            e.dma_start(out=out, in_=in_, **kw).then_inc(sem, 16)
        self.streams[eng].append(closure)
        self._commit(ev, reads, writes)
        return ev

    def _all_events(self, include_persistent):
        evs = []
        for e in ENGS:
            if self.last_sig[e] is not None:
                evs.append(self.last_sig[e])
        for d in self.dsems:
            if d.count > 0 and (include_persistent or not d.persistent):
                evs.append((d.sem, d.count, "dma"))
        return evs

    def barrier(self):
        evs = self._all_events(False)
        for e in ENGS:
            self.pending[e] = self.pending[e] + evs
        for r in self.all_res:
            if not r.persistent:
                r.w = None
                r.r = {}
        for e in ENGS:
            if self.cnt[e] > 12000:
                self._fresh(e)

    def final_wait(self, eng="sp"):
        evs = self._all_events(True) + self.pending[eng]
        self.pending[eng] = []

        def closure(e, evs=evs):
            for s, v, _ in evs:
                e.wait_ge(s, v)
        self.streams[eng].append(closure)

    def emit(self):
        with self.nc.Block() as block:
            @block.tensor
            def _(e):
                for c in self.streams["pe"]:
                    c(e)

            @block.scalar
            def _(e):
                for c in self.streams["act"]:
                    c(e)

            @block.vector
            def _(e):
                for c in self.streams["dve"]:
                    c(e)

            @block.gpsimd
            def _(e):
                for c in self.streams["pool"]:
                    c(e)

            @block.sync
            def _(e):
                for c in self.streams["sp"]:
                    c(e)


def MM(out, lhsT, rhs, start=True, stop=True):
    return lambda e: e.matmul(out, lhsT=lhsT, rhs=rhs, start=start, stop=stop)


def TR(out, in_, ident):
    return lambda e: e.transpose(out, in_, ident)


def ACT(out, in_, func, **kw):
    return lambda e: e.activation(out=out, in_=in_, func=func, **kw)


def TT(out, a, b, op):
    return lambda e: e.tensor_tensor(out=out, in0=a, in1=b, op=op)


def TS(out, a, s1, s2, op0, op1=None):
    if op1 is None:
        return lambda e: e.tensor_scalar(out=out, in0=a, scalar1=s1, scalar2=None, op0=op0)
    return lambda e: e.tensor_scalar(out=out, in0=a, scalar1=s1, scalar2=s2, op0=op0, op1=op1)


def STT(out, in0, scalar, in1, op0, op1):
    return lambda e: e.scalar_tensor_tensor(out=out, in0=in0, scalar=scalar, in1=in1, op0=op0, op1=op1)


def CP(out, in_):
    return lambda e: e.tensor_copy(out=out, in_=in_)


def RCP(out, in_):
    return lambda e: e.reciprocal(out=out, in_=in_)


def MSET(ap, v):
    return lambda e: e.memset(ap, v)


def pipeline(stages, n, skew=1):
    ns = len(stages)
    for step in range(n + (ns - 1) * skew):
        for si, st in enumerate(stages):
            i = step - si * skew
            if 0 <= i < n:
                st(i)


def build_program(depth=DEPTH, stop=None):
    nc = bass.Bass("TRN2", target_bir_lowering=False)

    def din(name, shape, dt=F32):
        return nc.dram_tensor(name, list(shape), dt, kind="ExternalInput").ap()

    def dint(name, shape, dt=BF16):
        return nc.dram_tensor(name, list(shape), dt, kind="Internal").ap()

    x_in = din("x", [S, D])
    c_in = din("c", [1, D])
    w_ada = din("w_ada", [DEPTH, D, 3 * D])
    b_ada = din("b_ada", [DEPTH, 3 * D])
    g_pre = din("g_pre", [DEPTH, D])
    g_post = din("g_post", [DEPTH, D])
    w_a = din("w_a", [DEPTH, D, WA_COLS])
    w_b = din("w_b", [DEPTH, D, WB_COLS])
    w_uq = din("w_uq", [DEPTH, 256, 768])
    w_ukv = din("w_ukv", [DEPTH, 128, 1024])
    g_qkv = din("g_qkv", [DEPTH, 384])
    sink = din("sink", [DEPTH, 8])
    lam = din("lam", [DEPTH, 256])
    g_sub = din("g_sub", [DEPTH, 128])
    w_o = din("w_o", [DEPTH, 1536, D])
    w_out = din("w_out", [DEPTH, D, D])
    rbx = din("rbx", [33, 12])
    oh = din("oh", [33, 1280])
    cs = din("cs", [S, 32])
    ident = din("ident", [128, 128])
    out = nc.dram_tensor("out", [S, D], F32, kind="ExternalOutput").ap()

    wada_b = dint("wada_b", [DEPTH, D, 3 * D])
    wa_b = dint("wa_b", [DEPTH, D, WA_COLS])
    wb_b = dint("wb_b", [DEPTH, D, WB_COLS])
    wuq_b = dint("wuq_b", [DEPTH, 256, 768])
    wukv_b = dint("wukv_b", [DEPTH, 128, 1024])
    wo_b = dint("wo_b", [DEPTH, 1536, D])
    wout_b = dint("wout_b", [DEPTH, D, D])
    ZT = dint("ZT", [1536, S])
    QTm = dint("QTm", [8, 96, S])
    KTm = dint("KTm", [8, 96, S])
    Vm = dint("Vm", [8, 128, NTB, 64])
    QTs = dint("QTs", [512, S])
    KTs = dint("KTs", [128, S])
    Vs = dint("Vs", [2, 128, NTB, 64])
    QTd = dint("QTd", [512, S])
    KTd = dint("KTd", [512, S])
    Vd = dint("Vd", [4, 128, NTB, 128])
    GATES = dint("GATES", [3072, S])
    GT = dint("GT", [1536, S])
    rr_d = dint("rr_d", [12, 1280], F32)
    Bd = dint("Bd", [12, 128, 1152], F32)

    with contextlib.ExitStack() as st:
        P = Prog(nc, st)

        def sb(name, shape, dt):
            return st.enter_context(nc.sbuf_tensor("sb_" + name, list(shape), dt))

        PS = [st.enter_context(nc.psum_tensor(f"ps{i}", [128, 512], F32)) for i in range(8)]
        R_PS = [P.res(f"ps{i}") for i in range(8)]

        IDB = sb("idb", [128, 128], BF16)
        ONESB = sb("onesb", [128, 128], BF16)
        CACT = sb("cact", [128, 8], F32)
        CREP = sb("crep", [128, 8, 128], BF16)
        CS = sb("cs", [128, NTB, 32], F32)
        CB = sb("cb", [128, 2, 12], F32)
        MOD = sb("mod", [128, 3 * D], F32)
        GQKV = sb("gqkv", [128, 384], F32)
        LSM = sb("lsm", [128, 8], F32)
        NEGLAM = sb("neglam", [128, 1], F32)
        GSUBC = sb("gsubc", [128, 1], F32)
        SINKE = sb("sinke", [128, 8], F32)
        WUQ = sb("wuq", [128, 2, 768], BF16)
        WUKV = sb("wukv", [128, 1024], BF16)
        KPE = sb("kpe", [128, NTB, 32], BF16)
        XB = [sb(f"xb{i}", [128, D], F32) for i in range(2)]
        HF = sb("hf", [128, D], F32)
        IDF = HF[:, 0:128]
        LAMT = HF[:, 128:384]
        HB = [sb(f"hb{i}", [128, D], BF16) for i in range(2)]
        JUNK = sb("junk", [128, D], BF16)
        WS = [sb(f"ws{i}", [128, 8, 512], BF16) for i in range(2)]
        EV = [sb(f"ev{i}", [128, 512], BF16) for i in range(4)]
        QST = sb("qst", [96, 8, 512], BF16)
        KST = sb("kst", [96, 8, 512], BF16)
        QTM = [sb(f"qtm{i}", [128, 8, 96], BF16) for i in range(2)]
        KTM = [sb(f"ktm{i}", [128, 8, 96], BF16) for i in range(2)]
        SM = sb("sm", [128, 64], F32)
        RT = [sb(f"rt{i}", [128, 64], F32) for i in range(2)]
        VSB = [sb(f"vsb{i}", [128, 128], BF16) for i in range(2)]
        VDB = [sb(f"vdb{i}", [128, 512], BF16) for i in range(2)]
        VMB = [sb(f"vmb{i}", [128, 512], BF16) for i in range(2)]
        CQN = [sb(f"cqn{i}", [128, 384], BF16) for i in range(2)]
        ARENA_BYTES = 112 * 1024
        ARENA = sb("arena", [128, ARENA_BYTES // 2], BF16)

        class Bump:
            def __init__(self):
                self.o = 0

            def get(self, nbytes, dt=BF16):
                nb = (nbytes + 63) // 64 * 64
                a = ARENA[:, self.o // 2:(self.o + nb) // 2]
                self.o += nb
                assert self.o <= ARENA_BYTES, self.o
                return a.bitcast(F32) if dt == F32 else a

        bA = Bump()
        hT = bA.get(65536).rearrange("p (k t) -> p k t", k=8)
        CQNT = bA.get(24576).rearrange("p (k t) -> p k t", k=3)
        W_A = bA.get(8 * WA_COLS * 2).rearrange("p (k n) -> p k n", k=8)
        bC = Bump()
        QTS = [bC.get(8192) for _ in range(2)]
        KTS = [bC.get(8192) for _ in range(2)]
        QT1 = [bC.get(8192) for _ in range(2)]
        VSL = [bC.get(8192).rearrange("p (k d) -> p k d", k=NTB) for _ in range(2)]
        BSK = [bC.get(1152 * 4, F32) for _ in range(2)]
        PB = [bC.get(1024) for _ in range(4)]
        TMP = [bC.get(2048, F32) for _ in range(2)]
        RS = [bC.get(2048, F32) for _ in range(2)]
        OF = [bC.get(2048, F32) for _ in range(2)]
        O0 = bC.get(2048, F32)
        DD = bC.get(2048, F32)
        SQ = bC.get(2048, F32)
        RTT = bC.get(2048, F32)
        ZS = [bC.get(1024) for _ in range(2)]
        GO = [bC.get(1024) for _ in range(2)]
        SQL = bC.get(1024)
        BH = [bC.get(2304) for _ in range(2)]
        BL = [bC.get(2304) for _ in range(2)]
        bD = Bump()
        WO = bD.get(12 * 1024 * 2).rearrange("p (k n) -> p k n", k=12)
        WOUT = bD.get(8 * 1024 * 2).rearrange("p (k n) -> p k n", k=8)
        GTL = [bD.get(12 * 512 * 2).rearrange("p (k t) -> p k t", k=12) for _ in range(2)]
        GL = [bD.get(3 * 512 * 2).rearrange("p (k t) -> p k t", k=3) for _ in range(4)]
        MIXT = [bD.get(8 * 512 * 2).rearrange("p (k t) -> p k t", k=8) for _ in range(2)]
        MX = [[bD.get(2048, F32) for _ in range(3)] for _ in range(2)]
        YT = [bD.get(4096, F32) for _ in range(2)]

        R = lambda n, p=False: P.res(n, p)
        R_W = [{g: R(f"w{l}{g}", True) for g in ("ada", "a", "b", "o")} for l in range(DEPTH)]
        R_const = R("const", True)
        R_bd = R("bd", True)
        R_cs = R("cs", True)
        R_crep = R("crep", True)
        R_cb = R("cb", True)
        R_mod = R("mod")
        R_small = R("small")
        R_wa = R("wa")
        R_wuq = R("wuq")
        R_xb = [R("xb0"), R("xb1")]
        R_hf = R("hf")
        R_hb = [R("hb0"), R("hb1")]
        R_junk = R("junk")
        R_sm = [R(f"sm{i}") for i in range(16)]
        R_hT = [R(f"hT{i}") for i in range(NTB)]
        R_cqnt = [R(f"cqnt{i}") for i in range(NTB)]
        R_kpe = [R(f"kpe{i}") for i in range(NTB)]
        R_cqn = [R("cqn0"), R("cqn1")]
        R_vsb = [R("vsb0"), R("vsb1")]
        R_vdb = [R("vdb0"), R("vdb1")]
        R_vmb = [R("vmb0"), R("vmb1")]
        R_ws = [R("ws0"), R("ws1")]
        R_ev = [R(f"ev{i}") for i in range(4)]
        R_qst = R("qst")
        R_kst = R("kst")
        R_qtm = [R("qtm0"), R("qtm1")]
        R_ktm = [R("ktm0"), R("ktm1")]
        R_rt = [R("rt0"), R("rt1")]
        R_kpef = R("kpef")
        R_dram = R("dram")
        R_qts = [R("qts0"), R("qts1")]
        R_kts = [R("kts0"), R("kts1")]
        R_qt1 = [R("qt10"), R("qt11")]
        R_vsl = [R("vsl0"), R("vsl1")]
        R_bsk = [R("bsk0"), R("bsk1")]
        R_pb = [R(f"pb{i}") for i in range(4)]
        R_tmp = [R("tmp0"), R("tmp1")]
        R_rs = [R("rs0"), R("rs1")]
        R_of = [R("of0"), R("of1")]
        R_o0 = R("o0")
        R_dd = R("dd")
        R_sq = R("sq")
        R_rtt = R("rtt")
        R_zs = [R("zs0"), R("zs1")]
        R_go = [R("go0"), R("go1")]
        R_sql = R("sql")
        R_bh = [R("bh0"), R("bh1")]
        R_wo = R("wo")
        R_gtl = [R("gtl0"), R("gtl1")]
        R_gl = [R(f"gl{i}") for i in range(4)]
        R_mixt = [R("mixt0"), R("mixt1")]
        R_mx = [[R(f"mx{a}{b}") for b in range(3)] for a in range(2)]
        R_yt = [R("yt0"), R("yt1")]

        DS = {}

        def ds(name, persistent=False):
            if name not in DS:
                DS[name] = P.dsem(persistent)
            return DS[name]

        last_cast = {}

        def cast(dst, src, l, g):
            n = 1
            for s_ in src.shape:
                n *= s_
            rows = n // 1024
            s2 = src.rearrange("a b -> (a b)").rearrange("(a b) -> a b", b=1024)
            d2 = dst.rearrange("a b -> (a b)").rearrange("(a b) -> a b", b=1024)
            for r0 in range(0, rows, 1024):
                r1 = min(rows, r0 + 1024)
                dn = f"cast{l}{g}" if l == 0 else f"cast{l}"
                last_cast[l] = P.dma("pool", d2[r0:r1, :], s2[r0:r1, :], ds(dn, True), writes=[R_W[l][g]])

        def issue_casts(l):
            cast(wada_b[l], w_ada[l], l, "ada")
            cast(wa_b[l], w_a[l], l, "a")
            cast(wuq_b[l], w_uq[l], l, "a")
            cast(wukv_b[l], w_ukv[l], l, "a")
            cast(wb_b[l], w_b[l], l, "b")
            cast(wo_b[l], w_o[l], l, "o")
            cast(wout_b[l], w_out[l], l, "o")
            if l > 0:
                for g in R_W[l]:
                    R_W[l][g].w = last_cast[l]

        issue_casts(0)

        P.dma("sp", IDF, ident, ds("c0"), writes=[R_const])
        P.op("dve", CP(IDB[:], IDF), reads=[R_const], writes=[R_const])
        P.op("dve", MSET(ONESB[:], 1.0), writes=[R_const])
        for q4 in range(4):
            P.dma("sp", CS[:, q4 * 8:(q4 + 1) * 8, :],
                  cs[q4 * 1024:(q4 + 1) * 1024, :].rearrange("(t p) f -> p t f", p=128),
                  ds("c1"), writes=[R_cs])
        P.dma("sp", CB[:, 0, :], rbx[15:16, :].broadcast_to([128, 12]), ds("c2"), writes=[R_cb])
        P.dma("sp", CB[:, 1, :], rbx[31:32, :].broadcast_to([128, 12]), ds("c2"), writes=[R_cb])
        P.dma("sp", CACT[:], c_in.rearrange("o (k p) -> p (o k)", p=128), ds("c3"), writes=[R_crep],
              allow_slow_non_contiguous=True)
        P.op("act", ACT(CACT[:], CACT[:], AF.Silu), reads=[R_crep], writes=[R_crep])
        P.op("dve", CP(CREP[:], CACT[:].unsqueeze(2).broadcast_to([128, 8, 128])), reads=[R_crep], writes=[R_crep])
        RBX = TMP[0][0:33, 0:12]
        OHS = QTS[0].bitcast(F32)[:, 0:1280]
        RRS = KTS[0].bitcast(F32)[:, 0:1280]
        R_t5 = R("t5")
        P.dma("sp", RBX, rbx, ds("c4"), writes=[R_t5])
        P.dma("sp", OHS[0:33, :], oh, ds("c4"), writes=[R_t5])
        RBH, RBL, RBX2 = PB[0][0:33, 0:12], PB[1][0:33, 0:12], TMP[1][0:33, 0:12]
        OHB = KTS[1][0:33, 0:1280]
        P.op("dve", CP(RBH, RBX), reads=[R_t5], writes=[R_t5])
        P.op("dve", TT(RBX2, RBX, RBH, ALU.subtract), reads=[R_t5], writes=[R_t5])
        P.op("dve", CP(RBL, RBX2), reads=[R_t5], writes=[R_t5])
        P.op("dve", CP(OHB, OHS[0:33, :]), reads=[R_t5], writes=[R_t5])
        for j, (c0, c1) in enumerate(((0, 512), (512, 1024), (1024, 1280))):
            P.op("pe", MM(PS[j][0:12, 0:c1 - c0], RBH, OHB[:, c0:c1], True, False), reads=[R_t5], writes=[R_PS[j]], signal=False)
            P.op("pe", MM(PS[j][0:12, 0:c1 - c0], RBL, OHB[:, c0:c1], False, True), reads=[R_t5], writes=[R_PS[j]])
            P.op("dve", CP(RRS[0:12, c0:c1], PS[j][0:12, 0:c1 - c0]), reads=[R_PS[j]], writes=[R_small])
        P.dma("sp", rr_d, RRS[0:12, :], ds("c5"), reads=[R_small], writes=[R_bd])
        for p in range(128):
            P.dma("sp", Bd[:, p, :], rr_d[:, 127 - p:127 - p + 1152], ds("bd", True),
                  reads=[R_bd] if p == 0 else [], writes=[R_bd] if p == 127 else [])
        R_bd.persistent = True
        P.barrier()
        layers = range(depth) if stop != "pro" else []

        for l in layers:
            xsrc = x_in if l == 0 else out
            lam_init = 0.8 - 0.6 * math.exp(-0.3 * l)

            for j in range(6):
                par = j % 2
                P.dma("sp", WS[par][:], wada_b[l].rearrange("(k p) n -> p k n", p=128)[:, :, j * 512:(j + 1) * 512],
                      ds(f"ws{par}"), reads=[R_W[l]["ada"]], writes=[R_ws[par]])
                P.dma("sp", XB[par][:, 0:512], b_ada[l:l + 1, j * 512:(j + 1) * 512].broadcast_to([128, 512]),
                      ds(f"xb{par}"), writes=[R_xb[par]])
                for kc in range(8):
                    P.op("pe", MM(PS[par][:, :], CREP[:, kc, :], WS[par][:, kc, :], kc == 0, kc == 7),
                         reads=[R_crep, R_ws[par]], writes=[R_PS[par]], signal=(kc == 7))
                P.op("dve", TT(MOD[:, j * 512:(j + 1) * 512], PS[par][:, :], XB[par][:, 0:512], ALU.add),
                     reads=[R_PS[par], R_xb[par]], writes=[R_mod])
            P.dma("sp", XB[0][:], g_pre[l:l + 1, :].broadcast_to([128, D]), ds("xb0"), writes=[R_xb[0]])
            P.dma("sp", XB[1][:], g_post[l:l + 1, :].broadcast_to([128, D]), ds("xb1"), writes=[R_xb[1]])
            P.op("dve", STT(MOD[:, D:2 * D], MOD[:, D:2 * D], 1.0, XB[0][:], ALU.add, ALU.mult),
                 reads=[R_mod, R_xb[0]], writes=[R_mod])
            P.op("dve", TT(MOD[:, 2 * D:3 * D], MOD[:, 2 * D:3 * D], XB[1][:], ALU.mult),
                 reads=[R_mod, R_xb[1]], writes=[R_mod])
            SH, G1, GP = MOD[:, 0:D], MOD[:, D:2 * D], MOD[:, 2 * D:3 * D]
            P.dma("sp", GQKV[:], g_qkv[l:l + 1, :].broadcast_to([128, 384]), ds("s0"), writes=[R_small])
            P.dma("sp", LAMT, lam[l:l + 1, :].broadcast_to([128, 256]), ds("s0"), writes=[R_small])
            P.dma("sp", SINKE[:], sink[l:l + 1, :].broadcast_to([128, 8]), ds("s0"), writes=[R_small])
            P.dma("sp", GSUBC[:], g_sub[l:l + 1, :].rearrange("o p -> p o"), ds("s0"), writes=[R_small])
            P.op("act", ACT(SINKE[:], SINKE[:], AF.Exp), reads=[R_small], writes=[R_small])
            P.op("dve", TS(GSUBC[:], GSUBC[:], float(1.0 - lam_init), None, ALU.mult), reads=[R_small], writes=[R_small])
            P.op("dve", TT(LAMT[:, 0:64], LAMT[:, 0:64], LAMT[:, 64:128], ALU.mult), reads=[R_small], writes=[R_small])
            P.op("dve", TT(LAMT[:, 128:192], LAMT[:, 128:192], LAMT[:, 192:256], ALU.mult), reads=[R_small], writes=[R_small])
            P.op("dve", lambda e: e.reduce_sum(out=LSM[:, 0:1], in_=LAMT[:, 0:64], axis=mybir.AxisListType.X),
                 reads=[R_small], writes=[R_small])
            P.op("dve", lambda e: e.reduce_sum(out=LSM[:, 1:2], in_=LAMT[:, 128:192], axis=mybir.AxisListType.X),
                 reads=[R_small], writes=[R_small])
            P.op("act", ACT(LSM[:, 2:4], LSM[:, 0:2], AF.Exp), reads=[R_small], writes=[R_small])
            P.op("dve", TT(LSM[:, 4:5], LSM[:, 3:4], LSM[:, 2:3], ALU.subtract), reads=[R_small], writes=[R_small])
            P.op("dve", TS(NEGLAM[:], LSM[:, 4:5], float(-lam_init), None, ALU.add), reads=[R_small], writes=[R_small])
            P.dma("sp", W_A, wa_b[l].rearrange("(k p) n -> p k n", p=128), ds("wa"), reads=[R_W[l]["a"]], writes=[R_wa])
            P.dma("sp", WUQ[:], wuq_b[l].rearrange("(k p) n -> p k n", p=128), ds("wuq"), reads=[R_W[l]["a"]], writes=[R_wuq])
            P.dma("sp", WUKV[:], wukv_b[l], ds("wuq"), reads=[R_W[l]["a"]], writes=[R_wuq])

            if stop == "setup":
                break
            def bankA(tb, i):
                return (tb % 2) * 4 + i

            def A1(tb):
                par = tb % 2
                P.dma("sp", XB[par][:], xsrc[tb * 128:(tb + 1) * 128, :], ds(f"xb{par}"), writes=[R_xb[par]])
                c = tb % 8
                P.op("act", ACT(JUNK[:], XB[par][:], AF.Square, accum_out=SM[:, c:c + 1]),
                     reads=[R_xb[par]], writes=[R_junk, R_sm[c]])
                P.op("act", ACT(SM[:, c:c + 1], SM[:, c:c + 1], AF.Sqrt, scale=1.0 / D, bias=EPS),
                     reads=[R_sm[c]], writes=[R_sm[c]])
                P.op("dve", RCP(SM[:, c:c + 1], SM[:, c:c + 1]), reads=[R_sm[c]], writes=[R_sm[c]])
                P.op("dve", STT(HF[:], XB[par][:], SM[:, c:c + 1], G1, ALU.mult, ALU.mult),
                     reads=[R_xb[par], R_sm[c], R_mod], writes=[R_hf])
                P.op("dve", TT(HB[par][:], HF[:], SH, ALU.add), reads=[R_hf, R_mod], writes=[R_hb[par]])

            def A2(tb):
                par = tb % 2
                b0 = bankA(tb, 0)
                ptv = PS[b0][:].bitcast(BF16)
                for kc in range(8):
                    P.op("pe", TR(ptv[:, kc * 128:(kc + 1) * 128], HB[par][:, kc * 128:(kc + 1) * 128], IDB[:]),
                         reads=[R_hb[par], R_const], writes=[R_PS[b0]], signal=(kc == 7))
                P.op("act", ACT(hT[:, :, tb * 128:(tb + 1) * 128], ptv.rearrange("p (k t) -> p k t", k=8), AF.Copy),
                     reads=[R_PS[b0]], writes=[R_hT[tb]])

            def A3(tb):
                par = tb % 2
                b1, b2, b3 = bankA(tb, 1), bankA(tb, 2), bankA(tb, 3)
                for kc in range(8):
                    lt = hT[:, kc, tb * 128:(tb + 1) * 128]
                    P.op("pe", MM(PS[b1][:, 0:416], lt, W_A[:, kc, 0:416], kc == 0, kc == 7),
                         reads=[R_hT[tb], R_wa], writes=[R_PS[b1]], signal=False)
                    P.op("pe", MM(PS[b2][:, 0:128], lt, W_A[:, kc, 416:544], kc == 0, kc == 7),
                         reads=[R_hT[tb], R_wa], writes=[R_PS[b2]], signal=False)
                    P.op("pe", MM(PS[b3][:, :], lt, W_A[:, kc, 544:1056], kc == 0, kc == 7),
                         reads=[R_hT[tb], R_wa], writes=[R_PS[b3]], signal=(kc == 7))
                P.op("act", ACT(VSB[par][:], PS[b2][:, 0:128], AF.Copy), reads=[R_PS[b2]], writes=[R_vsb[par]])
                P.dma("pool", Vs[:, :, tb, :].rearrange("g p d -> p g d"), VSB[par][:].rearrange("p (g d) -> p g d", g=2),
                      ds(f"vsb{par}"), reads=[R_vsb[par]])
                P.op("act", ACT(VDB[par][:], PS[b3][:, :], AF.Copy), reads=[R_PS[b3]], writes=[R_vdb[par]])
                P.dma("pool", Vd[:, :, tb, :].rearrange("g p d -> p g d"), VDB[par][:].rearrange("p (g d) -> p g d", g=4),
                      ds(f"vdb{par}"), reads=[R_vdb[par]])

            def A4(tb):
                par = tb % 2
                b1, b2 = bankA(tb, 1), bankA(tb, 2)
                c = 8 + (tb % 4) * 2
                rc = R_sm[8 + tb % 4]
                P.op("act", ACT(JUNK[:, 0:256], PS[b1][:, 0:256], AF.Square, accum_out=SM[:, c:c + 1]),
                     reads=[R_PS[b1]], writes=[R_junk, rc])
                P.op("act", ACT(JUNK[:, 256:384], PS[b1][:, 256:384], AF.Square, accum_out=SM[:, c + 1:c + 2]),
                     reads=[R_PS[b1]], writes=[R_junk, rc])
                P.op("act", ACT(SM[:, c:c + 1], SM[:, c:c + 1], AF.Sqrt, scale=1.0 / 256, bias=EPS), reads=[rc], writes=[rc])
                P.op("act", ACT(SM[:, c + 1:c + 2], SM[:, c + 1:c + 2], AF.Sqrt, scale=1.0 / 128, bias=EPS), reads=[rc], writes=[rc])
                P.op("dve", RCP(SM[:, c:c + 2], SM[:, c:c + 2]), reads=[rc], writes=[rc])
                P.op("dve", STT(CQN[par][:, 0:256], PS[b1][:, 0:256], SM[:, c:c + 1], GQKV[:, 0:256], ALU.mult, ALU.mult),
                     reads=[R_PS[b1], rc, R_small], writes=[R_cqn[par]])
                P.op("dve", STT(CQN[par][:, 256:384], PS[b1][:, 256:384], SM[:, c + 1:c + 2], GQKV[:, 256:384], ALU.mult, ALU.mult),
                     reads=[R_PS[b1], rc, R_small], writes=[R_cqn[par]])
                x1, x2 = PS[b1][:, 384:400], PS[b1][:, 400:416]
                cos, sin = CS[:, tb, 0:16], CS[:, tb, 16:32]
                rt = RT[par]
                P.op("dve", TT(rt[:, 0:16], x1, cos, ALU.mult), reads=[R_PS[b1], R_cs], writes=[R_rt[par]])
                P.op("dve", TT(rt[:, 16:32], x2, sin, ALU.mult), reads=[R_PS[b1], R_cs], writes=[R_rt[par]])
                P.op("dve", TT(rt[:, 32:48], x2, cos, ALU.mult), reads=[R_PS[b1], R_cs], writes=[R_rt[par]])
                P.op("dve", TT(rt[:, 48:64], x1, sin, ALU.mult), reads=[R_PS[b1], R_cs], writes=[R_rt[par]])
                P.op("dve", TT(KPE[:, tb, 0:16], rt[:, 0:16], rt[:, 16:32], ALU.subtract), reads=[R_rt[par]], writes=[R_kpe[tb]])
                P.op("dve", TT(KPE[:, tb, 16:32], rt[:, 32:48], rt[:, 48:64], ALU.add), reads=[R_rt[par]], writes=[R_kpe[tb]])
                ptv = PS[b2][:].bitcast(BF16)
                for k3 in range(3):
                    P.op("pe", TR(ptv[:, 512 + k3 * 128:512 + (k3 + 1) * 128], CQN[par][:, k3 * 128:(k3 + 1) * 128], IDB[:]),
                         reads=[R_cqn[par], R_const], writes=[R_PS[b2]], signal=(k3 == 2))
                P.op("act", ACT(CQNT[:, :, tb * 128:(tb + 1) * 128], ptv[:, 512:896].rearrange("p (k t) -> p k t", k=3), AF.Copy),
                     reads=[R_PS[b2]], writes=[R_cqnt[tb]])

            import os
            pipeline([A1, A2, A3, A4][:int(os.environ.get('KA', '4'))], int(os.environ.get('KNTB', NTB)))
            P.barrier()
            if stop == "A":
                break

            def B1(tb):
                par = tb % 2
                tsl = slice(tb * 128, (tb + 1) * 128)
                for kc in range(2):
                    P.op("pe", MM(PS[0][:, 0:384], CQNT[:, kc, tsl], WUQ[:, kc, 0:384], kc == 0, kc == 1),
                         reads=[R_cqnt[tb], R_wuq], writes=[R_PS[0]], signal=False)
                    P.op("pe", MM(PS[1][:, 0:384], CQNT[:, kc, tsl], WUQ[:, kc, 384:768], kc == 0, kc == 1),
                         reads=[R_cqnt[tb], R_wuq], writes=[R_PS[1]], signal=(kc == 1))
                P.op("pe", MM(PS[2][:, :], CQNT[:, 2, tsl], WUKV[:, 0:512]), reads=[R_cqnt[tb], R_wuq], writes=[R_PS[2]], signal=False)
                P.op("pe", MM(PS[3][:, :], CQNT[:, 2, tsl], WUKV[:, 512:1024]), reads=[R_cqnt[tb], R_wuq], writes=[R_PS[3]])
                lvl = int(os.environ.get('KB1', '9'))
                if lvl < 2:
                    return
                P.op("act", ACT(VMB[par][:], PS[3][:, :], AF.Copy), reads=[R_PS[3]], writes=[R_vmb[par]])
                P.dma("pool", Vm[:, :, tb, :].rearrange("g p d -> p g d"), VMB[par][:].rearrange("p (g d) -> p g d", g=8),
                      ds(f"vmb{par}"), reads=[R_vmb[par]])
                if lvl < 3:
                    return
                P.op("act", ACT(KTM[par][:, :, 0:64], PS[2][:, :].rearrange("p (h d) -> p h d", h=8), AF.Copy),
                     reads=[R_PS[2]], writes=[R_ktm[par]])
                P.op("dve", CP(KTM[par][:, :, 64:96], KPE[:, tb:tb + 1, :].broadcast_to([128, 8, 32])),
                     reads=[R_kpe[tb]], writes=[R_ktm[par]])
                if lvl < 4:
                    return
                QR = HF[:, 512:768].rearrange("p (h d) -> p h d", h=8)
                C8 = HF[:, 768:896].rearrange("p (h d) -> p h d", h=8)
                S8 = HF[:, 896:1024].rearrange("p (h d) -> p h d", h=8)
                tq = [HF[:, a * 128:(a + 1) * 128].rearrange("p (h d) -> p h d", h=8) for a in range(4)]
                for hb in range(2):
                    qv = PS[hb][:, 0:384].rearrange("p (h d) -> p h d", h=4)
                    qo = QTM[par][:, hb * 4:(hb + 1) * 4, :]
                    P.op("act", ACT(qo[:, :, 0:64], qv[:, :, 0:64], AF.Copy), reads=[R_PS[hb]], writes=[R_qtm[par]])
                    P.op("act", ACT(QR[:, hb * 4:(hb + 1) * 4, :], qv[:, :, 64:96], AF.Copy), reads=[R_PS[hb]], writes=[R_hf])
                P.op("dve", CP(C8, CS[:, tb:tb + 1, 0:16].broadcast_to([128, 8, 16])), reads=[R_cs], writes=[R_hf])
                P.op("dve", CP(S8, CS[:, tb:tb + 1, 16:32].broadcast_to([128, 8, 16])), reads=[R_cs], writes=[R_hf])
                x1, x2 = QR[:, :, 0:16], QR[:, :, 16:32]
                P.op("dve", TT(tq[0], x1, C8, ALU.mult), reads=[R_hf], writes=[R_hf])
                P.op("dve", TT(tq[1], x2, S8, ALU.mult), reads=[R_hf], writes=[R_hf])
                P.op("dve", TT(tq[2], x2, C8, ALU.mult), reads=[R_hf], writes=[R_hf])
                P.op("dve", TT(tq[3], x1, S8, ALU.mult), reads=[R_hf], writes=[R_hf])
                P.op("dve", TT(QTM[par][:, :, 64:80], tq[0], tq[1], ALU.subtract), reads=[R_hf], writes=[R_qtm[par]])
                P.op("dve", TT(QTM[par][:, :, 80:96], tq[2], tq[3], ALU.add), reads=[R_hf], writes=[R_qtm[par]])

            def B2(tb):
                par = tb % 2
                tl = tb % 4
                pq = PS[4 + 2 * par][:].bitcast(BF16)
                pk = PS[5 + 2 * par][:].bitcast(BF16)
                bq, bk = 4 + 2 * par, 5 + 2 * par
                for h in range(8):
                    P.op("pe", TR(pq[0:96, h * 128:(h + 1) * 128], QTM[par][:, h, :], IDB[:]),
                         reads=[R_qtm[par], R_const], writes=[R_PS[bq]], signal=(h == 7))
                for h in range(8):
                    P.op("pe", TR(pk[0:96, h * 128:(h + 1) * 128], KTM[par][:, h, :], IDB[:]),
                         reads=[R_ktm[par], R_const], writes=[R_PS[bk]], signal=(h == 7))
                P.op("act", ACT(QST[:, :, tl * 128:(tl + 1) * 128], pq[0:96, :].rearrange("p (h t) -> p h t", h=8), AF.Copy),
                     reads=[R_PS[bq]], writes=[R_qst])
                P.op("dve", CP(KST[:, :, tl * 128:(tl + 1) * 128], pk[0:96, :].rearrange("p (h t) -> p h t", h=8)),
                     reads=[R_PS[bk]], writes=[R_kst])
                if tl == 3:
                    tc = tb // 4
                    P.dma("pool", QTm[:, :, tc * 512:(tc + 1) * 512].rearrange("h r t -> r h t"), QST[:], ds("qst"),
                          reads=[R_qst])
                    P.dma("pool", KTm[:, :, tc * 512:(tc + 1) * 512].rearrange("h r t -> r h t"), KST[:], ds("kst"),
                          reads=[R_kst])

            pipeline([B1, B2][:int(os.environ.get('KB', '2'))], int(os.environ.get('KNTB2', NTB)))
            if stop == "A2":
                break

            wbv = wb_b[l].rearrange("(k p) n -> p k n", p=128)
            groups = [(g * 4, min(49, g * 4 + 4)) for g in range(13)]
            tile_i = 0
            for gi, (j0, j1) in enumerate(groups):
                par = gi % 2
                ncol = (j1 - j0) * 128
                P.dma("sp", WS[par][:, :, 0:ncol], wbv[:, :, j0 * 128:j1 * 128], ds(f"ws{par}"),
                      reads=[R_W[l]["b"]], writes=[R_ws[par]])
                for tc in range(NTC):
                    tsl = slice(tc * 512, (tc + 1) * 512)
                    for j in range(j0, j1):
                        bk = tile_i % 8
                        evb = tile_i % 4
                        tile_i += 1
                        m = j - j0
                        for kc in range(8):
                            P.op("pe", MM(PS[bk][:, :], WS[par][:, kc, m * 128:(m + 1) * 128], hT[:, kc, tsl], kc == 0, kc == 7),
                                 reads=[R_ws[par]] + ([R_hT[tc * 4 + i] for i in range(4)] if kc == 0 else []),
                                 writes=[R_PS[bk]], signal=(kc == 7))
                        if j < 12:
                            fn, kw, dst = AF.Silu, {}, ZT[j * 128:(j + 1) * 128, tsl]
                        elif j < 16:
                            fn, kw, dst = AF.Copy, {"scale": 0.125}, QTs[(j - 12) * 128:(j - 11) * 128, tsl]
                        elif j == 16:
                            fn, kw, dst = AF.Copy, {}, KTs[:, tsl]
                        elif j < 21:
                            fn, kw, dst = AF.Copy, {"scale": 0.125}, QTd[(j - 17) * 128:(j - 16) * 128, tsl]
                        elif j < 25:
                            fn, kw, dst = AF.Copy, {}, KTd[(j - 21) * 128:(j - 20) * 128, tsl]
                        else:
                            fn, kw, dst = AF.Sigmoid, {}, GATES[(j - 25) * 128:(j - 24) * 128, tsl]
                        P.op("act", ACT(EV[evb][:], PS[bk][:, :], fn, **kw), reads=[R_PS[bk]], writes=[R_ev[evb]])
                        P.dma("pool", dst, EV[evb][:], ds(f"ev{evb}"), reads=[R_ev[evb]])
            P.barrier()
            if stop == "B":
                break

            for s2 in range(2):
                P.op("pool", MSET(VSL[s2][:, :, 64:128], 1.0), writes=[R_vsl[s2]])
                P.op("pool", MSET(QT1[s2][0:64, :], 0.0), writes=[R_qt1[s2]])
            heads = [("m", h) for h in range(8)] + [("s", h) for h in range(8)] + [("d", h) for h in range(4)]

            def load_head(hi):
                br, h = heads[hi]
                sl = hi % 2
                if br == "m":
                    P.dma("sp", QTS[sl][0:96, :], QTm[h], ds(f"qts{sl}"), writes=[R_qts[sl]])
                    P.dma("sp", KTS[sl][0:96, :], KTm[h], ds(f"kts{sl}"), writes=[R_kts[sl]])
                    for q4 in range(4):
                        P.dma("sp", VSL[sl][:, q4 * 8:(q4 + 1) * 8, 0:64], Vm[h][:, q4 * 8:(q4 + 1) * 8, :], ds(f"vsl{sl}"),
                              writes=[R_vsl[sl]])
                elif br == "s":
                    P.op("pool", MSET(QTS[sl][64:128, :], 0.0), writes=[R_qts[sl]])
                    P.op("pool", MSET(KTS[sl][64:128, :], 0.0), writes=[R_kts[sl]])
                    P.dma("sp", QTS[sl][0:64, :], QTs[h * 64:(h + 1) * 64, :], ds(f"qts{sl}"), writes=[R_qts[sl]])
                    P.dma("sp", KTS[sl][0:64, :], KTs[(h // 4) * 64:(h // 4 + 1) * 64, :], ds(f"kts{sl}"),
                          writes=[R_kts[sl]])
                    for q4 in range(4):
                        P.dma("sp", VSL[sl][:, q4 * 8:(q4 + 1) * 8, 0:64], Vs[h // 4][:, q4 * 8:(q4 + 1) * 8, :], ds(f"vsl{sl}"),
                              writes=[R_vsl[sl]])
                    P.dma("sp", BSK[sl], Bd[h], ds(f"bsk{sl}"), reads=[R_bd], writes=[R_bsk[sl]])
                    P.op("pool", CP(BH[sl], BSK[sl]), reads=[R_bsk[sl]], writes=[R_bh[sl]])
                    P.op("pool", TT(BSK[sl], BSK[sl], BH[sl], ALU.subtract), reads=[R_bsk[sl], R_bh[sl]], writes=[R_bsk[sl]])
                    P.op("pool", CP(BL[sl], BSK[sl]), reads=[R_bsk[sl]], writes=[R_bh[sl]])
                else:
                    P.op("pool", MSET(QTS[sl][64:128, :], 0.0), writes=[R_qts[sl]])
                    P.dma("sp", QTS[sl][0:64, :], QTd[h * 128:h * 128 + 64, :], ds(f"qts{sl}"), writes=[R_qts[sl]])
                    P.dma("sp", QT1[sl][64:128, :], QTd[h * 128 + 64:(h + 1) * 128, :], ds(f"qt1{sl}"), writes=[R_qt1[sl]])
                    P.dma("sp", KTS[sl], KTd[h * 128:(h + 1) * 128, :], ds(f"kts{sl}"), writes=[R_kts[sl]])
                    P.dma("sp", VSL[sl], Vd[h], ds(f"vsl{sl}"), writes=[R_vsl[sl]])
                    P.dma("sp", BSK[sl], Bd[8 + h], ds(f"bsk{sl}"), reads=[R_bd], writes=[R_bsk[sl]])

            tiles = []
            grp = 0
            for hi, (br, h) in enumerate(heads):
                for qc in range(NTC):
                    if br == "m":
                        kbs = list(range(NTB))
                    elif br == "s":
                        kbs = [kb for kb in range(4 * qc - 1, 4 * qc + 5) if 0 <= kb < NTB]
                    else:
                        kbs = list(range(NTB))
                    for comp in ((0, 1) if br == "d" else (0,)):
                        for ki, kb in enumerate(kbs):
                            o = kb - 4 * qc
                            if br == "m":
                                mode = ("none",)
                            elif -1 <= o <= 4:
                                mode = ("tile", (4 - o) * 128)
                            else:
                                mode = ("const", 0 if o < 0 else 1)
                            cr = (0, 512)
                            if br == "s":
                                cr = {-1: (0, 128), 0: (0, 256), 1: (0, 384), 2: (128, 512), 3: (256, 512), 4: (384, 512)}[o]
                            tiles.append(dict(hi=hi, br=br, h=h, qc=qc, comp=comp, kb=kb, first=(ki == 0),
                                              last=(ki == len(kbs) - 1), mode=mode, grp=grp, cr=cr))
                        grp += 1
            nt = len(tiles)
            SBK = [0, 1, 2, 3]
            MISC = 3
            ACCO = [4, 6, 5, 7]
            ACCS = [5, 7]

            def emit_qk(i):
                t = tiles[i]
                sl = t["hi"] % 2
                br = t["br"]
                kb, qc = t["kb"], t["qc"]
                bk = SBK[i % 4]
                if br == "m":
                    lhs, rhs, rq = KTS[sl][0:96, kb * 128:(kb + 1) * 128], QTS[sl][0:96, qc * 512:(qc + 1) * 512], R_qts[sl]
                elif t["comp"] == 1:
                    lhs, rhs, rq = KTS[sl][:, kb * 128:(kb + 1) * 128], QT1[sl][:, qc * 512:(qc + 1) * 512], R_qt1[sl]
                else:
                    lhs, rhs, rq = KTS[sl][:, kb * 128:(kb + 1) * 128], QTS[sl][:, qc * 512:(qc + 1) * 512], R_qts[sl]
                pb = i % 4
                if t["first"] and (br != "d" or t["comp"] == 1):
                    zz = t["grp"] % 2
                    tslz = slice(qc * 512, (qc + 1) * 512)
                    if br == "d":
                        P.dma("sp", ZS[zz], ZT[1024 + t["h"] * 128:1024 + (t["h"] + 1) * 128, tslz], ds(f"zs{zz}"), writes=[R_zs[zz]])
                    else:
                        bz = 0 if br == "m" else 512
                        P.dma("sp", ZS[zz][0:64, :], ZT[bz + t["h"] * 64:bz + (t["h"] + 1) * 64, tslz], ds(f"zs{zz}"),
                              writes=[R_zs[zz]])
                if br == "s":
                    off = t["mode"][1]
                    c0, c1 = t["cr"]
                    rhs = QTS[sl][:, qc * 512 + c0:qc * 512 + c1]
                    P.op("pe", MM(PS[bk][:, c0:c1], lhs, rhs, True, False), reads=[rq, R_kts[sl]], writes=[R_PS[bk]], signal=False)
                    P.op("pe", MM(PS[bk][:, c0:c1], IDB[:], BH[sl][:, off + c0:off + c1], False, False), reads=[R_bh[sl], R_const],
                         writes=[R_PS[bk]], signal=False)
                    P.op("pe", MM(PS[bk][:, c0:c1], IDB[:], BL[sl][:, off + c0:off + c1], False, True), reads=[R_bh[sl], R_const],
                         writes=[R_PS[bk]])
                    P.op("act", ACT(PB[pb][:, c0:c1], PS[bk][:, c0:c1], AF.Exp), reads=[R_PS[bk]], writes=[R_pb[pb]])
                    return
                P.op("pe", MM(PS[bk][:, :], lhs, rhs), reads=[rq, R_kts[sl]], writes=[R_PS[bk]])
                if t["mode"][0] == "none":
                    P.op("act", ACT(PB[pb], PS[bk][:, :], AF.Exp, scale=MLA_SCALE), reads=[R_PS[bk]], writes=[R_pb[pb]])
                elif t["mode"][0] == "const":
                    P.op("act", ACT(PB[pb], PS[bk][:, :], AF.Exp, bias=CB[:, t["mode"][1], 8 + t["h"]:9 + t["h"]], scale=1.0),
                         reads=[R_PS[bk], R_cb], writes=[R_pb[pb]])
                else:
                    off = t["mode"][1]
                    tm = i % 2
                    P.op("dve", TT(TMP[tm], PS[bk][:, :], BSK[sl][:, off:off + 512], ALU.add),
                         reads=[R_PS[bk], R_bsk[sl]], writes=[R_tmp[tm]])
                    P.op("act", ACT(PB[pb], TMP[tm], AF.Exp), reads=[R_tmp[tm]], writes=[R_pb[pb]])

            def emit_pv(i):
                t = tiles[i]
                sl = t["hi"] % 2
                br = t["br"]
                st_ = t["grp"] % 2 if br == "d" else t["grp"] % 4
                pb = i % 4
                kb = t["kb"]
                if t["first"] and t["qc"] == 0 and t["comp"] == 0 and t["hi"] + 1 < len(heads):
                    load_head(t["hi"] + 1)
                    if t["hi"] == 16 and l + 1 < depth:
                        issue_casts(l + 1)
                c0, c1 = t["cr"]
                P.op("pe", lambda e, o_=PS[ACCO[st_]][:, c0:c1], l_=VSL[sl][:, kb, :], r_=PB[pb][:, c0:c1], a=t["first"], b=t["last"]:
                     e.matmul(o_, lhsT=l_, rhs=r_, start=a, stop=b, skip_group_check=True),
                     reads=[R_pb[pb], R_vsl[sl]], writes=[R_PS[ACCO[st_]]])
                if br == "d":
                    P.op("pe", MM(PS[ACCS[st_]][:, :], ONESB[:], PB[pb], t["first"], t["last"]),
                         reads=[R_pb[pb], R_const], writes=[R_PS[ACCS[st_]]])
                if t["last"]:
                    finalize(t, st_)

            def finalize(t, st_):
                br, h, qc = t["br"], t["h"], t["qc"]
                tsl = slice(qc * 512, (qc + 1) * 512)
                ao = PS[ACCO[st_]]
                as_ = PS[ACCS[st_]] if br == "d" else None
                z = t["grp"] % 2
                if br in ("m", "s"):
                    boff = 0 if br == "m" else 512
                    rows = slice(boff + h * 64, boff + (h + 1) * 64)
                    if br == "s":
                        P.op("dve", TS(RS[z][0:64, :], ao[64:128, :], SINKE[64:128, h:h + 1], None, ALU.add),
                             reads=[R_PS[ACCO[st_]], R_small], writes=[R_rs[z]])
                        P.op("dve", RCP(RS[z][0:64, :], RS[z][0:64, :]), reads=[R_rs[z]], writes=[R_rs[z]])
                    else:
                        P.op("dve", RCP(RS[z][0:64, :], ao[64:128, :]), reads=[R_PS[ACCO[st_]]], writes=[R_rs[z]])
                    P.op("dve", TT(OF[z][0:64, :], ao[0:64, :], RS[z][0:64, :], ALU.mult),
                         reads=[R_PS[ACCO[st_]], R_rs[z]], writes=[R_of[z]])
                    P.op("pool", TT(GO[z][0:64, :], OF[z][0:64, :], ZS[z][0:64, :], ALU.mult),
                         reads=[R_of[z], R_zs[z]], writes=[R_go[z]])
                    P.dma("pool", GT[rows, tsl], GO[z][0:64, :], ds(f"go{z}"), reads=[R_go[z]])
                elif t["comp"] == 0:
                    P.op("dve", RCP(RS[z], as_[:, :]), reads=[R_PS[ACCS[st_]]], writes=[R_rs[z]])
                    P.op("dve", TT(O0, ao[:, :], RS[z], ALU.mult), reads=[R_PS[ACCO[st_]], R_rs[z]], writes=[R_o0])
                else:
                    rows = slice(1024 + h * 128, 1024 + (h + 1) * 128)
                    P.op("dve", RCP(RS[z], as_[:, :]), reads=[R_PS[ACCS[st_]]], writes=[R_rs[z]])
                    P.op("dve", TT(OF[z], ao[:, :], RS[z], ALU.mult), reads=[R_PS[ACCO[st_]], R_rs[z]], writes=[R_of[z]])
                    P.op("dve", STT(DD, OF[z], NEGLAM[:, 0:1], O0, ALU.mult, ALU.add),
                         reads=[R_of[z], R_o0, R_small], writes=[R_dd])
                    P.op("dve", TT(SQ, DD, DD, ALU.mult), reads=[R_dd], writes=[R_sq])
                    P.op("pool", CP(GO[z], SQ), reads=[R_sq], writes=[R_go[z]])
                    P.op("pool", TT(SQ, SQ, GO[z], ALU.subtract), reads=[R_sq, R_go[z]], writes=[R_sq])
                    P.op("pool", CP(SQL, SQ), reads=[R_sq], writes=[R_sql])

                    def rest(z=z, rows=rows, tsl=tsl):
                        P.op("pe", MM(PS[MISC][:, :], ONESB[:], GO[z], True, False), reads=[R_go[z], R_const],
                             writes=[R_PS[MISC]], signal=False)
                        P.op("pe", MM(PS[MISC][:, :], ONESB[:], SQL, False, True), reads=[R_sql, R_const], writes=[R_PS[MISC]])
                        P.op("act", ACT(RTT, PS[MISC][:, :], AF.Ln, scale=1.0 / 128, bias=EPS), reads=[R_PS[MISC]], writes=[R_rtt])
                        P.op("act", ACT(RTT, RTT, AF.Exp, scale=-0.5), reads=[R_rtt], writes=[R_rtt])
                        P.op("dve", STT(DD, DD, GSUBC[:, 0:1], RTT, ALU.mult, ALU.mult),
                             reads=[R_dd, R_rtt, R_small], writes=[R_dd])
                        P.op("dve", TT(GO[z], DD, ZS[z], ALU.mult), reads=[R_dd, R_zs[z]], writes=[R_go[z]])
                        P.dma("pool", GT[rows, tsl], GO[z], ds(f"go{z}"), reads=[R_go[z]])
                    deferred.append([16, rest])

            load_head(0)
            LA = 3
            deferred = []
            for i in range(nt + LA):
                if i < nt:
                    emit_qk(i)
                if i >= LA:
                    emit_pv(i - LA)
                for dfr in list(deferred):
                    dfr[0] -= 1
                    if dfr[0] <= 0:
                        deferred.remove(dfr)
                        dfr[1]()
            for dfr in deferred:
                dfr[1]()
            P.barrier()
            if stop == "C":
                break

            P.dma("sp", WO, wo_b[l].rearrange("(k p) n -> p k n", p=128), ds("wo"), reads=[R_W[l]["o"]], writes=[R_wo])
            P.dma("sp", WOUT, wout_b[l].rearrange("(k p) n -> p k n", p=128), ds("wo"), reads=[R_W[l]["o"]], writes=[R_wo])
            gtv = GT.rearrange("(j p) t -> p j t", p=128)
            gav = GATES.rearrange("(i f p) t -> p i f t", i=3, f=8, p=128)
            fi = 0

            def d_mix(tc):
                nonlocal fi
                tsl = slice(tc * 512, (tc + 1) * 512)
                gp = tc % 2
                P.dma("sp", GTL[gp], gtv[:, :, tsl], ds(f"gtl{gp}"), writes=[R_gtl[gp]])
                for fc in range(8):
                    g2 = fi % 2
                    g4 = fi % 4
                    fi += 1
                    P.dma("sp", GL[g4], gav[:, :, fc, tsl], ds(f"gl{g4}"), writes=[R_gl[g4]])
                    bo = [g2 * 3 + i for i in range(3)]
                    for i in range(3):
                        for k in range(4):
                            P.op("pe", MM(PS[bo[i]][:, :], WO[:, i * 4 + k, fc * 128:(fc + 1) * 128], GTL[gp][:, i * 4 + k, :],
                                          k == 0, k == 3),
                                 reads=[R_wo, R_gtl[gp]], writes=[R_PS[bo[i]]], signal=(k == 3))
                    mm = MX[g2]
                    for i in range(3):
                        P.op("dve", TT(mm[i], PS[bo[i]][:, :], GL[g4][:, i, :], ALU.mult),
                             reads=[R_PS[bo[i]], R_gl[g4]], writes=[R_mx[g2][i]])
                    P.op("pool", TT(mm[0], mm[0], mm[1], ALU.add), reads=[R_mx[g2][0], R_mx[g2][1]], writes=[R_mx[g2][0]])
                    P.op("pool", TT(MIXT[gp][:, fc, :], mm[0], mm[2], ALU.add), reads=[R_mx[g2][0], R_mx[g2][2]],
                         writes=[R_mixt[gp]])

            def d_out(tc):
                gp = tc % 2
                for t4 in range(4):
                    tb = tc * 4 + t4
                    par = tb % 2
                    P.dma("sp", XB[par][:], xsrc[tb * 128:(tb + 1) * 128, :], ds(f"xb{par}"), writes=[R_xb[par]])
                    by = [6, 7]
                    for hf in range(2):
                        for fc in range(8):
                            P.op("pe", MM(PS[by[hf]][:, :], MIXT[gp][:, fc, t4 * 128:(t4 + 1) * 128], WOUT[:, fc, hf * 512:(hf + 1) * 512],
                                          fc == 0, fc == 7),
                                 reads=[R_mixt[gp], R_wo], writes=[R_PS[by[hf]]], signal=(fc == 7))
                    c = (tb % 4) * 3
                    rc = R_sm[tb % 4]
                    for hf in range(2):
                        P.op("act", ACT(JUNK[:, 0:512], PS[by[hf]][:, :], AF.Square, accum_out=SM[:, c + hf:c + hf + 1]),
                             reads=[R_PS[by[hf]]], writes=[R_junk, rc])
                    P.op("dve", TT(SM[:, c + 2:c + 3], SM[:, c:c + 1], SM[:, c + 1:c + 2], ALU.add), reads=[rc], writes=[rc])
                    P.op("act", ACT(SM[:, c + 2:c + 3], SM[:, c + 2:c + 3], AF.Sqrt, scale=1.0 / D, bias=EPS), reads=[rc], writes=[rc])
                    P.op("dve", RCP(SM[:, c + 2:c + 3], SM[:, c + 2:c + 3]), reads=[rc], writes=[rc])
                    for hf in range(2):
                        hs = slice(hf * 512, (hf + 1) * 512)
                        P.op("dve", STT(YT[par][:, hs], PS[by[hf]][:, :], SM[:, c + 2:c + 3], GP[:, hs], ALU.mult, ALU.mult),
                             reads=[R_PS[by[hf]], rc, R_mod], writes=[R_yt[par]])
                    P.op("pool", TT(XB[par][:], XB[par][:], YT[par], ALU.add), reads=[R_xb[par], R_yt[par]], writes=[R_xb[par]])
                    P.dma("pool", out[tb * 128:(tb + 1) * 128, :], XB[par][:], ds(f"xbst{par}"), reads=[R_xb[par]])

            pipeline([d_mix, d_out], NTC)
            P.barrier()

        P.final_wait("sp")
        P.emit()
    return nc


def _t5_bucket_np(rel):
    import jax
    import jax.numpy as jnp
    try:
        dev = jax.devices("cpu")[0]
    except Exception:
        dev = None

    def f(rel):
        nb = 16
        max_exact = 8
        base = jnp.where(rel > 0, nb, 0)
        n = jnp.abs(rel)
        nf = jnp.maximum(n, 1).astype(jnp.float32)
        large = max_exact + (jnp.log(nf / max_exact) / math.log(128 / max_exact) * (nb - max_exact)).astype(jnp.int32)
        large = jnp.minimum(large, nb - 1)
        return base + jnp.where(n < max_exact, n, large)
    if dev is not None:
        with jax.default_device(dev):
            return np.asarray(f(jnp.asarray(rel, dtype=jnp.int32)))
    rel = np.asarray(rel, dtype=np.int64)
    n = np.abs(rel)
    nf = np.maximum(n, 1).astype(np.float32)
    large = 8 + (np.log(nf / np.float32(8)) / np.float32(math.log(16.0)) * np.float32(8)).astype(np.int32)
    large = np.minimum(large, 15)
    return np.where(rel > 0, 16, 0) + np.where(n < 8, n, large)


_CONSTS = {}


def _consts():
    if _CONSTS:
        return _CONSTS
    i = np.arange(1280)
    delta = 639 - i
    bucket = _t5_bucket_np(delta.astype(np.int32))
    oh = np.zeros((33, 1280), np.float32)
    oh[bucket, i] = 1.0
    oh[:, 1279] = 0.0
    oh[32, :] = (np.abs(delta) > 128).astype(np.float32)
    pos = np.arange(S, dtype=np.float32)
    inv = (np.float32(10000.0) ** (-np.arange(0, 32, 2, dtype=np.float32) / np.float32(32))).astype(np.float32)
    ang = (pos[:, None] * inv[None, :]).astype(np.float32)
    cs = np.concatenate([np.cos(ang), np.sin(ang)], axis=1).astype(np.float32)
    _CONSTS.update(oh=oh, cs=cs, ident=np.eye(128, dtype=np.float32))
    return _CONSTS


_NC_CACHE = {}


def kernel(x, c, w_ada, b_ada, g_pre, g_post, w_in, g_q, w_uq, g_kv, w_ukv, sink,
           lam_q1, lam_k1, lam_q2, lam_k2, g_sub, w_o_mla, w_o_swa, w_o_diff, w_out, rel_bias, _depth=DEPTH):
    f = lambda a: np.ascontiguousarray(np.asarray(a, dtype=np.float32))
    x, c, w_in = f(x), f(c), f(w_in)
    k = _consts()
    o = np.cumsum([0, 256, 128, 32, 512, 512, 128, 128, 512, 512, 512, 512, 512, 3072])
    col = lambda i: np.arange(o[i], o[i + 1])
    ia = np.concatenate([col(0), col(1), col(2), col(6), col(10)])
    ib = np.concatenate([col(3), col(7), col(11), col(4), col(5), col(8), col(9), col(12)])
    w_a = np.ascontiguousarray(w_in[:, :, ia])
    w_b = np.ascontiguousarray(w_in[:, :, ib])
    wk = f(w_ukv).reshape(DEPTH, 128, 8, 128)
    w_ukv2 = np.ascontiguousarray(np.concatenate([wk[..., :64].reshape(DEPTH, 128, 512),
                                                  wk[..., 64:].reshape(DEPTH, 128, 512)], axis=2))
    g_qkv = np.ascontiguousarray(np.concatenate([f(g_q), f(g_kv)], axis=1))
    lam = np.ascontiguousarray(np.concatenate([f(lam_q1), f(lam_k1), f(lam_q2), f(lam_k2)], axis=1))
    w_o = np.ascontiguousarray(np.concatenate([f(w_o_mla), f(w_o_swa), f(w_o_diff)], axis=1))
    rbx = np.zeros((33, 12), np.float32)
    rbx[:32] = f(rel_bias)
    rbx[32, :8] = -30000.0
    shared = dict(w_ada=f(w_ada), b_ada=f(b_ada), g_pre=f(g_pre), g_post=f(g_post), w_a=w_a, w_b=w_b,
                  w_uq=f(w_uq), w_ukv=w_ukv2, g_qkv=g_qkv, sink=f(sink), lam=lam, g_sub=f(g_sub),
                  w_o=w_o, w_out=f(w_out), rbx=rbx, oh=k["oh"], cs=k["cs"], ident=k["ident"])
    if _depth not in _NC_CACHE:
        import os
        _NC_CACHE[_depth] = build_program(_depth, os.environ.get("KSTOP"))
    nc = _NC_CACHE[_depth]
    in_maps = []
    for b in range(NCORES):
        m = dict(shared)
        m["x"] = x[b]
        m["c"] = c[b:b + 1]
        in_maps.append(m)
    res = run_bass_kernel_spmd(nc, in_maps, core_ids=list(range(NCORES)))
    return np.stack([np.asarray(r["out"], dtype=np.float32) for r in res.results], axis=0)
```

```python
import contextlib
import math
import numpy as np
import concourse.bass as bass
import concourse.mybir as mybir
from concourse.bass_utils import run_bass_kernel_spmd

F32 = mybir.dt.float32
BF16 = mybir.dt.bfloat16
AF = mybir.ActivationFunctionType
ALU = mybir.AluOpType

S = 4096
D = 1024
DEPTH = 4
NTB = 32
NTC = 8
EPS = 1e-6
NCORES = 8
WA_COLS = 1056
WB_COLS = 6272
MLA_SCALE = 96 ** -0.5

ENGS = ("pe", "act", "dve", "pool", "sp")
SAME_ENGINE_RAW = ("act", "dve", "pool")


class Res:
    __slots__ = ("name", "w", "r", "persistent")

    def __init__(self, name, persistent=False):
        self.name = name
        self.w = None
        self.r = {}
        self.persistent = persistent


class DSem:
    def __init__(self, sem, persistent=False):
        self.sem = sem
        self.count = 0
        self.persistent = persistent


class Prog:
    def __init__(self, nc, stack, n_eng_sems=30, n_dma_sems=70):
        self.nc = nc
        self.streams = {e: [] for e in ENGS}
        self.free_sems = [stack.enter_context(nc.semaphore(f"s{i}")) for i in range(n_eng_sems)]
        self.free_dsems = [stack.enter_context(nc.semaphore(f"d{i}")) for i in range(n_dma_sems)]
        self.dsems = []
        self.sem, self.cnt, self.last_sig = {}, {}, {}
        for e in ENGS:
            self._fresh(e)
        self.waited = {e: {} for e in ENGS}
        self.pending = {e: [] for e in ENGS}
        self.all_res = []

    def _fresh(self, e):
        self.sem[e] = self.free_sems.pop(0)
        self.cnt[e] = 0
        self.last_sig[e] = None

    def res(self, name, persistent=False):
        r = Res(name, persistent)
        self.all_res.append(r)
        return r

    def dsem(self, persistent=False):
        d = DSem(self.free_dsems.pop(0), persistent)
        self.dsems.append(d)
        return d

    def _deps(self, reads, writes):
        d = []
        for r in reads:
            if r.w is not None:
                d.append(r.w)
        for w in writes:
            if w.w is not None:
                d.append(w.w)
            d.extend(w.r.values())
        return d

    def _commit(self, ev, reads, writes):
        for r in reads:
            r.r[id(ev[0])] = ev
        for w in writes:
            w.w = ev
            w.r = {}

    def _waits(self, eng, deps, raw):
        waits = []
        for ev in deps:
            sem, val, src = ev
            if src == eng and (eng not in SAME_ENGINE_RAW or not any(ev is x for x in raw)):
                continue
            key = id(sem)
            if self.waited[eng].get(key, 0) >= val:
                continue
            self.waited[eng][key] = val
            waits.append((sem, val))
        return waits

    def op(self, eng, fn, reads=(), writes=(), signal=True):
        raw = [r.w for r in reads if r.w is not None]
        alld = self._deps(reads, writes) + self.pending[eng]
        self.pending[eng] = []
        waits = self._waits(eng, alld, raw)
        sem = self.sem[eng]
        if signal:
            self.cnt[eng] += 1
            ev = (sem, self.cnt[eng], eng)
            self.last_sig[eng] = ev
        else:
            ev = (sem, self.cnt[eng] + 1, eng)

        def closure(e, fn=fn, waits=waits, sem=sem, signal=signal):
            for s, v in waits:
                e.wait_ge(s, v)
            ins = fn(e)
            if signal:
                ins.then_inc(sem, 1)
        self.streams[eng].append(closure)
        self._commit(ev, reads, writes)
        return ev

    def dma(self, eng, out, in_, dsem, reads=(), writes=(), **kw):
        alld = [d for d in self._deps(reads, writes) if d[0] is not dsem.sem] + self.pending[eng]
        self.pending[eng] = []
        waits = self._waits(eng, alld, [d for d in alld if d[2] == eng])
        dsem.count += 16
        ev = (dsem.sem, dsem.count, "dma")

        def closure(e, out=out, in_=in_, waits=waits, sem=dsem.sem, kw=kw):
            for s, v in waits:
                e.wait_ge(s, v)
            e.dma_start(out=out, in_=in_, **kw).then_inc(sem, 16)
        self.streams[eng].append(closure)
        self._commit(ev, reads, writes)
        return ev

    def _all_events(self, include_persistent):
        evs = []
        for e in ENGS:
            if self.last_sig[e] is not None:
                evs.append(self.last_sig[e])
        for d in self.dsems:
            if d.count > 0 and (include_persistent or not d.persistent):
                evs.append((d.sem, d.count, "dma"))
        return evs

    def barrier(self):
        evs = self._all_events(False)
        for e in ENGS:
            self.pending[e] = self.pending[e] + evs
        for r in self.all_res:
            if not r.persistent:
                r.w = None
                r.r = {}
        for e in ENGS:
            if self.cnt[e] > 12000:
                self._fresh(e)

    def final_wait(self, eng="sp"):
        evs = self._all_events(True) + self.pending[eng]
        self.pending[eng] = []

        def closure(e, evs=evs):
            for s, v, _ in evs:
                e.wait_ge(s, v)
        self.streams[eng].append(closure)

    def emit(self):
        with self.nc.Block() as block:
            @block.tensor
            def _(e):
                for c in self.streams["pe"]:
                    c(e)

            @block.scalar
            def _(e):
                for c in self.streams["act"]:
                    c(e)

            @block.vector
            def _(e):
                for c in self.streams["dve"]:
                    c(e)

            @block.gpsimd
            def _(e):
                for c in self.streams["pool"]:
                    c(e)

            @block.sync
            def _(e):
                for c in self.streams["sp"]:
                    c(e)


def MM(out, lhsT, rhs, start=True, stop=True):
    return lambda e: e.matmul(out, lhsT=lhsT, rhs=rhs, start=start, stop=stop)


def TR(out, in_, ident):
    return lambda e: e.transpose(out, in_, ident)


def ACT(out, in_, func, **kw):
    return lambda e: e.activation(out=out, in_=in_, func=func, **kw)


def TT(out, a, b, op):
    return lambda e: e.tensor_tensor(out=out, in0=a, in1=b, op=op)


def TS(out, a, s1, s2, op0, op1=None):
    if op1 is None:
        return lambda e: e.tensor_scalar(out=out, in0=a, scalar1=s1, scalar2=None, op0=op0)
    return lambda e: e.tensor_scalar(out=out, in0=a, scalar1=s1, scalar2=s2, op0=op0, op1=op1)


def STT(out, in0, scalar, in1, op0, op1):
    return lambda e: e.scalar_tensor_tensor(out=out, in0=in0, scalar=scalar, in1=in1, op0=op0, op1=op1)


def CP(out, in_):
    return lambda e: e.tensor_copy(out=out, in_=in_)


def RCP(out, in_):
    return lambda e: e.reciprocal(out=out, in_=in_)


def RCPA(out, in_, scratch):
    return lambda e: e.reciprocal_approx_accurate(out=out, in_=in_, scratch=scratch)


def MSET(ap, v):
    return lambda e: e.memset(ap, v)


def pipeline(stages, n, skew=1):
    ns = len(stages)
    for step in range(n + (ns - 1) * skew):
        for si, st in enumerate(stages):
            i = step - si * skew
            if 0 <= i < n:
                st(i)


def build_program(depth=DEPTH, stop=None):
    nc = bass.Bass("TRN2", target_bir_lowering=False)

    def din(name, shape, dt=F32):
        return nc.dram_tensor(name, list(shape), dt, kind="ExternalInput").ap()

    def dint(name, shape, dt=BF16):
        return nc.dram_tensor(name, list(shape), dt, kind="Internal").ap()

    x_in = din("x", [S, D])
    c_in = din("c", [1, D])
    w_ada = din("w_ada", [DEPTH, D, 3 * D])
    b_ada = din("b_ada", [DEPTH, 3 * D])
    g_pre = din("g_pre", [DEPTH, D])
    g_post = din("g_post", [DEPTH, D])
    w_a = din("w_a", [DEPTH, D, WA_COLS])
    w_b = din("w_b", [DEPTH, D, WB_COLS])
    w_uq = din("w_uq", [DEPTH, 256, 768])
    w_ukv = din("w_ukv", [DEPTH, 128, 1024])
    g_qkv = din("g_qkv", [DEPTH, 384])
    sink = din("sink", [DEPTH, 8])
    lam = din("lam", [DEPTH, 256])
    g_sub = din("g_sub", [DEPTH, 128])
    w_o = din("w_o", [DEPTH, 1536, D])
    w_out = din("w_out", [DEPTH, D, D])
    rbx = din("rbx", [33, 12])
    oh = din("oh", [33, 1280])
    cs = din("cs", [S, 32])
    ident = din("ident", [128, 128])
    out = nc.dram_tensor("out", [S, D], F32, kind="ExternalOutput").ap()

    wada_b = dint("wada_b", [DEPTH, D, 3 * D])
    wa_b = dint("wa_b", [DEPTH, D, WA_COLS])
    wb_b = dint("wb_b", [DEPTH, D, WB_COLS])
    wuq_b = dint("wuq_b", [DEPTH, 256, 768])
    wukv_b = dint("wukv_b", [DEPTH, 128, 1024])
    wo_b = dint("wo_b", [DEPTH, 1536, D])
    wout_b = dint("wout_b", [DEPTH, D, D])
    ZT = dint("ZT", [1536, S])
    QTm = dint("QTm", [8, 96, S])
    KTm = dint("KTm", [8, 96, S])
    Vm = dint("Vm", [8, 128, NTB, 64])
    QTs = dint("QTs", [512, S])
    KTs = dint("KTs", [128, S])
    Vs = dint("Vs", [2, 128, NTB, 64])
    QTd = dint("QTd", [512, S])
    KTd = dint("KTd", [512, S])
    Vd = dint("Vd", [4, 128, NTB, 128])
    GATES = dint("GATES", [3072, S])
    GT = dint("GT", [1536, S])
    rr_d = dint("rr_d", [12, 1280], F32)
    Bd = dint("Bd", [12, 128, 1152], F32)

    with contextlib.ExitStack() as st:
        P = Prog(nc, st)

        def sb(name, shape, dt):
            return st.enter_context(nc.sbuf_tensor("sb_" + name, list(shape), dt))

        PS = [st.enter_context(nc.psum_tensor(f"ps{i}", [128, 512], F32)) for i in range(8)]
        R_PS = [P.res(f"ps{i}") for i in range(8)]

        IDB = sb("idb", [128, 128], BF16)
        ONESB = sb("onesb", [128, 128], BF16)
        CACT = sb("cact", [128, 8], F32)
        CREP = sb("crep", [128, 8, 128], BF16)
        CS = sb("cs", [128, NTB, 32], F32)
        CB = sb("cb", [128, 2, 12], F32)
        MOD = sb("mod", [128, 3 * D], F32)
        GQKV = sb("gqkv", [128, 384], F32)
        LSM = sb("lsm", [128, 8], F32)
        NEGLAM = sb("neglam", [128, 1], F32)
        GSUBC = sb("gsubc", [128, 1], F32)
        SINKE = sb("sinke", [128, 8], F32)
        WUQ = sb("wuq", [128, 2, 768], BF16)
        WUKV = sb("wukv", [128, 1024], BF16)
        KPE = sb("kpe", [128, NTB, 32], BF16)
        XB = [sb(f"xb{i}", [128, D], F32) for i in range(2)]
        HF = sb("hf", [128, D], F32)
        IDF = HF[:, 0:128]
        LAMT = HF[:, 128:384]
        HB = [sb(f"hb{i}", [128, D], BF16) for i in range(2)]
        JUNK = sb("junk", [128, D], BF16)
        WS = [sb(f"ws{i}", [128, 8, 512], BF16) for i in range(2)]
        EV = [sb(f"ev{i}", [128, 512], BF16) for i in range(4)]
        QST = sb("qst", [96, 8, 512], BF16)
        KST = sb("kst", [96, 8, 512], BF16)
        QTM = [sb(f"qtm{i}", [128, 8, 96], BF16) for i in range(2)]
        KTM = [sb(f"ktm{i}", [128, 8, 96], BF16) for i in range(2)]
        SM = sb("sm", [128, 64], F32)
        RT = [sb(f"rt{i}", [128, 64], F32) for i in range(2)]
        VSB = [sb(f"vsb{i}", [128, 128], BF16) for i in range(2)]
        VDB = [sb(f"vdb{i}", [128, 512], BF16) for i in range(2)]
        VMB = [sb(f"vmb{i}", [128, 512], BF16) for i in range(2)]
        CQN = [sb(f"cqn{i}", [128, 384], BF16) for i in range(2)]
        ARENA_BYTES = 112 * 1024
        ARENA = sb("arena", [128, ARENA_BYTES // 2], BF16)

        class Bump:
            def __init__(self):
                self.o = 0

            def get(self, nbytes, dt=BF16):
                nb = (nbytes + 63) // 64 * 64
                a = ARENA[:, self.o // 2:(self.o + nb) // 2]
                self.o += nb
                assert self.o <= ARENA_BYTES, self.o
                return a.bitcast(F32) if dt == F32 else a

        bA = Bump()
        hT = bA.get(65536).rearrange("p (k t) -> p k t", k=8)
        CQNT = bA.get(24576).rearrange("p (k t) -> p k t", k=3)
        W_A = bA.get(8 * WA_COLS * 2).rearrange("p (k n) -> p k n", k=8)
        bC = Bump()
        QTS = [bC.get(8192) for _ in range(2)]
        KTS = [bC.get(8192) for _ in range(2)]
        QT1 = [bC.get(8192) for _ in range(2)]
        VSL = [bC.get(8192).rearrange("p (k d) -> p k d", k=NTB) for _ in range(2)]
        BSK = [bC.get(1152 * 4, F32) for _ in range(2)]
        PB = [bC.get(1024) for _ in range(4)]
        TMP = [bC.get(2048, F32) for _ in range(2)]
        RS = [bC.get(2048, F32) for _ in range(2)]
        OF = [bC.get(2048, F32) for _ in range(2)]
        O0 = bC.get(2048, F32)
        DD = bC.get(2048, F32)
        SQ = bC.get(2048, F32)
        RTT = bC.get(2048, F32)
        ZS = [bC.get(1024) for _ in range(2)]
        GO = [bC.get(1024) for _ in range(2)]
        SQL = bC.get(1024)
        BH = [bC.get(2304) for _ in range(2)]
        BL = [bC.get(2304) for _ in range(2)]
        bD = Bump()
        WO = bD.get(12 * 1024 * 2).rearrange("p (k n) -> p k n", k=12)
        WOUT = bD.get(8 * 1024 * 2).rearrange("p (k n) -> p k n", k=8)
        GTL = [bD.get(12 * 512 * 2).rearrange("p (k t) -> p k t", k=12) for _ in range(2)]
        GL = [bD.get(3 * 512 * 2).rearrange("p (k t) -> p k t", k=3) for _ in range(4)]
        MIXT = [bD.get(8 * 512 * 2).rearrange("p (k t) -> p k t", k=8) for _ in range(2)]
        MX = [[bD.get(2048, F32) for _ in range(3)] for _ in range(2)]
        YT = [bD.get(4096, F32) for _ in range(2)]

        R = lambda n, p=False: P.res(n, p)
        R_W = [{g: R(f"w{l}{g}", True) for g in ("ada", "a", "b", "o")} for l in range(DEPTH)]
        R_const = R("const", True)
        R_bd = R("bd", True)
        R_cs = R("cs", True)
        R_crep = R("crep", True)
        R_cb = R("cb", True)
        R_mod = R("mod")
        R_small = R("small")
        R_wa = R("wa")
        R_wuq = R("wuq")
        R_xb = [R("xb0"), R("xb1")]
        R_hf = R("hf")
        R_hb = [R("hb0"), R("hb1")]
        R_junk = R("junk")
        R_sm = [R(f"sm{i}") for i in range(16)]
        R_hT = [R(f"hT{i}") for i in range(NTB)]
        R_cqnt = [R(f"cqnt{i}") for i in range(NTB)]
        R_kpe = [R(f"kpe{i}") for i in range(NTB)]
        R_cqn = [R("cqn0"), R("cqn1")]
        R_vsb = [R("vsb0"), R("vsb1")]
        R_vdb = [R("vdb0"), R("vdb1")]
        R_vmb = [R("vmb0"), R("vmb1")]
        R_ws = [R("ws0"), R("ws1")]
        R_ev = [R(f"ev{i}") for i in range(4)]
        R_qst = R("qst")
        R_kst = R("kst")
        R_qtm = [R("qtm0"), R("qtm1")]
        R_ktm = [R("ktm0"), R("ktm1")]
        R_rt = [R("rt0"), R("rt1")]
        R_kpef = R("kpef")
        R_dram = R("dram")
        R_qts = [R("qts0"), R("qts1")]
        R_kts = [R("kts0"), R("kts1")]
        R_qt1 = [R("qt10"), R("qt11")]
        R_vsl = [R("vsl0"), R("vsl1")]
        R_bsk = [R("bsk0"), R("bsk1")]
        R_pb = [R(f"pb{i}") for i in range(4)]
        R_tmp = [R("tmp0"), R("tmp1")]
        R_rs = [R("rs0"), R("rs1")]
        R_of = [R("of0"), R("of1")]
        R_o0 = R("o0")
        R_dd = R("dd")
        R_sq = R("sq")
        R_rtt = R("rtt")
        R_zs = [R("zs0"), R("zs1")]
        R_go = [R("go0"), R("go1")]
        R_sql = R("sql")
        R_bh = [R("bh0"), R("bh1")]
        R_wo = R("wo")
        R_gtl = [R("gtl0"), R("gtl1")]
        R_gl = [R(f"gl{i}") for i in range(4)]
        R_mixt = [R("mixt0"), R("mixt1")]
        R_mx = [[R(f"mx{a}{b}") for b in range(3)] for a in range(2)]
        R_yt = [R("yt0"), R("yt1")]

        DS = {}

        def ds(name, persistent=False):
            if name not in DS:
                DS[name] = P.dsem(persistent)
            return DS[name]

        last_cast = {}

        def cast(dst, src, l, g):
            n = 1
            for s_ in src.shape:
                n *= s_
            rows = n // 1024
            s2 = src.rearrange("a b -> (a b)").rearrange("(a b) -> a b", b=1024)
            d2 = dst.rearrange("a b -> (a b)").rearrange("(a b) -> a b", b=1024)
            for r0 in range(0, rows, 1024):
                r1 = min(rows, r0 + 1024)
                dn = f"cast{l}{g}" if l == 0 else f"cast{l}"
                last_cast[l] = P.dma("pool", d2[r0:r1, :], s2[r0:r1, :], ds(dn, True), writes=[R_W[l][g]])

        def issue_casts(l):
            cast(wada_b[l], w_ada[l], l, "ada")
            cast(wa_b[l], w_a[l], l, "a")
            cast(wuq_b[l], w_uq[l], l, "a")
            cast(wukv_b[l], w_ukv[l], l, "a")
            cast(wb_b[l], w_b[l], l, "b")
            cast(wo_b[l], w_o[l], l, "o")
            cast(wout_b[l], w_out[l], l, "o")
            if l > 0:
                for g in R_W[l]:
                    R_W[l][g].w = last_cast[l]

        issue_casts(0)

        P.dma("sp", IDF, ident, ds("c0"), writes=[R_const])
        P.op("dve", CP(IDB[:], IDF), reads=[R_const], writes=[R_const])
        P.op("dve", MSET(ONESB[:], 1.0), writes=[R_const])
        for q4 in range(4):
            P.dma("sp", CS[:, q4 * 8:(q4 + 1) * 8, :],
                  cs[q4 * 1024:(q4 + 1) * 1024, :].rearrange("(t p) f -> p t f", p=128),
                  ds("c1"), writes=[R_cs])
        P.dma("sp", CB[:, 0, :], rbx[15:16, :].broadcast_to([128, 12]), ds("c2"), writes=[R_cb])
        P.dma("sp", CB[:, 1, :], rbx[31:32, :].broadcast_to([128, 12]), ds("c2"), writes=[R_cb])
        P.dma("sp", CACT[:], c_in.rearrange("o (k p) -> p (o k)", p=128), ds("c3"), writes=[R_crep],
              allow_slow_non_contiguous=True)
        P.op("act", ACT(CACT[:], CACT[:], AF.Silu), reads=[R_crep], writes=[R_crep])
        P.op("dve", CP(CREP[:], CACT[:].unsqueeze(2).broadcast_to([128, 8, 128])), reads=[R_crep], writes=[R_crep])
        RBX = TMP[0][0:33, 0:12]
        OHS = QTS[0].bitcast(F32)[:, 0:1280]
        RRS = KTS[0].bitcast(F32)[:, 0:1280]
        R_t5 = R("t5")
        P.dma("sp", RBX, rbx, ds("c4"), writes=[R_t5])
        P.dma("sp", OHS[0:33, :], oh, ds("c4"), writes=[R_t5])
        RBH, RBL, RBX2 = PB[0][0:33, 0:12], PB[1][0:33, 0:12], TMP[1][0:33, 0:12]
        OHB = KTS[1][0:33, 0:1280]
        P.op("dve", CP(RBH, RBX), reads=[R_t5], writes=[R_t5])
        P.op("dve", TT(RBX2, RBX, RBH, ALU.subtract), reads=[R_t5], writes=[R_t5])
        P.op("dve", CP(RBL, RBX2), reads=[R_t5], writes=[R_t5])
        P.op("dve", CP(OHB, OHS[0:33, :]), reads=[R_t5], writes=[R_t5])
        for j, (c0, c1) in enumerate(((0, 512), (512, 1024), (1024, 1280))):
            P.op("pe", MM(PS[j][0:12, 0:c1 - c0], RBH, OHB[:, c0:c1], True, False), reads=[R_t5], writes=[R_PS[j]], signal=False)
            P.op("pe", MM(PS[j][0:12, 0:c1 - c0], RBL, OHB[:, c0:c1], False, True), reads=[R_t5], writes=[R_PS[j]])
            P.op("dve", CP(RRS[0:12, c0:c1], PS[j][0:12, 0:c1 - c0]), reads=[R_PS[j]], writes=[R_small])
        P.dma("sp", rr_d, RRS[0:12, :], ds("c5"), reads=[R_small], writes=[R_bd])
        for p in range(128):
            P.dma("sp", Bd[:, p, :], rr_d[:, 127 - p:127 - p + 1152], ds("bd", True),
                  reads=[R_bd] if p == 0 else [], writes=[R_bd] if p == 127 else [])
        R_bd.persistent = True
        P.barrier()
        layers = range(depth) if stop != "pro" else []

        for l in layers:
            xsrc = x_in if l == 0 else out
            lam_init = 0.8 - 0.6 * math.exp(-0.3 * l)

            for j in range(6):
                par = j % 2
                P.dma("sp", WS[par][:], wada_b[l].rearrange("(k p) n -> p k n", p=128)[:, :, j * 512:(j + 1) * 512],
                      ds(f"ws{par}"), reads=[R_W[l]["ada"]], writes=[R_ws[par]])
                P.dma("sp", XB[par][:, 0:512], b_ada[l:l + 1, j * 512:(j + 1) * 512].broadcast_to([128, 512]),
                      ds(f"xb{par}"), writes=[R_xb[par]])
                for kc in range(8):
                    P.op("pe", MM(PS[par][:, :], CREP[:, kc, :], WS[par][:, kc, :], kc == 0, kc == 7),
                         reads=[R_crep, R_ws[par]], writes=[R_PS[par]], signal=(kc == 7))
                P.op("dve", TT(MOD[:, j * 512:(j + 1) * 512], PS[par][:, :], XB[par][:, 0:512], ALU.add),
                     reads=[R_PS[par], R_xb[par]], writes=[R_mod])
            P.dma("sp", XB[0][:], g_pre[l:l + 1, :].broadcast_to([128, D]), ds("xb0"), writes=[R_xb[0]])
            P.dma("sp", XB[1][:], g_post[l:l + 1, :].broadcast_to([128, D]), ds("xb1"), writes=[R_xb[1]])
            P.op("dve", STT(MOD[:, D:2 * D], MOD[:, D:2 * D], 1.0, XB[0][:], ALU.add, ALU.mult),
                 reads=[R_mod, R_xb[0]], writes=[R_mod])
            P.op("dve", TT(MOD[:, 2 * D:3 * D], MOD[:, 2 * D:3 * D], XB[1][:], ALU.mult),
                 reads=[R_mod, R_xb[1]], writes=[R_mod])
            SH, G1, GP = MOD[:, 0:D], MOD[:, D:2 * D], MOD[:, 2 * D:3 * D]
            P.dma("sp", GQKV[:], g_qkv[l:l + 1, :].broadcast_to([128, 384]), ds("s0"), writes=[R_small])
            P.dma("sp", LAMT, lam[l:l + 1, :].broadcast_to([128, 256]), ds("s0"), writes=[R_small])
            P.dma("sp", SINKE[:], sink[l:l + 1, :].broadcast_to([128, 8]), ds("s0"), writes=[R_small])
            P.dma("sp", GSUBC[:], g_sub[l:l + 1, :].rearrange("o p -> p o"), ds("s0"), writes=[R_small])
            P.op("act", ACT(SINKE[:], SINKE[:], AF.Exp), reads=[R_small], writes=[R_small])
            P.op("dve", TS(GSUBC[:], GSUBC[:], float(1.0 - lam_init), None, ALU.mult), reads=[R_small], writes=[R_small])
            P.op("dve", TT(LAMT[:, 0:64], LAMT[:, 0:64], LAMT[:, 64:128], ALU.mult), reads=[R_small], writes=[R_small])
            P.op("dve", TT(LAMT[:, 128:192], LAMT[:, 128:192], LAMT[:, 192:256], ALU.mult), reads=[R_small], writes=[R_small])
            P.op("dve", lambda e: e.reduce_sum(out=LSM[:, 0:1], in_=LAMT[:, 0:64], axis=mybir.AxisListType.X),
                 reads=[R_small], writes=[R_small])
            P.op("dve", lambda e: e.reduce_sum(out=LSM[:, 1:2], in_=LAMT[:, 128:192], axis=mybir.AxisListType.X),
                 reads=[R_small], writes=[R_small])
            P.op("act", ACT(LSM[:, 2:4], LSM[:, 0:2], AF.Exp), reads=[R_small], writes=[R_small])
            P.op("dve", TT(LSM[:, 4:5], LSM[:, 3:4], LSM[:, 2:3], ALU.subtract), reads=[R_small], writes=[R_small])
            P.op("dve", TS(NEGLAM[:], LSM[:, 4:5], float(-lam_init), None, ALU.add), reads=[R_small], writes=[R_small])
            P.dma("sp", W_A, wa_b[l].rearrange("(k p) n -> p k n", p=128), ds("wa"), reads=[R_W[l]["a"]], writes=[R_wa])
            P.dma("sp", WUQ[:], wuq_b[l].rearrange("(k p) n -> p k n", p=128), ds("wuq"), reads=[R_W[l]["a"]], writes=[R_wuq])
            P.dma("sp", WUKV[:], wukv_b[l], ds("wuq"), reads=[R_W[l]["a"]], writes=[R_wuq])

            if stop == "setup":
                break
            def bankA(tb, i):
                return (tb % 2) * 4 + i

            def A1(tb):
                par = tb % 2
                P.dma("sp", XB[par][:], xsrc[tb * 128:(tb + 1) * 128, :], ds(f"xb{par}"), writes=[R_xb[par]])
                c = tb % 8
                P.op("act", ACT(JUNK[:], XB[par][:], AF.Square, accum_out=SM[:, c:c + 1]),
                     reads=[R_xb[par]], writes=[R_junk, R_sm[c]])
                P.op("act", ACT(SM[:, c:c + 1], SM[:, c:c + 1], AF.Sqrt, scale=1.0 / D, bias=EPS),
                     reads=[R_sm[c]], writes=[R_sm[c]])
                P.op("dve", RCP(SM[:, c:c + 1], SM[:, c:c + 1]), reads=[R_sm[c]], writes=[R_sm[c]])
                P.op("dve", STT(HF[:], XB[par][:], SM[:, c:c + 1], G1, ALU.mult, ALU.mult),
                     reads=[R_xb[par], R_sm[c], R_mod], writes=[R_hf])
                P.op("dve", TT(HB[par][:], HF[:], SH, ALU.add), reads=[R_hf, R_mod], writes=[R_hb[par]])

            def A2(tb):
                par = tb % 2
                b0 = bankA(tb, 0)
                ptv = PS[b0][:].bitcast(BF16)
                for kc in range(8):
                    P.op("pe", TR(ptv[:, kc * 128:(kc + 1) * 128], HB[par][:, kc * 128:(kc + 1) * 128], IDB[:]),
                         reads=[R_hb[par], R_const], writes=[R_PS[b0]], signal=(kc == 7))
                P.op("act", ACT(hT[:, :, tb * 128:(tb + 1) * 128], ptv.rearrange("p (k t) -> p k t", k=8), AF.Copy),
                     reads=[R_PS[b0]], writes=[R_hT[tb]])

            def A3(tb):
                par = tb % 2
                b1, b2, b3 = bankA(tb, 1), bankA(tb, 2), bankA(tb, 3)
                for kc in range(8):
                    lt = hT[:, kc, tb * 128:(tb + 1) * 128]
                    P.op("pe", MM(PS[b1][:, 0:416], lt, W_A[:, kc, 0:416], kc == 0, kc == 7),
                         reads=[R_hT[tb], R_wa], writes=[R_PS[b1]], signal=False)
                    P.op("pe", MM(PS[b2][:, 0:128], lt, W_A[:, kc, 416:544], kc == 0, kc == 7),
                         reads=[R_hT[tb], R_wa], writes=[R_PS[b2]], signal=False)
                    P.op("pe", MM(PS[b3][:, :], lt, W_A[:, kc, 544:1056], kc == 0, kc == 7),
                         reads=[R_hT[tb], R_wa], writes=[R_PS[b3]], signal=(kc == 7))
                P.op("act", ACT(VSB[par][:], PS[b2][:, 0:128], AF.Copy), reads=[R_PS[b2]], writes=[R_vsb[par]])
                P.dma("pool", Vs[:, :, tb, :].rearrange("g p d -> p g d"), VSB[par][:].rearrange("p (g d) -> p g d", g=2),
                      ds(f"vsb{par}"), reads=[R_vsb[par]])
                P.op("act", ACT(VDB[par][:], PS[b3][:, :], AF.Copy), reads=[R_PS[b3]], writes=[R_vdb[par]])
                P.dma("pool", Vd[:, :, tb, :].rearrange("g p d -> p g d"), VDB[par][:].rearrange("p (g d) -> p g d", g=4),
                      ds(f"vdb{par}"), reads=[R_vdb[par]])

            def A4(tb):
                par = tb % 2
                b1, b2 = bankA(tb, 1), bankA(tb, 2)
                c = 8 + (tb % 4) * 2
                rc = R_sm[8 + tb % 4]
                P.op("act", ACT(JUNK[:, 0:256], PS[b1][:, 0:256], AF.Square, accum_out=SM[:, c:c + 1]),
                     reads=[R_PS[b1]], writes=[R_junk, rc])
                P.op("act", ACT(JUNK[:, 256:384], PS[b1][:, 256:384], AF.Square, accum_out=SM[:, c + 1:c + 2]),
                     reads=[R_PS[b1]], writes=[R_junk, rc])
                P.op("act", ACT(SM[:, c:c + 1], SM[:, c:c + 1], AF.Sqrt, scale=1.0 / 256, bias=EPS), reads=[rc], writes=[rc])
                P.op("act", ACT(SM[:, c + 1:c + 2], SM[:, c + 1:c + 2], AF.Sqrt, scale=1.0 / 128, bias=EPS), reads=[rc], writes=[rc])
                P.op("dve", RCP(SM[:, c:c + 2], SM[:, c:c + 2]), reads=[rc], writes=[rc])
                P.op("dve", STT(CQN[par][:, 0:256], PS[b1][:, 0:256], SM[:, c:c + 1], GQKV[:, 0:256], ALU.mult, ALU.mult),
                     reads=[R_PS[b1], rc, R_small], writes=[R_cqn[par]])
                P.op("dve", STT(CQN[par][:, 256:384], PS[b1][:, 256:384], SM[:, c + 1:c + 2], GQKV[:, 256:384], ALU.mult, ALU.mult),
                     reads=[R_PS[b1], rc, R_small], writes=[R_cqn[par]])
                x1, x2 = PS[b1][:, 384:400], PS[b1][:, 400:416]
                cos, sin = CS[:, tb, 0:16], CS[:, tb, 16:32]
                rt = RT[par]
                P.op("dve", TT(rt[:, 0:16], x1, cos, ALU.mult), reads=[R_PS[b1], R_cs], writes=[R_rt[par]])
                P.op("dve", TT(rt[:, 16:32], x2, sin, ALU.mult), reads=[R_PS[b1], R_cs], writes=[R_rt[par]])
                P.op("dve", TT(rt[:, 32:48], x2, cos, ALU.mult), reads=[R_PS[b1], R_cs], writes=[R_rt[par]])
                P.op("dve", TT(rt[:, 48:64], x1, sin, ALU.mult), reads=[R_PS[b1], R_cs], writes=[R_rt[par]])
                P.op("dve", TT(KPE[:, tb, 0:16], rt[:, 0:16], rt[:, 16:32], ALU.subtract), reads=[R_rt[par]], writes=[R_kpe[tb]])
                P.op("dve", TT(KPE[:, tb, 16:32], rt[:, 32:48], rt[:, 48:64], ALU.add), reads=[R_rt[par]], writes=[R_kpe[tb]])
                ptv = PS[b2][:].bitcast(BF16)
                for k3 in range(3):
                    P.op("pe", TR(ptv[:, 512 + k3 * 128:512 + (k3 + 1) * 128], CQN[par][:, k3 * 128:(k3 + 1) * 128], IDB[:]),
                         reads=[R_cqn[par], R_const], writes=[R_PS[b2]], signal=(k3 == 2))
                P.op("act", ACT(CQNT[:, :, tb * 128:(tb + 1) * 128], ptv[:, 512:896].rearrange("p (k t) -> p k t", k=3), AF.Copy),
                     reads=[R_PS[b2]], writes=[R_cqnt[tb]])

            import os
            pipeline([A1, A2, A3, A4][:int(os.environ.get('KA', '4'))], int(os.environ.get('KNTB', NTB)))
            if stop == "A":
                break

            def B1(tb):
                par = tb % 2
                tsl = slice(tb * 128, (tb + 1) * 128)
                for kc in range(2):
                    P.op("pe", MM(PS[0][:, 0:384], CQNT[:, kc, tsl], WUQ[:, kc, 0:384], kc == 0, kc == 1),
                         reads=[R_cqnt[tb], R_wuq], writes=[R_PS[0]], signal=False)
                    P.op("pe", MM(PS[1][:, 0:384], CQNT[:, kc, tsl], WUQ[:, kc, 384:768], kc == 0, kc == 1),
                         reads=[R_cqnt[tb], R_wuq], writes=[R_PS[1]], signal=(kc == 1))
                P.op("pe", MM(PS[2][:, :], CQNT[:, 2, tsl], WUKV[:, 0:512]), reads=[R_cqnt[tb], R_wuq], writes=[R_PS[2]], signal=False)
                P.op("pe", MM(PS[3][:, :], CQNT[:, 2, tsl], WUKV[:, 512:1024]), reads=[R_cqnt[tb], R_wuq], writes=[R_PS[3]])
                lvl = int(os.environ.get('KB1', '9'))
                if lvl < 2:
                    return
                P.op("act", ACT(VMB[par][:], PS[3][:, :], AF.Copy), reads=[R_PS[3]], writes=[R_vmb[par]])
                P.dma("pool", Vm[:, :, tb, :].rearrange("g p d -> p g d"), VMB[par][:].rearrange("p (g d) -> p g d", g=8),
                      ds(f"vmb{par}"), reads=[R_vmb[par]])
                if lvl < 3:
                    return
                P.op("act", ACT(KTM[par][:, :, 0:64], PS[2][:, :].rearrange("p (h d) -> p h d", h=8), AF.Copy),
                     reads=[R_PS[2]], writes=[R_ktm[par]])
                P.op("dve", CP(KTM[par][:, :, 64:96], KPE[:, tb:tb + 1, :].broadcast_to([128, 8, 32])),
                     reads=[R_kpe[tb]], writes=[R_ktm[par]])
                if lvl < 4:
                    return
                QR = HF[:, 512:768].rearrange("p (h d) -> p h d", h=8)
                C8 = HF[:, 768:896].rearrange("p (h d) -> p h d", h=8)
                S8 = HF[:, 896:1024].rearrange("p (h d) -> p h d", h=8)
                tq = [HF[:, a * 128:(a + 1) * 128].rearrange("p (h d) -> p h d", h=8) for a in range(4)]
                for hb in range(2):
                    qv = PS[hb][:, 0:384].rearrange("p (h d) -> p h d", h=4)
                    qo = QTM[par][:, hb * 4:(hb + 1) * 4, :]
                    P.op("act", ACT(qo[:, :, 0:64], qv[:, :, 0:64], AF.Copy), reads=[R_PS[hb]], writes=[R_qtm[par]])
                    P.op("act", ACT(QR[:, hb * 4:(hb + 1) * 4, :], qv[:, :, 64:96], AF.Copy), reads=[R_PS[hb]], writes=[R_hf])
                P.op("dve", CP(C8, CS[:, tb:tb + 1, 0:16].broadcast_to([128, 8, 16])), reads=[R_cs], writes=[R_hf])
                P.op("dve", CP(S8, CS[:, tb:tb + 1, 16:32].broadcast_to([128, 8, 16])), reads=[R_cs], writes=[R_hf])
                x1, x2 = QR[:, :, 0:16], QR[:, :, 16:32]
                P.op("dve", TT(tq[0], x1, C8, ALU.mult), reads=[R_hf], writes=[R_hf])
                P.op("dve", TT(tq[1], x2, S8, ALU.mult), reads=[R_hf], writes=[R_hf])
                P.op("dve", TT(tq[2], x2, C8, ALU.mult), reads=[R_hf], writes=[R_hf])
                P.op("dve", TT(tq[3], x1, S8, ALU.mult), reads=[R_hf], writes=[R_hf])
                P.op("dve", TT(QTM[par][:, :, 64:80], tq[0], tq[1], ALU.subtract), reads=[R_hf], writes=[R_qtm[par]])
                P.op("dve", TT(QTM[par][:, :, 80:96], tq[2], tq[3], ALU.add), reads=[R_hf], writes=[R_qtm[par]])

            def B2(tb):
                par = tb % 2
                tl = tb % 4
                pq = PS[4 + 2 * par][:].bitcast(BF16)
                pk = PS[5 + 2 * par][:].bitcast(BF16)
                bq, bk = 4 + 2 * par, 5 + 2 * par
                for h in range(8):
                    P.op("pe", TR(pq[0:96, h * 128:(h + 1) * 128], QTM[par][:, h, :], IDB[:]),
                         reads=[R_qtm[par], R_const], writes=[R_PS[bq]], signal=(h == 7))
                for h in range(8):
                    P.op("pe", TR(pk[0:96, h * 128:(h + 1) * 128], KTM[par][:, h, :], IDB[:]),
                         reads=[R_ktm[par], R_const], writes=[R_PS[bk]], signal=(h == 7))
                P.op("act", ACT(QST[:, :, tl * 128:(tl + 1) * 128], pq[0:96, :].rearrange("p (h t) -> p h t", h=8), AF.Copy),
                     reads=[R_PS[bq]], writes=[R_qst])
                P.op("dve", CP(KST[:, :, tl * 128:(tl + 1) * 128], pk[0:96, :].rearrange("p (h t) -> p h t", h=8)),
                     reads=[R_PS[bk]], writes=[R_kst])
                if tl == 3:
                    tc = tb // 4
                    P.dma("pool", QTm[:, :, tc * 512:(tc + 1) * 512].rearrange("h r t -> r h t"), QST[:], ds("qst"),
                          reads=[R_qst])
                    P.dma("pool", KTm[:, :, tc * 512:(tc + 1) * 512].rearrange("h r t -> r h t"), KST[:], ds("kst"),
                          reads=[R_kst])

            pipeline([B1, B2][:int(os.environ.get('KB', '2'))], int(os.environ.get('KNTB2', NTB)))
            if stop == "A2":
                break

            wbv = wb_b[l].rearrange("(k p) n -> p k n", p=128)
            groups = [(g * 4, min(49, g * 4 + 4)) for g in range(13)]
            tile_i = 0
            for gi, (j0, j1) in enumerate(groups):
                par = gi % 2
                ncol = (j1 - j0) * 128
                P.dma("sp", WS[par][:, :, 0:ncol], wbv[:, :, j0 * 128:j1 * 128], ds(f"ws{par}"),
                      reads=[R_W[l]["b"]], writes=[R_ws[par]])
                for tc in range(NTC):
                    tsl = slice(tc * 512, (tc + 1) * 512)
                    for j in range(j0, j1):
                        bk = tile_i % 8
                        evb = tile_i % 4
                        tile_i += 1
                        m = j - j0
                        for kc in range(8):
                            P.op("pe", MM(PS[bk][:, :], WS[par][:, kc, m * 128:(m + 1) * 128], hT[:, kc, tsl], kc == 0, kc == 7),
                                 reads=[R_ws[par]] + ([R_hT[tc * 4 + i] for i in range(4)] if kc == 0 else []),
                                 writes=[R_PS[bk]], signal=(kc == 7))
                        if j < 12:
                            fn, kw, dst = AF.Silu, {}, ZT[j * 128:(j + 1) * 128, tsl]
                        elif j < 16:
                            fn, kw, dst = AF.Copy, {"scale": 0.125}, QTs[(j - 12) * 128:(j - 11) * 128, tsl]
                        elif j == 16:
                            fn, kw, dst = AF.Copy, {}, KTs[:, tsl]
                        elif j < 21:
                            fn, kw, dst = AF.Copy, {"scale": 0.125}, QTd[(j - 17) * 128:(j - 16) * 128, tsl]
                        elif j < 25:
                            fn, kw, dst = AF.Copy, {}, KTd[(j - 21) * 128:(j - 20) * 128, tsl]
                        else:
                            fn, kw, dst = AF.Sigmoid, {}, GATES[(j - 25) * 128:(j - 24) * 128, tsl]
                        P.op("act", ACT(EV[evb][:], PS[bk][:, :], fn, **kw), reads=[R_PS[bk]], writes=[R_ev[evb]])
                        P.dma("pool", dst, EV[evb][:], ds(f"ev{evb}"), reads=[R_ev[evb]])
            P.barrier()
            if stop == "B":
                break

            for s2 in range(2):
                P.op("pool", MSET(VSL[s2][:, :, 64:128], 1.0), writes=[R_vsl[s2]])
                P.op("pool", MSET(QT1[s2][0:64, :], 0.0), writes=[R_qt1[s2]])
            heads = [("m", h) for h in range(8)] + [("s", h) for h in range(8)] + [("d", h) for h in range(4)]

            def load_head(hi):
                br, h = heads[hi]
                sl = hi % 2
                if br == "m":
                    P.dma("sp", QTS[sl][0:96, :], QTm[h], ds(f"qts{sl}"), writes=[R_qts[sl]])
                    P.dma("sp", KTS[sl][0:96, :], KTm[h], ds(f"kts{sl}"), writes=[R_kts[sl]])
                    for q4 in range(4):
                        P.dma("sp", VSL[sl][:, q4 * 8:(q4 + 1) * 8, 0:64], Vm[h][:, q4 * 8:(q4 + 1) * 8, :], ds(f"vsl{sl}"),
                              writes=[R_vsl[sl]])
                elif br == "s":
                    P.op("pool", MSET(QTS[sl][64:128, :], 0.0), writes=[R_qts[sl]])
                    P.op("pool", MSET(KTS[sl][64:128, :], 0.0), writes=[R_kts[sl]])
                    P.dma("sp", QTS[sl][0:64, :], QTs[h * 64:(h + 1) * 64, :], ds(f"qts{sl}"), writes=[R_qts[sl]])
                    P.dma("sp", KTS[sl][0:64, :], KTs[(h // 4) * 64:(h // 4 + 1) * 64, :], ds(f"kts{sl}"),
                          writes=[R_kts[sl]])
                    for q4 in range(4):
                        P.dma("sp", VSL[sl][:, q4 * 8:(q4 + 1) * 8, 0:64], Vs[h // 4][:, q4 * 8:(q4 + 1) * 8, :], ds(f"vsl{sl}"),
                              writes=[R_vsl[sl]])
                    P.dma("sp", BSK[sl], Bd[h], ds(f"bsk{sl}"), reads=[R_bd], writes=[R_bsk[sl]])
                    P.op("pool", CP(BH[sl], BSK[sl]), reads=[R_bsk[sl]], writes=[R_bh[sl]])
                    P.op("pool", TT(BSK[sl], BSK[sl], BH[sl], ALU.subtract), reads=[R_bsk[sl], R_bh[sl]], writes=[R_bsk[sl]])
                    P.op("pool", CP(BL[sl], BSK[sl]), reads=[R_bsk[sl]], writes=[R_bh[sl]])
                else:
                    P.op("pool", MSET(QTS[sl][64:128, :], 0.0), writes=[R_qts[sl]])
                    P.dma("sp", QTS[sl][0:64, :], QTd[h * 128:h * 128 + 64, :], ds(f"qts{sl}"), writes=[R_qts[sl]])
                    P.dma("sp", QT1[sl][64:128, :], QTd[h * 128 + 64:(h + 1) * 128, :], ds(f"qt1{sl}"), writes=[R_qt1[sl]])
                    P.dma("sp", KTS[sl], KTd[h * 128:(h + 1) * 128, :], ds(f"kts{sl}"), writes=[R_kts[sl]])
                    P.dma("sp", VSL[sl], Vd[h], ds(f"vsl{sl}"), writes=[R_vsl[sl]])
                    P.dma("sp", BSK[sl], Bd[8 + h], ds(f"bsk{sl}"), reads=[R_bd], writes=[R_bsk[sl]])

            tiles = []
            grp = 0
            for hi, (br, h) in enumerate(heads):
                for qc in range(NTC):
                    if br == "m":
                        kbs = list(range(NTB))
                    elif br == "s":
                        kbs = [kb for kb in range(4 * qc - 1, 4 * qc + 5) if 0 <= kb < NTB]
                    else:
                        kbs = list(range(NTB))
                    for comp in ((0, 1) if br == "d" else (0,)):
                        for ki, kb in enumerate(kbs):
                            o = kb - 4 * qc
                            if br == "m":
                                mode = ("none",)
                            elif -1 <= o <= 4:
                                mode = ("tile", (4 - o) * 128)
                            else:
                                mode = ("const", 0 if o < 0 else 1)
                            cr = (0, 512)
                            if br == "s":
                                cr = {-1: (0, 128), 0: (0, 256), 1: (0, 384), 2: (128, 512), 3: (256, 512), 4: (384, 512)}[o]
                            tiles.append(dict(hi=hi, br=br, h=h, qc=qc, comp=comp, kb=kb, first=(ki == 0),
                                              last=(ki == len(kbs) - 1), mode=mode, grp=grp, cr=cr))
                        grp += 1
            nt = len(tiles)
            SBK = [0, 1, 2, 3]
            MISC = 3
            ACCO = [4, 6, 5, 7]
            ACCS = [5, 7]

            def emit_qk(i):
                t = tiles[i]
                sl = t["hi"] % 2
                br = t["br"]
                kb, qc = t["kb"], t["qc"]
                bk = SBK[i % 4]
                if br == "m":
                    lhs, rhs, rq = KTS[sl][0:96, kb * 128:(kb + 1) * 128], QTS[sl][0:96, qc * 512:(qc + 1) * 512], R_qts[sl]
                elif t["comp"] == 1:
                    lhs, rhs, rq = KTS[sl][:, kb * 128:(kb + 1) * 128], QT1[sl][:, qc * 512:(qc + 1) * 512], R_qt1[sl]
                else:
                    lhs, rhs, rq = KTS[sl][:, kb * 128:(kb + 1) * 128], QTS[sl][:, qc * 512:(qc + 1) * 512], R_qts[sl]
                pb = i % 4
                if t["first"] and (br != "d" or t["comp"] == 1):
                    zz = t["grp"] % 2
                    tslz = slice(qc * 512, (qc + 1) * 512)
                    if br == "d":
                        P.dma("sp", ZS[zz], ZT[1024 + t["h"] * 128:1024 + (t["h"] + 1) * 128, tslz], ds(f"zs{zz}"), writes=[R_zs[zz]])
                    else:
                        bz = 0 if br == "m" else 512
                        P.dma("sp", ZS[zz][0:64, :], ZT[bz + t["h"] * 64:bz + (t["h"] + 1) * 64, tslz], ds(f"zs{zz}"),
                              writes=[R_zs[zz]])
                if br == "s":
                    off = t["mode"][1]
                    c0, c1 = t["cr"]
                    rhs = QTS[sl][:, qc * 512 + c0:qc * 512 + c1]
                    P.op("pe", MM(PS[bk][:, c0:c1], lhs, rhs, True, False), reads=[rq, R_kts[sl]], writes=[R_PS[bk]], signal=False)
                    P.op("pe", MM(PS[bk][:, c0:c1], IDB[:], BH[sl][:, off + c0:off + c1], False, False), reads=[R_bh[sl], R_const],
                         writes=[R_PS[bk]], signal=False)
                    P.op("pe", MM(PS[bk][:, c0:c1], IDB[:], BL[sl][:, off + c0:off + c1], False, True), reads=[R_bh[sl], R_const],
                         writes=[R_PS[bk]])
                    P.op("act", ACT(PB[pb][:, c0:c1], PS[bk][:, c0:c1], AF.Exp), reads=[R_PS[bk]], writes=[R_pb[pb]])
                    return
                P.op("pe", MM(PS[bk][:, :], lhs, rhs), reads=[rq, R_kts[sl]], writes=[R_PS[bk]])
                if t["mode"][0] == "none":
                    P.op("act", ACT(PB[pb], PS[bk][:, :], AF.Exp, scale=MLA_SCALE), reads=[R_PS[bk]], writes=[R_pb[pb]])
                elif t["mode"][0] == "const":
                    P.op("act", ACT(PB[pb], PS[bk][:, :], AF.Exp, bias=CB[:, t["mode"][1], 8 + t["h"]:9 + t["h"]], scale=1.0),
                         reads=[R_PS[bk], R_cb], writes=[R_pb[pb]])
                else:
                    off = t["mode"][1]
                    tm = i % 2
                    P.op("dve", TT(TMP[tm], PS[bk][:, :], BSK[sl][:, off:off + 512], ALU.add),
                         reads=[R_PS[bk], R_bsk[sl]], writes=[R_tmp[tm]])
                    P.op("act", ACT(PB[pb], TMP[tm], AF.Exp), reads=[R_tmp[tm]], writes=[R_pb[pb]])

            def emit_pv(i):
                t = tiles[i]
                sl = t["hi"] % 2
                br = t["br"]
                st_ = t["grp"] % 2 if br == "d" else t["grp"] % 4
                pb = i % 4
                kb = t["kb"]
                if t["first"] and t["qc"] == 0 and t["comp"] == 0 and t["hi"] + 1 < len(heads):
                    load_head(t["hi"] + 1)
                    if t["hi"] == 16 and l + 1 < depth:
                        issue_casts(l + 1)
                c0, c1 = t["cr"]
                P.op("pe", lambda e, o_=PS[ACCO[st_]][:, c0:c1], l_=VSL[sl][:, kb, :], r_=PB[pb][:, c0:c1], a=t["first"], b=t["last"]:
                     e.matmul(o_, lhsT=l_, rhs=r_, start=a, stop=b, skip_group_check=True),
                     reads=[R_pb[pb], R_vsl[sl]], writes=[R_PS[ACCO[st_]]])
                if br == "d":
                    P.op("pe", MM(PS[ACCS[st_]][:, :], ONESB[:], PB[pb], t["first"], t["last"]),
                         reads=[R_pb[pb], R_const], writes=[R_PS[ACCS[st_]]])
                if t["last"]:
                    finalize(t, st_)

            def finalize(t, st_):
                br, h, qc = t["br"], t["h"], t["qc"]
                tsl = slice(qc * 512, (qc + 1) * 512)
                ao = PS[ACCO[st_]]
                as_ = PS[ACCS[st_]] if br == "d" else None
                z = t["grp"] % 2
                if br in ("m", "s"):
                    boff = 0 if br == "m" else 512
                    rows = slice(boff + h * 64, boff + (h + 1) * 64)
                    if br == "s":
                        P.op("act", ACT(RS[z][0:64, :], ao[64:128, :], AF.Ln, bias=SINKE[64:128, h:h + 1], scale=1.0),
                             reads=[R_PS[ACCO[st_]], R_small], writes=[R_rs[z]])
                        P.op("act", ACT(RS[z][0:64, :], RS[z][0:64, :], AF.Exp, scale=-1.0), reads=[R_rs[z]], writes=[R_rs[z]])
                    else:
                        P.op("dve", RCP(RS[z][0:64, :], ao[64:128, :]), reads=[R_PS[ACCO[st_]]], writes=[R_rs[z]])
                    P.op("dve", TT(OF[z][0:64, :], ao[0:64, :], RS[z][0:64, :], ALU.mult),
                         reads=[R_PS[ACCO[st_]], R_rs[z]], writes=[R_of[z]])
                    P.op("pool", TT(GO[z][0:64, :], OF[z][0:64, :], ZS[z][0:64, :], ALU.mult),
                         reads=[R_of[z], R_zs[z]], writes=[R_go[z]])
                    P.dma("pool", GT[rows, tsl], GO[z][0:64, :], ds(f"go{z}"), reads=[R_go[z]])
                elif t["comp"] == 0:
                    P.op("act", ACT(RS[z], as_[:, :], AF.Ln), reads=[R_PS[ACCS[st_]]], writes=[R_rs[z]])
                    P.op("act", ACT(RS[z], RS[z], AF.Exp, scale=-1.0), reads=[R_rs[z]], writes=[R_rs[z]])
                    P.op("dve", TT(O0, ao[:, :], RS[z], ALU.mult), reads=[R_PS[ACCO[st_]], R_rs[z]], writes=[R_o0])
                else:
                    rows = slice(1024 + h * 128, 1024 + (h + 1) * 128)
                    P.op("act", ACT(RS[z], as_[:, :], AF.Ln), reads=[R_PS[ACCS[st_]]], writes=[R_rs[z]])
                    P.op("act", ACT(RS[z], RS[z], AF.Exp, scale=-1.0), reads=[R_rs[z]], writes=[R_rs[z]])
                    P.op("dve", TT(OF[z], ao[:, :], RS[z], ALU.mult), reads=[R_PS[ACCO[st_]], R_rs[z]], writes=[R_of[z]])
                    P.op("dve", STT(DD, OF[z], NEGLAM[:, 0:1], O0, ALU.mult, ALU.add),
                         reads=[R_of[z], R_o0, R_small], writes=[R_dd])
                    P.op("dve", TT(SQ, DD, DD, ALU.mult), reads=[R_dd], writes=[R_sq])
                    P.op("pool", CP(GO[z], SQ), reads=[R_sq], writes=[R_go[z]])
                    P.op("pool", TT(SQ, SQ, GO[z], ALU.subtract), reads=[R_sq, R_go[z]], writes=[R_sq])
                    P.op("pool", CP(SQL, SQ), reads=[R_sq], writes=[R_sql])

                    def rest(z=z, rows=rows, tsl=tsl):
                        P.op("pe", MM(PS[MISC][:, :], ONESB[:], GO[z], True, False), reads=[R_go[z], R_const],
                             writes=[R_PS[MISC]], signal=False)
                        P.op("pe", MM(PS[MISC][:, :], ONESB[:], SQL, False, True), reads=[R_sql, R_const], writes=[R_PS[MISC]])
                        P.op("act", ACT(RTT, PS[MISC][:, :], AF.Ln, scale=1.0 / 128, bias=EPS), reads=[R_PS[MISC]], writes=[R_rtt])
                        P.op("act", ACT(RTT, RTT, AF.Exp, scale=-0.5), reads=[R_rtt], writes=[R_rtt])
                        P.op("dve", STT(DD, DD, GSUBC[:, 0:1], RTT, ALU.mult, ALU.mult),
                             reads=[R_dd, R_rtt, R_small], writes=[R_dd])
                        P.op("dve", TT(GO[z], DD, ZS[z], ALU.mult), reads=[R_dd, R_zs[z]], writes=[R_go[z]])
                        P.dma("pool", GT[rows, tsl], GO[z], ds(f"go{z}"), reads=[R_go[z]])
                    deferred.append([16, rest])

            load_head(0)
            LA = 3
            deferred = []
            for i in range(nt + LA):
                if i < nt:
                    emit_qk(i)
                if i >= LA:
                    emit_pv(i - LA)
                for dfr in list(deferred):
                    dfr[0] -= 1
                    if dfr[0] <= 0:
                        deferred.remove(dfr)
                        dfr[1]()
            for dfr in deferred:
                dfr[1]()
            P.barrier()
            if stop == "C":
                break

            P.dma("sp", WO, wo_b[l].rearrange("(k p) n -> p k n", p=128), ds("wo"), reads=[R_W[l]["o"]], writes=[R_wo])
            P.dma("sp", WOUT, wout_b[l].rearrange("(k p) n -> p k n", p=128), ds("wo"), reads=[R_W[l]["o"]], writes=[R_wo])
            gtv = GT.rearrange("(j p) t -> p j t", p=128)
            gav = GATES.rearrange("(i f p) t -> p i f t", i=3, f=8, p=128)
            fi = 0

            def d_mix(tc):
                nonlocal fi
                tsl = slice(tc * 512, (tc + 1) * 512)
                gp = tc % 2
                if tc == 0:
                    P.dma("sp", GTL[0], gtv[:, :, 0:512], ds("gtl0"), writes=[R_gtl[0]])
                if tc + 1 < NTC:
                    gn = (tc + 1) % 2
                    P.dma("sp", GTL[gn], gtv[:, :, (tc + 1) * 512:(tc + 2) * 512], ds(f"gtl{gn}"), writes=[R_gtl[gn]])
                for fc in range(8):
                    g2 = fi % 2
                    g4 = fi % 4
                    fi += 1
                    P.dma("sp", GL[g4], gav[:, :, fc, tsl], ds(f"gl{g4}"), writes=[R_gl[g4]])
                    bo = [g2 * 3 + i for i in range(3)]
                    for i in range(3):
                        for k in range(4):
                            P.op("pe", MM(PS[bo[i]][:, :], WO[:, i * 4 + k, fc * 128:(fc + 1) * 128], GTL[gp][:, i * 4 + k, :],
                                          k == 0, k == 3),
                                 reads=[R_wo, R_gtl[gp]], writes=[R_PS[bo[i]]], signal=(k == 3))
                    mm = MX[g2]
                    for i in range(3):
                        P.op("dve", TT(mm[i], PS[bo[i]][:, :], GL[g4][:, i, :], ALU.mult),
                             reads=[R_PS[bo[i]], R_gl[g4]], writes=[R_mx[g2][i]])
                    P.op("pool", TT(mm[0], mm[0], mm[1], ALU.add), reads=[R_mx[g2][0], R_mx[g2][1]], writes=[R_mx[g2][0]])
                    P.op("pool", TT(MIXT[gp][:, fc, :], mm[0], mm[2], ALU.add), reads=[R_mx[g2][0], R_mx[g2][2]],
                         writes=[R_mixt[gp]])

            def d_out(tc):
                gp = tc % 2
                for t4 in range(4):
                    tb = tc * 4 + t4
                    par = tb % 2
                    P.dma("sp", XB[par][:], xsrc[tb * 128:(tb + 1) * 128, :], ds(f"xb{par}"), writes=[R_xb[par]])
                    by = [6, 7]
                    for hf in range(2):
                        for fc in range(8):
                            P.op("pe", MM(PS[by[hf]][:, :], MIXT[gp][:, fc, t4 * 128:(t4 + 1) * 128], WOUT[:, fc, hf * 512:(hf + 1) * 512],
                                          fc == 0, fc == 7),
                                 reads=[R_mixt[gp], R_wo], writes=[R_PS[by[hf]]], signal=(fc == 7))
                    c = (tb % 4) * 3
                    rc = R_sm[tb % 4]
                    for hf in range(2):
                        P.op("act", ACT(JUNK[:, 0:512], PS[by[hf]][:, :], AF.Square, accum_out=SM[:, c + hf:c + hf + 1]),
                             reads=[R_PS[by[hf]]], writes=[R_junk, rc])
                    P.op("dve", TT(SM[:, c + 2:c + 3], SM[:, c:c + 1], SM[:, c + 1:c + 2], ALU.add), reads=[rc], writes=[rc])
                    P.op("act", ACT(SM[:, c + 2:c + 3], SM[:, c + 2:c + 3], AF.Sqrt, scale=1.0 / D, bias=EPS), reads=[rc], writes=[rc])
                    P.op("dve", RCP(SM[:, c + 2:c + 3], SM[:, c + 2:c + 3]), reads=[rc], writes=[rc])
                    for hf in range(2):
                        hs = slice(hf * 512, (hf + 1) * 512)
                        P.op("dve", STT(YT[par][:, hs], PS[by[hf]][:, :], SM[:, c + 2:c + 3], GP[:, hs], ALU.mult, ALU.mult),
                             reads=[R_PS[by[hf]], rc, R_mod], writes=[R_yt[par]])
                    P.op("pool", TT(XB[par][:], XB[par][:], YT[par], ALU.add), reads=[R_xb[par], R_yt[par]], writes=[R_xb[par]])
                    P.dma("pool", out[tb * 128:(tb + 1) * 128, :], XB[par][:], ds(f"xbst{par}"), reads=[R_xb[par]])

            pipeline([d_mix, d_out], NTC)
            P.barrier()

        P.final_wait("sp")
        P.emit()
    return nc


def _t5_bucket_np(rel):
    import jax
    import jax.numpy as jnp
    try:
        dev = jax.devices("cpu")[0]
    except Exception:
        dev = None

    def f(rel):
        nb = 16
        max_exact = 8
        base = jnp.where(rel > 0, nb, 0)
        n = jnp.abs(rel)
        nf = jnp.maximum(n, 1).astype(jnp.float32)
        large = max_exact + (jnp.log(nf / max_exact) / math.log(128 / max_exact) * (nb - max_exact)).astype(jnp.int32)
        large = jnp.minimum(large, nb - 1)
        return base + jnp.where(n < max_exact, n, large)
    if dev is not None:
        with jax.default_device(dev):
            return np.asarray(f(jnp.asarray(rel, dtype=jnp.int32)))
    rel = np.asarray(rel, dtype=np.int64)
    n = np.abs(rel)
    nf = np.maximum(n, 1).astype(np.float32)
    large = 8 + (np.log(nf / np.float32(8)) / np.float32(math.log(16.0)) * np.float32(8)).astype(np.int32)
    large = np.minimum(large, 15)
    return np.where(rel > 0, 16, 0) + np.where(n < 8, n, large)


_CONSTS = {}


def _consts():
    if _CONSTS:
        return _CONSTS
    i = np.arange(1280)
    delta = 639 - i
    bucket = _t5_bucket_np(delta.astype(np.int32))
    oh = np.zeros((33, 1280), np.float32)
    oh[bucket, i] = 1.0
    oh[:, 1279] = 0.0
    oh[32, :] = (np.abs(delta) > 128).astype(np.float32)
    pos = np.arange(S, dtype=np.float32)
    inv = (np.float32(10000.0) ** (-np.arange(0, 32, 2, dtype=np.float32) / np.float32(32))).astype(np.float32)
    ang = (pos[:, None] * inv[None, :]).astype(np.float32)
    cs = np.concatenate([np.cos(ang), np.sin(ang)], axis=1).astype(np.float32)
    _CONSTS.update(oh=oh, cs=cs, ident=np.eye(128, dtype=np.float32))
    return _CONSTS


_NC_CACHE = {}


def kernel(x, c, w_ada, b_ada, g_pre, g_post, w_in, g_q, w_uq, g_kv, w_ukv, sink,
           lam_q1, lam_k1, lam_q2, lam_k2, g_sub, w_o_mla, w_o_swa, w_o_diff, w_out, rel_bias, _depth=DEPTH):
    f = lambda a: np.ascontiguousarray(np.asarray(a, dtype=np.float32))
    x, c, w_in = f(x), f(c), f(w_in)
    k = _consts()
    o = np.cumsum([0, 256, 128, 32, 512, 512, 128, 128, 512, 512, 512, 512, 512, 3072])
    col = lambda i: np.arange(o[i], o[i + 1])
    ia = np.concatenate([col(0), col(1), col(2), col(6), col(10)])
    ib = np.concatenate([col(3), col(7), col(11), col(4), col(5), col(8), col(9), col(12)])
    w_a = np.ascontiguousarray(w_in[:, :, ia])
    w_b = np.ascontiguousarray(w_in[:, :, ib])
    wk = f(w_ukv).reshape(DEPTH, 128, 8, 128)
    w_ukv2 = np.ascontiguousarray(np.concatenate([wk[..., :64].reshape(DEPTH, 128, 512),
                                                  wk[..., 64:].reshape(DEPTH, 128, 512)], axis=2))
    g_qkv = np.ascontiguousarray(np.concatenate([f(g_q), f(g_kv)], axis=1))
    lam = np.ascontiguousarray(np.concatenate([f(lam_q1), f(lam_k1), f(lam_q2), f(lam_k2)], axis=1))
    w_o = np.ascontiguousarray(np.concatenate([f(w_o_mla), f(w_o_swa), f(w_o_diff)], axis=1))
    rbx = np.zeros((33, 12), np.float32)
    rbx[:32] = f(rel_bias)
    rbx[32, :8] = -30000.0
    shared = dict(w_ada=f(w_ada), b_ada=f(b_ada), g_pre=f(g_pre), g_post=f(g_post), w_a=w_a, w_b=w_b,
                  w_uq=f(w_uq), w_ukv=w_ukv2, g_qkv=g_qkv, sink=f(sink), lam=lam, g_sub=f(g_sub),
                  w_o=w_o, w_out=f(w_out), rbx=rbx, oh=k["oh"], cs=k["cs"], ident=k["ident"])
    if _depth not in _NC_CACHE:
        import os
        _NC_CACHE[_depth] = build_program(_depth, os.environ.get("KSTOP"))
    nc = _NC_CACHE[_depth]
    in_maps = []
    for b in range(NCORES):
        m = dict(shared)
        m["x"] = x[b]
        m["c"] = c[b:b + 1]
        in_maps.append(m)
    res = run_bass_kernel_spmd(nc, in_maps, core_ids=list(range(NCORES)))
    return np.stack([np.asarray(r["out"], dtype=np.float32) for r in res.results], axis=0)
```

```python
import contextlib
import math
import numpy as np
import concourse.bass as bass
import concourse.mybir as mybir
from concourse.bass_utils import run_bass_kernel_spmd

F32 = mybir.dt.float32
BF16 = mybir.dt.bfloat16
AF = mybir.ActivationFunctionType
ALU = mybir.AluOpType

S = 4096
D = 1024
DEPTH = 4
NTB = 32
NTC = 8
EPS = 1e-6
NCORES = 8
WA_COLS = 1056
WB_COLS = 6272
MLA_SCALE = 96 ** -0.5

ENGS = ("pe", "act", "dve", "pool", "sp")
SAME_ENGINE_RAW = ("act", "dve", "pool")


class Res:
    __slots__ = ("name", "w", "r", "persistent")

    def __init__(self, name, persistent=False):
        self.name = name
        self.w = None
        self.r = {}
        self.persistent = persistent


class DSem:
    def __init__(self, sem, persistent=False):
        self.sem = sem
        self.count = 0
        self.persistent = persistent


class Prog:
    def __init__(self, nc, stack, n_eng_sems=30, n_dma_sems=70):
        self.nc = nc
        self.streams = {e: [] for e in ENGS}
        self.free_sems = [stack.enter_context(nc.semaphore(f"s{i}")) for i in range(n_eng_sems)]
        self.free_dsems = [stack.enter_context(nc.semaphore(f"d{i}")) for i in range(n_dma_sems)]
        self.dsems = []
        self.sem, self.cnt, self.last_sig = {}, {}, {}
        for e in ENGS:
            self._fresh(e)
        self.waited = {e: {} for e in ENGS}
        self.pending = {e: [] for e in ENGS}
        self.all_res = []

    def _fresh(self, e):
        self.sem[e] = self.free_sems.pop(0)
        self.cnt[e] = 0
        self.last_sig[e] = None

    def res(self, name, persistent=False):
        r = Res(name, persistent)
        self.all_res.append(r)
        return r

    def dsem(self, persistent=False):
        d = DSem(self.free_dsems.pop(0), persistent)
        self.dsems.append(d)
        return d

    def _deps(self, reads, writes):
        d = []
        for r in reads:
            if r.w is not None:
                d.append(r.w)
        for w in writes:
            if w.w is not None:
                d.append(w.w)
            d.extend(w.r.values())
        return d

    def _commit(self, ev, reads, writes):
        for r in reads:
            r.r[id(ev[0])] = ev
        for w in writes:
            w.w = ev
            w.r = {}

    def _waits(self, eng, deps, raw):
        waits = []
        for ev in deps:
            sem, val, src = ev
            if src == eng and (eng not in SAME_ENGINE_RAW or not any(ev is x for x in raw)):
                continue
            key = id(sem)
            if self.waited[eng].get(key, 0) >= val:
                continue
            self.waited[eng][key] = val
            waits.append((sem, val))
        return waits

    def op(self, eng, fn, reads=(), writes=(), signal=True):
        raw = [r.w for r in reads if r.w is not None]
        alld = self._deps(reads, writes) + self.pending[eng]
        self.pending[eng] = []
        waits = self._waits(eng, alld, raw)
        sem = self.sem[eng]
        if signal:
            self.cnt[eng] += 1
            ev = (sem, self.cnt[eng], eng)
            self.last_sig[eng] = ev
        else:
            ev = (sem, self.cnt[eng] + 1, eng)

        def closure(e, fn=fn, waits=waits, sem=sem, signal=signal):
            for s, v in waits:
                e.wait_ge(s, v)
            ins = fn(e)
            if signal:
                ins.then_inc(sem, 1)
        self.streams[eng].append(closure)
        self._commit(ev, reads, writes)
        return ev

    def dma(self, eng, out, in_, dsem, reads=(), writes=(), **kw):
        alld = [d for d in self._deps(reads, writes) if d[0] is not dsem.sem] + self.pending[eng]
        self.pending[eng] = []
        waits = self._waits(eng, alld, [d for d in alld if d[2] == eng])
        dsem.count += 16
        ev = (dsem.sem, dsem.count, "dma")

        def closure(e, out=out, in_=in_, waits=waits, sem=dsem.sem, kw=kw):
            for s, v in waits:
                e.wait_ge(s, v)
            e.dma_start(out=out, in_=in_, **kw).then_inc(sem, 16)
        self.streams[eng].append(closure)
        self._commit(ev, reads, writes)
        return ev

    def _all_events(self, include_persistent):
        evs = []
        for e in ENGS:
            if self.last_sig[e] is not None:
                evs.append(self.last_sig[e])
        for d in self.dsems:
            if d.count > 0 and (include_persistent or not d.persistent):
                evs.append((d.sem, d.count, "dma"))
        return evs

    def barrier(self):
        evs = self._all_events(False)
        for e in ENGS:
            self.pending[e] = self.pending[e] + evs
        for r in self.all_res:
            if not r.persistent:
                r.w = None
                r.r = {}
        for e in ENGS:
            if self.cnt[e] > 12000:
                self._fresh(e)

    def final_wait(self, eng="sp"):
        evs = self._all_events(True) + self.pending[eng]
        self.pending[eng] = []

        def closure(e, evs=evs):
            for s, v, _ in evs:
                e.wait_ge(s, v)
        self.streams[eng].append(closure)

    def emit(self):
        with self.nc.Block() as block:
            @block.tensor
            def _(e):
                for c in self.streams["pe"]:
                    c(e)

            @block.scalar
            def _(e):
                for c in self.streams["act"]:
                    c(e)

            @block.vector
            def _(e):
                for c in self.streams["dve"]:
                    c(e)

            @block.gpsimd
            def _(e):
                for c in self.streams["pool"]:
                    c(e)

            @block.sync
            def _(e):
                for c in self.streams["sp"]:
                    c(e)


def MM(out, lhsT, rhs, start=True, stop=True):
    return lambda e: e.matmul(out, lhsT=lhsT, rhs=rhs, start=start, stop=stop)


def TR(out, in_, ident):
    return lambda e: e.transpose(out, in_, ident)


def ACT(out, in_, func, **kw):
    return lambda e: e.activation(out=out, in_=in_, func=func, **kw)


def TT(out, a, b, op):
    return lambda e: e.tensor_tensor(out=out, in0=a, in1=b, op=op)


def TS(out, a, s1, s2, op0, op1=None):
    if op1 is None:
        return lambda e: e.tensor_scalar(out=out, in0=a, scalar1=s1, scalar2=None, op0=op0)
    return lambda e: e.tensor_scalar(out=out, in0=a, scalar1=s1, scalar2=s2, op0=op0, op1=op1)


def STT(out, in0, scalar, in1, op0, op1):
    return lambda e: e.scalar_tensor_tensor(out=out, in0=in0, scalar=scalar, in1=in1, op0=op0, op1=op1)


def CP(out, in_):
    return lambda e: e.tensor_copy(out=out, in_=in_)


def RCP(out, in_):
    return lambda e: e.reciprocal(out=out, in_=in_)


def RCPA(out, in_, scratch):
    return lambda e: e.reciprocal_approx_accurate(out=out, in_=in_, scratch=scratch)


def MSET(ap, v):
    return lambda e: e.memset(ap, v)


def pipeline(stages, n, skew=1):
    ns = len(stages)
    for step in range(n + (ns - 1) * skew):
        for si, st in enumerate(stages):
            i = step - si * skew
            if 0 <= i < n:
                st(i)


def build_program(depth=DEPTH, stop=None):
    nc = bass.Bass("TRN2", target_bir_lowering=False)

    def din(name, shape, dt=F32):
        return nc.dram_tensor(name, list(shape), dt, kind="ExternalInput").ap()

    def dint(name, shape, dt=BF16):
        return nc.dram_tensor(name, list(shape), dt, kind="Internal").ap()

    x_in = din("x", [S, D])
    c_in = din("c", [1, D])
    w_ada = din("w_ada", [DEPTH, D, 3 * D])
    b_ada = din("b_ada", [DEPTH, 3 * D])
    g_pre = din("g_pre", [DEPTH, D])
    g_post = din("g_post", [DEPTH, D])
    w_a = din("w_a", [DEPTH, D, WA_COLS])
    w_b = din("w_b", [DEPTH, D, WB_COLS])
    w_uq = din("w_uq", [DEPTH, 256, 768])
    w_ukv = din("w_ukv", [DEPTH, 128, 1024])
    g_qkv = din("g_qkv", [DEPTH, 384])
    sink = din("sink", [DEPTH, 8])
    lam = din("lam", [DEPTH, 256])
    g_sub = din("g_sub", [DEPTH, 128])
    w_o = din("w_o", [DEPTH, 1536, D])
    w_out = din("w_out", [DEPTH, D, D])
    rbx = din("rbx", [33, 12])
    oh = din("oh", [33, 1280])
    cs = din("cs", [S, 32])
    ident = din("ident", [128, 128])
    out = nc.dram_tensor("out", [S, D], F32, kind="ExternalOutput").ap()

    wada_b = dint("wada_b", [DEPTH, D, 3 * D])
    wa_b = dint("wa_b", [DEPTH, D, WA_COLS])
    wb_b = dint("wb_b", [DEPTH, D, WB_COLS])
    wuq_b = dint("wuq_b", [DEPTH, 256, 768])
    wukv_b = dint("wukv_b", [DEPTH, 128, 1024])
    wo_b = dint("wo_b", [DEPTH, 1536, D])
    wout_b = dint("wout_b", [DEPTH, D, D])
    ZT = dint("ZT", [1536, S])
    QTm = dint("QTm", [8, 96, S])
    KTm = dint("KTm", [8, 96, S])
    Vm = dint("Vm", [8, 128, NTB, 64])
    QTs = dint("QTs", [512, S])
    KTs = dint("KTs", [128, S])
    Vs = dint("Vs", [2, 128, NTB, 64])
    QTd = dint("QTd", [512, S])
    KTd = dint("KTd", [512, S])
    Vd = dint("Vd", [4, 128, NTB, 128])
    GATES = dint("GATES", [3072, S])
    GT = dint("GT", [1536, S])
    rr_d = dint("rr_d", [12, 1280], F32)
    Bd = dint("Bd", [12, 128, 1152], F32)

    with contextlib.ExitStack() as st:
        P = Prog(nc, st)

        def sb(name, shape, dt):
            return st.enter_context(nc.sbuf_tensor("sb_" + name, list(shape), dt))

        PS = [st.enter_context(nc.psum_tensor(f"ps{i}", [128, 512], F32)) for i in range(8)]
        R_PS = [P.res(f"ps{i}") for i in range(8)]

        IDB = sb("idb", [128, 128], BF16)
        ONESB = sb("onesb", [128, 128], BF16)
        CACT = sb("cact", [128, 8], F32)
        CREP = sb("crep", [128, 8, 128], BF16)
        CS = sb("cs", [128, NTB, 32], F32)
        CB = sb("cb", [128, 2, 12], F32)
        MOD = sb("mod", [128, 3 * D], F32)
        GQKV = sb("gqkv", [128, 384], F32)
        LSM = sb("lsm", [128, 8], F32)
        NEGLAM = sb("neglam", [128, 1], F32)
        GSUBC = sb("gsubc", [128, 1], F32)
        SINKE = sb("sinke", [128, 8], F32)
        WUQ = sb("wuq", [128, 2, 768], BF16)
        WUKV = sb("wukv", [128, 1024], BF16)
        KPE = sb("kpe", [128, NTB, 32], BF16)
        XB = [sb(f"xb{i}", [128, D], F32) for i in range(2)]
        HF = sb("hf", [128, D], F32)
        IDF = HF[:, 0:128]
        LAMT = HF[:, 128:384]
        HB = [sb(f"hb{i}", [128, D], BF16) for i in range(2)]
        JUNK = sb("junk", [128, D], BF16)
        WS = [sb(f"ws{i}", [128, 8, 512], BF16) for i in range(2)]
        EV = [sb(f"ev{i}", [128, 512], BF16) for i in range(4)]
        QST = sb("qst", [96, 8, 512], BF16)
        KST = sb("kst", [96, 8, 512], BF16)
        QTM = [sb(f"qtm{i}", [128, 8, 96], BF16) for i in range(2)]
        KTM = [sb(f"ktm{i}", [128, 8, 96], BF16) for i in range(2)]
        SM = sb("sm", [128, 64], F32)
        RT = [sb(f"rt{i}", [128, 64], F32) for i in range(2)]
        VSB = [sb(f"vsb{i}", [128, 128], BF16) for i in range(2)]
        VDB = [sb(f"vdb{i}", [128, 512], BF16) for i in range(2)]
        VMB = [sb(f"vmb{i}", [128, 512], BF16) for i in range(2)]
        CQN = [sb(f"cqn{i}", [128, 384], BF16) for i in range(2)]
        ARENA_BYTES = 112 * 1024
        ARENA = sb("arena", [128, ARENA_BYTES // 2], BF16)

        class Bump:
            def __init__(self):
                self.o = 0

            def get(self, nbytes, dt=BF16):
                nb = (nbytes + 63) // 64 * 64
                a = ARENA[:, self.o // 2:(self.o + nb) // 2]
                self.o += nb
                assert self.o <= ARENA_BYTES, self.o
                return a.bitcast(F32) if dt == F32 else a

        bA = Bump()
        hT = bA.get(65536).rearrange("p (k t) -> p k t", k=8)
        CQNT = bA.get(24576).rearrange("p (k t) -> p k t", k=3)
        W_A = bA.get(8 * WA_COLS * 2).rearrange("p (k n) -> p k n", k=8)
        bC = Bump()
        QTS = [bC.get(8192) for _ in range(2)]
        KTS = [bC.get(8192) for _ in range(2)]
        QT1 = [bC.get(8192) for _ in range(2)]
        VSL = [bC.get(8192).rearrange("p (k d) -> p k d", k=NTB) for _ in range(2)]
        BSK = [bC.get(1152 * 4, F32) for _ in range(2)]
        PB = [bC.get(1024) for _ in range(4)]
        TMP = [bC.get(2048, F32) for _ in range(2)]
        RS = [bC.get(2048, F32) for _ in range(2)]
        OF = [bC.get(2048, F32) for _ in range(2)]
        O0 = bC.get(2048, F32)
        DD = bC.get(2048, F32)
        SQ = bC.get(2048, F32)
        RTT = bC.get(2048, F32)
        ZS = [bC.get(1024) for _ in range(2)]
        GO = [bC.get(1024) for _ in range(2)]
        SQL = bC.get(1024)
        BH = [bC.get(2304) for _ in range(2)]
        BL = [bC.get(2304) for _ in range(2)]
        bD = Bump()
        WO = bD.get(12 * 1024 * 2).rearrange("p (k n) -> p k n", k=12)
        WOUT = bD.get(8 * 1024 * 2).rearrange("p (k n) -> p k n", k=8)
        GTL = [bD.get(12 * 512 * 2).rearrange("p (k t) -> p k t", k=12) for _ in range(2)]
        GL = [bD.get(3 * 512 * 2).rearrange("p (k t) -> p k t", k=3) for _ in range(4)]
        MIXT = [bD.get(8 * 512 * 2).rearrange("p (k t) -> p k t", k=8) for _ in range(2)]
        MX = [[bD.get(2048, F32) for _ in range(3)] for _ in range(2)]
        YT = [bD.get(4096, F32) for _ in range(2)]

        R = lambda n, p=False: P.res(n, p)
        R_W = [{g: R(f"w{l}{g}", True) for g in ("ada", "a", "b", "o")} for l in range(DEPTH)]
        R_const = R("const", True)
        R_bd = R("bd", True)
        R_cs = R("cs", True)
        R_crep = R("crep", True)
        R_cb = R("cb", True)
        R_mod = R("mod")
        R_small = R("small")
        R_wa = R("wa")
        R_wuq = R("wuq")
        R_xb = [R("xb0"), R("xb1")]
        R_hf = R("hf")
        R_hb = [R("hb0"), R("hb1")]
        R_junk = R("junk")
        R_sm = [R(f"sm{i}") for i in range(16)]
        R_hT = [R(f"hT{i}") for i in range(NTB)]
        R_cqnt = [R(f"cqnt{i}") for i in range(NTB)]
        R_kpe = [R(f"kpe{i}") for i in range(NTB)]
        R_cqn = [R("cqn0"), R("cqn1")]
        R_vsb = [R("vsb0"), R("vsb1")]
        R_vdb = [R("vdb0"), R("vdb1")]
        R_vmb = [R("vmb0"), R("vmb1")]
        R_ws = [R("ws0"), R("ws1")]
        R_ev = [R(f"ev{i}") for i in range(4)]
        R_qst = R("qst")
        R_kst = R("kst")
        R_qtm = [R("qtm0"), R("qtm1")]
        R_ktm = [R("ktm0"), R("ktm1")]
        R_rt = [R("rt0"), R("rt1")]
        R_kpef = R("kpef")
        R_dram = R("dram")
        R_qts = [R("qts0"), R("qts1")]
        R_kts = [R("kts0"), R("kts1")]
        R_qt1 = [R("qt10"), R("qt11")]
        R_vsl = [R("vsl0"), R("vsl1")]
        R_bsk = [R("bsk0"), R("bsk1")]
        R_pb = [R(f"pb{i}") for i in range(4)]
        R_tmp = [R("tmp0"), R("tmp1")]
        R_rs = [R("rs0"), R("rs1")]
        R_of = [R("of0"), R("of1")]
        R_o0 = R("o0")
        R_dd = R("dd")
        R_sq = R("sq")
        R_rtt = R("rtt")
        R_zs = [R("zs0"), R("zs1")]
        R_go = [R("go0"), R("go1")]
        R_sql = R("sql")
        R_bh = [R("bh0"), R("bh1")]
        R_wo = R("wo")
        R_gtl = [R("gtl0"), R("gtl1")]
        R_gl = [R(f"gl{i}") for i in range(4)]
        R_mixt = [R("mixt0"), R("mixt1")]
        R_mx = [[R(f"mx{a}{b}") for b in range(3)] for a in range(2)]
        R_yt = [R("yt0"), R("yt1")]

        DS = {}

        def ds(name, persistent=False):
            if name not in DS:
                DS[name] = P.dsem(persistent)
            return DS[name]

        last_cast = {}

        def cast(dst, src, l, g):
            n = 1
            for s_ in src.shape:
                n *= s_
            rows = n // 1024
            s2 = src.rearrange("a b -> (a b)").rearrange("(a b) -> a b", b=1024)
            d2 = dst.rearrange("a b -> (a b)").rearrange("(a b) -> a b", b=1024)
            for r0 in range(0, rows, 1024):
                r1 = min(rows, r0 + 1024)
                dn = f"cast{l}{g}" if l == 0 else f"cast{l}"
                last_cast[l] = P.dma("pool", d2[r0:r1, :], s2[r0:r1, :], ds(dn, True), writes=[R_W[l][g]])

        def issue_casts(l):
            cast(wada_b[l], w_ada[l], l, "ada")
            cast(wa_b[l], w_a[l], l, "a")
            cast(wuq_b[l], w_uq[l], l, "a")
            cast(wukv_b[l], w_ukv[l], l, "a")
            cast(wb_b[l], w_b[l], l, "b")
            cast(wo_b[l], w_o[l], l, "o")
            cast(wout_b[l], w_out[l], l, "o")
            if l > 0:
                for g in R_W[l]:
                    R_W[l][g].w = last_cast[l]

        issue_casts(0)

        P.dma("sp", IDF, ident, ds("c0"), writes=[R_const])
        P.op("dve", CP(IDB[:], IDF), reads=[R_const], writes=[R_const])
        P.op("dve", MSET(ONESB[:], 1.0), writes=[R_const])
        for q4 in range(4):
            P.dma("sp", CS[:, q4 * 8:(q4 + 1) * 8, :],
                  cs[q4 * 1024:(q4 + 1) * 1024, :].rearrange("(t p) f -> p t f", p=128),
                  ds("c1"), writes=[R_cs])
        P.dma("sp", CB[:, 0, :], rbx[15:16, :].broadcast_to([128, 12]), ds("c2"), writes=[R_cb])
        P.dma("sp", CB[:, 1, :], rbx[31:32, :].broadcast_to([128, 12]), ds("c2"), writes=[R_cb])
        P.dma("sp", CACT[:], c_in.rearrange("o (k p) -> p (o k)", p=128), ds("c3"), writes=[R_crep],
              allow_slow_non_contiguous=True)
        P.op("act", ACT(CACT[:], CACT[:], AF.Silu), reads=[R_crep], writes=[R_crep])
        P.op("dve", CP(CREP[:], CACT[:].unsqueeze(2).broadcast_to([128, 8, 128])), reads=[R_crep], writes=[R_crep])
        RBX = TMP[0][0:33, 0:12]
        OHS = QTS[0].bitcast(F32)[:, 0:1280]
        RRS = KTS[0].bitcast(F32)[:, 0:1280]
        R_t5 = R("t5")
        P.dma("sp", RBX, rbx, ds("c4"), writes=[R_t5])
        P.dma("sp", OHS[0:33, :], oh, ds("c4"), writes=[R_t5])
        RBH, RBL, RBX2 = PB[0][0:33, 0:12], PB[1][0:33, 0:12], TMP[1][0:33, 0:12]
        OHB = KTS[1][0:33, 0:1280]
        P.op("dve", CP(RBH, RBX), reads=[R_t5], writes=[R_t5])
        P.op("dve", TT(RBX2, RBX, RBH, ALU.subtract), reads=[R_t5], writes=[R_t5])
        P.op("dve", CP(RBL, RBX2), reads=[R_t5], writes=[R_t5])
        P.op("dve", CP(OHB, OHS[0:33, :]), reads=[R_t5], writes=[R_t5])
        for j, (c0, c1) in enumerate(((0, 512), (512, 1024), (1024, 1280))):
            P.op("pe", MM(PS[j][0:12, 0:c1 - c0], RBH, OHB[:, c0:c1], True, False), reads=[R_t5], writes=[R_PS[j]], signal=False)
            P.op("pe", MM(PS[j][0:12, 0:c1 - c0], RBL, OHB[:, c0:c1], False, True), reads=[R_t5], writes=[R_PS[j]])
            P.op("dve", CP(RRS[0:12, c0:c1], PS[j][0:12, 0:c1 - c0]), reads=[R_PS[j]], writes=[R_small])
        P.dma("sp", rr_d, RRS[0:12, :], ds("c5"), reads=[R_small], writes=[R_bd])
        for p in range(128):
            P.dma("sp", Bd[:, p, :], rr_d[:, 127 - p:127 - p + 1152], ds("bd", True),
                  reads=[R_bd] if p == 0 else [], writes=[R_bd] if p == 127 else [])
        R_bd.persistent = True
        P.barrier()
        layers = range(depth) if stop != "pro" else []

        for l in layers:
            xsrc = x_in if l == 0 else out
            lam_init = 0.8 - 0.6 * math.exp(-0.3 * l)

            for j in range(6):
                par = j % 2
                P.dma("sp", WS[par][:], wada_b[l].rearrange("(k p) n -> p k n", p=128)[:, :, j * 512:(j + 1) * 512],
                      ds(f"ws{par}"), reads=[R_W[l]["ada"]], writes=[R_ws[par]])
                P.dma("sp", XB[par][:, 0:512], b_ada[l:l + 1, j * 512:(j + 1) * 512].broadcast_to([128, 512]),
                      ds(f"xb{par}"), writes=[R_xb[par]])
                for kc in range(8):
                    P.op("pe", MM(PS[par][:, :], CREP[:, kc, :], WS[par][:, kc, :], kc == 0, kc == 7),
                         reads=[R_crep, R_ws[par]], writes=[R_PS[par]], signal=(kc == 7))
                P.op("dve", TT(MOD[:, j * 512:(j + 1) * 512], PS[par][:, :], XB[par][:, 0:512], ALU.add),
                     reads=[R_PS[par], R_xb[par]], writes=[R_mod])
            P.dma("sp", XB[0][:], g_pre[l:l + 1, :].broadcast_to([128, D]), ds("xb0"), writes=[R_xb[0]])
            P.dma("sp", XB[1][:], g_post[l:l + 1, :].broadcast_to([128, D]), ds("xb1"), writes=[R_xb[1]])
            P.op("dve", STT(MOD[:, D:2 * D], MOD[:, D:2 * D], 1.0, XB[0][:], ALU.add, ALU.mult),
                 reads=[R_mod, R_xb[0]], writes=[R_mod])
            P.op("dve", TT(MOD[:, 2 * D:3 * D], MOD[:, 2 * D:3 * D], XB[1][:], ALU.mult),
                 reads=[R_mod, R_xb[1]], writes=[R_mod])
            SH, G1, GP = MOD[:, 0:D], MOD[:, D:2 * D], MOD[:, 2 * D:3 * D]
            P.dma("sp", GQKV[:], g_qkv[l:l + 1, :].broadcast_to([128, 384]), ds("s0"), writes=[R_small])
            P.dma("sp", LAMT, lam[l:l + 1, :].broadcast_to([128, 256]), ds("s0"), writes=[R_small])
            P.dma("sp", SINKE[:], sink[l:l + 1, :].broadcast_to([128, 8]), ds("s0"), writes=[R_small])
            P.dma("sp", GSUBC[:], g_sub[l:l + 1, :].rearrange("o p -> p o"), ds("s0"), writes=[R_small])
            P.op("act", ACT(SINKE[:], SINKE[:], AF.Exp), reads=[R_small], writes=[R_small])
            P.op("dve", TS(GSUBC[:], GSUBC[:], float(1.0 - lam_init), None, ALU.mult), reads=[R_small], writes=[R_small])
            P.op("dve", TT(LAMT[:, 0:64], LAMT[:, 0:64], LAMT[:, 64:128], ALU.mult), reads=[R_small], writes=[R_small])
            P.op("dve", TT(LAMT[:, 128:192], LAMT[:, 128:192], LAMT[:, 192:256], ALU.mult), reads=[R_small], writes=[R_small])
            P.op("dve", lambda e: e.reduce_sum(out=LSM[:, 0:1], in_=LAMT[:, 0:64], axis=mybir.AxisListType.X),
                 reads=[R_small], writes=[R_small])
            P.op("dve", lambda e: e.reduce_sum(out=LSM[:, 1:2], in_=LAMT[:, 128:192], axis=mybir.AxisListType.X),
                 reads=[R_small], writes=[R_small])
            P.op("act", ACT(LSM[:, 2:4], LSM[:, 0:2], AF.Exp), reads=[R_small], writes=[R_small])
            P.op("dve", TT(LSM[:, 4:5], LSM[:, 3:4], LSM[:, 2:3], ALU.subtract), reads=[R_small], writes=[R_small])
            P.op("dve", TS(NEGLAM[:], LSM[:, 4:5], float(-lam_init), None, ALU.add), reads=[R_small], writes=[R_small])
            P.dma("sp", W_A, wa_b[l].rearrange("(k p) n -> p k n", p=128), ds("wa"), reads=[R_W[l]["a"]], writes=[R_wa])
            P.dma("sp", WUQ[:], wuq_b[l].rearrange("(k p) n -> p k n", p=128), ds("wuq"), reads=[R_W[l]["a"]], writes=[R_wuq])
            P.dma("sp", WUKV[:], wukv_b[l], ds("wuq"), reads=[R_W[l]["a"]], writes=[R_wuq])

            if stop == "setup":
                break
            def bankA(tb, i):
                return (tb % 2) * 4 + i

            def A1(tb):
                par = tb % 2
                P.dma("sp", XB[par][:], xsrc[tb * 128:(tb + 1) * 128, :], ds(f"xb{par}"), writes=[R_xb[par]])
                c = tb % 8
                P.op("act", ACT(JUNK[:], XB[par][:], AF.Square, accum_out=SM[:, c:c + 1]),
                     reads=[R_xb[par]], writes=[R_junk, R_sm[c]])
                P.op("act", ACT(SM[:, c:c + 1], SM[:, c:c + 1], AF.Sqrt, scale=1.0 / D, bias=EPS),
                     reads=[R_sm[c]], writes=[R_sm[c]])
                P.op("dve", RCP(SM[:, c:c + 1], SM[:, c:c + 1]), reads=[R_sm[c]], writes=[R_sm[c]])
                P.op("dve", STT(HF[:], XB[par][:], SM[:, c:c + 1], G1, ALU.mult, ALU.mult),
                     reads=[R_xb[par], R_sm[c], R_mod], writes=[R_hf])
                P.op("dve", TT(HB[par][:], HF[:], SH, ALU.add), reads=[R_hf, R_mod], writes=[R_hb[par]])

            def A2(tb):
                par = tb % 2
                b0 = bankA(tb, 0)
                ptv = PS[b0][:].bitcast(BF16)
                for kc in range(8):
                    P.op("pe", TR(ptv[:, kc * 128:(kc + 1) * 128], HB[par][:, kc * 128:(kc + 1) * 128], IDB[:]),
                         reads=[R_hb[par], R_const], writes=[R_PS[b0]], signal=(kc == 7))
                P.op("act", ACT(hT[:, :, tb * 128:(tb + 1) * 128], ptv.rearrange("p (k t) -> p k t", k=8), AF.Copy),
                     reads=[R_PS[b0]], writes=[R_hT[tb]])

            def A3(tb):
                par = tb % 2
                b1, b2, b3 = bankA(tb, 1), bankA(tb, 2), bankA(tb, 3)
                for kc in range(8):
                    lt = hT[:, kc, tb * 128:(tb + 1) * 128]
                    P.op("pe", MM(PS[b1][:, 0:416], lt, W_A[:, kc, 0:416], kc == 0, kc == 7),
                         reads=[R_hT[tb], R_wa], writes=[R_PS[b1]], signal=False)
                    P.op("pe", MM(PS[b2][:, 0:128], lt, W_A[:, kc, 416:544], kc == 0, kc == 7),
                         reads=[R_hT[tb], R_wa], writes=[R_PS[b2]], signal=False)
                    P.op("pe", MM(PS[b3][:, :], lt, W_A[:, kc, 544:1056], kc == 0, kc == 7),
                         reads=[R_hT[tb], R_wa], writes=[R_PS[b3]], signal=(kc == 7))
                P.op("act", ACT(VSB[par][:], PS[b2][:, 0:128], AF.Copy), reads=[R_PS[b2]], writes=[R_vsb[par]])
                P.dma("pool", Vs[:, :, tb, :].rearrange("g p d -> p g d"), VSB[par][:].rearrange("p (g d) -> p g d", g=2),
                      ds(f"vsb{par}"), reads=[R_vsb[par]])
                P.op("act", ACT(VDB[par][:], PS[b3][:, :], AF.Copy), reads=[R_PS[b3]], writes=[R_vdb[par]])
                P.dma("pool", Vd[:, :, tb, :].rearrange("g p d -> p g d"), VDB[par][:].rearrange("p (g d) -> p g d", g=4),
                      ds(f"vdb{par}"), reads=[R_vdb[par]])

            def A4(tb):
                par = tb % 2
                b1, b2 = bankA(tb, 1), bankA(tb, 2)
                c = 8 + (tb % 4) * 2
                rc = R_sm[8 + tb % 4]
                P.op("act", ACT(JUNK[:, 0:256], PS[b1][:, 0:256], AF.Square, accum_out=SM[:, c:c + 1]),
                     reads=[R_PS[b1]], writes=[R_junk, rc])
                P.op("act", ACT(JUNK[:, 256:384], PS[b1][:, 256:384], AF.Square, accum_out=SM[:, c + 1:c + 2]),
                     reads=[R_PS[b1]], writes=[R_junk, rc])
                P.op("act", ACT(SM[:, c:c + 1], SM[:, c:c + 1], AF.Sqrt, scale=1.0 / 256, bias=EPS), reads=[rc], writes=[rc])
                P.op("act", ACT(SM[:, c + 1:c + 2], SM[:, c + 1:c + 2], AF.Sqrt, scale=1.0 / 128, bias=EPS), reads=[rc], writes=[rc])
                P.op("dve", RCP(SM[:, c:c + 2], SM[:, c:c + 2]), reads=[rc], writes=[rc])
                P.op("dve", STT(CQN[par][:, 0:256], PS[b1][:, 0:256], SM[:, c:c + 1], GQKV[:, 0:256], ALU.mult, ALU.mult),
                     reads=[R_PS[b1], rc, R_small], writes=[R_cqn[par]])
                P.op("dve", STT(CQN[par][:, 256:384], PS[b1][:, 256:384], SM[:, c + 1:c + 2], GQKV[:, 256:384], ALU.mult, ALU.mult),
                     reads=[R_PS[b1], rc, R_small], writes=[R_cqn[par]])
                x1, x2 = PS[b1][:, 384:400], PS[b1][:, 400:416]
                cos, sin = CS[:, tb, 0:16], CS[:, tb, 16:32]
                rt = RT[par]
                P.op("dve", TT(rt[:, 0:16], x1, cos, ALU.mult), reads=[R_PS[b1], R_cs], writes=[R_rt[par]])
                P.op("dve", TT(rt[:, 16:32], x2, sin, ALU.mult), reads=[R_PS[b1], R_cs], writes=[R_rt[par]])
                P.op("dve", TT(rt[:, 32:48], x2, cos, ALU.mult), reads=[R_PS[b1], R_cs], writes=[R_rt[par]])
                P.op("dve", TT(rt[:, 48:64], x1, sin, ALU.mult), reads=[R_PS[b1], R_cs], writes=[R_rt[par]])
                P.op("dve", TT(KPE[:, tb, 0:16], rt[:, 0:16], rt[:, 16:32], ALU.subtract), reads=[R_rt[par]], writes=[R_kpe[tb]])
                P.op("dve", TT(KPE[:, tb, 16:32], rt[:, 32:48], rt[:, 48:64], ALU.add), reads=[R_rt[par]], writes=[R_kpe[tb]])
                ptv = PS[b2][:].bitcast(BF16)
                for k3 in range(3):
                    P.op("pe", TR(ptv[:, 512 + k3 * 128:512 + (k3 + 1) * 128], CQN[par][:, k3 * 128:(k3 + 1) * 128], IDB[:]),
                         reads=[R_cqn[par], R_const], writes=[R_PS[b2]], signal=(k3 == 2))
                P.op("act", ACT(CQNT[:, :, tb * 128:(tb + 1) * 128], ptv[:, 512:896].rearrange("p (k t) -> p k t", k=3), AF.Copy),
                     reads=[R_PS[b2]], writes=[R_cqnt[tb]])

            import os
            pipeline([A1, A2, A3, A4][:int(os.environ.get('KA', '4'))], int(os.environ.get('KNTB', NTB)))
            if stop == "A":
                break

            def B1(tb):
                par = tb % 2
                tsl = slice(tb * 128, (tb + 1) * 128)
                for kc in range(2):
                    P.op("pe", MM(PS[0][:, 0:384], CQNT[:, kc, tsl], WUQ[:, kc, 0:384], kc == 0, kc == 1),
                         reads=[R_cqnt[tb], R_wuq], writes=[R_PS[0]], signal=False)
                    P.op("pe", MM(PS[1][:, 0:384], CQNT[:, kc, tsl], WUQ[:, kc, 384:768], kc == 0, kc == 1),
                         reads=[R_cqnt[tb], R_wuq], writes=[R_PS[1]], signal=(kc == 1))
                P.op("pe", MM(PS[2][:, :], CQNT[:, 2, tsl], WUKV[:, 0:512]), reads=[R_cqnt[tb], R_wuq], writes=[R_PS[2]], signal=False)
                P.op("pe", MM(PS[3][:, :], CQNT[:, 2, tsl], WUKV[:, 512:1024]), reads=[R_cqnt[tb], R_wuq], writes=[R_PS[3]])
                lvl = int(os.environ.get('KB1', '9'))
                if lvl < 2:
                    return
                P.op("act", ACT(VMB[par][:], PS[3][:, :], AF.Copy), reads=[R_PS[3]], writes=[R_vmb[par]])
                P.dma("pool", Vm[:, :, tb, :].rearrange("g p d -> p g d"), VMB[par][:].rearrange("p (g d) -> p g d", g=8),
                      ds(f"vmb{par}"), reads=[R_vmb[par]])
                if lvl < 3:
                    return
                P.op("act", ACT(KTM[par][:, :, 0:64], PS[2][:, :].rearrange("p (h d) -> p h d", h=8), AF.Copy),
                     reads=[R_PS[2]], writes=[R_ktm[par]])
                P.op("dve", CP(KTM[par][:, :, 64:96], KPE[:, tb:tb + 1, :].broadcast_to([128, 8, 32])),
                     reads=[R_kpe[tb]], writes=[R_ktm[par]])
                if lvl < 4:
                    return
                QR = HF[:, 512:768].rearrange("p (h d) -> p h d", h=8)
                C8 = HF[:, 768:896].rearrange("p (h d) -> p h d", h=8)
                S8 = HF[:, 896:1024].rearrange("p (h d) -> p h d", h=8)
                tq = [HF[:, a * 128:(a + 1) * 128].rearrange("p (h d) -> p h d", h=8) for a in range(4)]
                for hb in range(2):
                    qv = PS[hb][:, 0:384].rearrange("p (h d) -> p h d", h=4)
                    qo = QTM[par][:, hb * 4:(hb + 1) * 4, :]
                    P.op("act", ACT(qo[:, :, 0:64], qv[:, :, 0:64], AF.Copy), reads=[R_PS[hb]], writes=[R_qtm[par]])
                    P.op("act", ACT(QR[:, hb * 4:(hb + 1) * 4, :], qv[:, :, 64:96], AF.Copy), reads=[R_PS[hb]], writes=[R_hf])
                P.op("dve", CP(C8, CS[:, tb:tb + 1, 0:16].broadcast_to([128, 8, 16])), reads=[R_cs], writes=[R_hf])
                P.op("dve", CP(S8, CS[:, tb:tb + 1, 16:32].broadcast_to([128, 8, 16])), reads=[R_cs], writes=[R_hf])
                x1, x2 = QR[:, :, 0:16], QR[:, :, 16:32]
                P.op("dve", TT(tq[0], x1, C8, ALU.mult), reads=[R_hf], writes=[R_hf])
                P.op("dve", TT(tq[1], x2, S8, ALU.mult), reads=[R_hf], writes=[R_hf])
                P.op("dve", TT(tq[2], x2, C8, ALU.mult), reads=[R_hf], writes=[R_hf])
                P.op("dve", TT(tq[3], x1, S8, ALU.mult), reads=[R_hf], writes=[R_hf])
                P.op("dve", TT(QTM[par][:, :, 64:80], tq[0], tq[1], ALU.subtract), reads=[R_hf], writes=[R_qtm[par]])
                P.op("dve", TT(QTM[par][:, :, 80:96], tq[2], tq[3], ALU.add), reads=[R_hf], writes=[R_qtm[par]])

            def B2(tb):
                par = tb % 2
                tl = tb % 4
                pq = PS[4 + 2 * par][:].bitcast(BF16)
                pk = PS[5 + 2 * par][:].bitcast(BF16)
                bq, bk = 4 + 2 * par, 5 + 2 * par
                for h in range(8):
                    P.op("pe", TR(pq[0:96, h * 128:(h + 1) * 128], QTM[par][:, h, :], IDB[:]),
                         reads=[R_qtm[par], R_const], writes=[R_PS[bq]], signal=(h == 7))
                for h in range(8):
                    P.op("pe", TR(pk[0:96, h * 128:(h + 1) * 128], KTM[par][:, h, :], IDB[:]),
                         reads=[R_ktm[par], R_const], writes=[R_PS[bk]], signal=(h == 7))
                P.op("act", ACT(QST[:, :, tl * 128:(tl + 1) * 128], pq[0:96, :].rearrange("p (h t) -> p h t", h=8), AF.Copy),
                     reads=[R_PS[bq]], writes=[R_qst])
                P.op("dve", CP(KST[:, :, tl * 128:(tl + 1) * 128], pk[0:96, :].rearrange("p (h t) -> p h t", h=8)),
                     reads=[R_PS[bk]], writes=[R_kst])
                if tl == 3:
                    tc = tb // 4
                    P.dma("pool", QTm[:, :, tc * 512:(tc + 1) * 512].rearrange("h r t -> r h t"), QST[:], ds("qst"),
                          reads=[R_qst])
                    P.dma("pool", KTm[:, :, tc * 512:(tc + 1) * 512].rearrange("h r t -> r h t"), KST[:], ds("kst"),
                          reads=[R_kst])

            pipeline([B1, B2][:int(os.environ.get('KB', '2'))], int(os.environ.get('KNTB2', NTB)))
            if stop == "A2":
                break

            wbv = wb_b[l].rearrange("(k p) n -> p k n", p=128)
            groups = [(g * 4, min(49, g * 4 + 4)) for g in range(13)]
            tile_i = 0
            for gi, (j0, j1) in enumerate(groups):
                par = gi % 2
                ncol = (j1 - j0) * 128
                P.dma("sp", WS[par][:, :, 0:ncol], wbv[:, :, j0 * 128:j1 * 128], ds(f"ws{par}"),
                      reads=[R_W[l]["b"]], writes=[R_ws[par]])
                for tc in range(NTC):
                    tsl = slice(tc * 512, (tc + 1) * 512)
                    for j in range(j0, j1):
                        bk = tile_i % 8
                        evb = tile_i % 4
                        tile_i += 1
                        m = j - j0
                        for kc in range(8):
                            P.op("pe", MM(PS[bk][:, :], WS[par][:, kc, m * 128:(m + 1) * 128], hT[:, kc, tsl], kc == 0, kc == 7),
                                 reads=[R_ws[par]] + ([R_hT[tc * 4 + i] for i in range(4)] if kc == 0 else []),
                                 writes=[R_PS[bk]], signal=(kc == 7))
                        if j < 12:
                            fn, kw, dst = AF.Silu, {}, ZT[j * 128:(j + 1) * 128, tsl]
                        elif j < 16:
                            fn, kw, dst = AF.Copy, {"scale": 0.125}, QTs[(j - 12) * 128:(j - 11) * 128, tsl]
                        elif j == 16:
                            fn, kw, dst = AF.Copy, {}, KTs[:, tsl]
                        elif j < 21:
                            fn, kw, dst = AF.Copy, {"scale": 0.125}, QTd[(j - 17) * 128:(j - 16) * 128, tsl]
                        elif j < 25:
                            fn, kw, dst = AF.Copy, {}, KTd[(j - 21) * 128:(j - 20) * 128, tsl]
                        else:
                            fn, kw, dst = AF.Sigmoid, {}, GATES[(j - 25) * 128:(j - 24) * 128, tsl]
                        P.op("act", ACT(EV[evb][:], PS[bk][:, :], fn, **kw), reads=[R_PS[bk]], writes=[R_ev[evb]])
                        P.dma("pool", dst, EV[evb][:], ds(f"ev{evb}"), reads=[R_ev[evb]])
            P.barrier()
            if stop == "B":
                break

            for s2 in range(2):
                P.op("pool", MSET(VSL[s2][:, :, 64:128], 1.0), writes=[R_vsl[s2]])
                P.op("pool", MSET(QT1[s2][0:64, :], 0.0), writes=[R_qt1[s2]])
            heads = [("m", h) for h in range(8)] + [("s", h) for h in range(8)] + [("d", h) for h in range(4)]

            def load_head(hi):
                br, h = heads[hi]
                sl = hi % 2
                if br == "m":
                    P.dma("sp", QTS[sl][0:96, :], QTm[h], ds(f"qts{sl}"), writes=[R_qts[sl]])
                    P.dma("sp", KTS[sl][0:96, :], KTm[h], ds(f"kts{sl}"), writes=[R_kts[sl]])
                    for q4 in range(4):
                        P.dma("sp", VSL[sl][:, q4 * 8:(q4 + 1) * 8, 0:64], Vm[h][:, q4 * 8:(q4 + 1) * 8, :], ds(f"vsl{sl}"),
                              writes=[R_vsl[sl]])
                elif br == "s":
                    if hi in (8, 9):
                        P.op("pool", MSET(QTS[sl][64:128, :], 0.0), writes=[R_qts[sl]])
                        P.op("pool", MSET(KTS[sl][64:128, :], 0.0), writes=[R_kts[sl]])
                    P.dma("sp", QTS[sl][0:64, :], QTs[h * 64:(h + 1) * 64, :], ds(f"qts{sl}"), writes=[R_qts[sl]])
                    P.dma("sp", KTS[sl][0:64, :], KTs[(h // 4) * 64:(h // 4 + 1) * 64, :], ds(f"kts{sl}"),
                          writes=[R_kts[sl]])
                    for q4 in range(4):
                        P.dma("sp", VSL[sl][:, q4 * 8:(q4 + 1) * 8, 0:64], Vs[h // 4][:, q4 * 8:(q4 + 1) * 8, :], ds(f"vsl{sl}"),
                              writes=[R_vsl[sl]])
                    P.dma("sp", BSK[sl], Bd[h], ds(f"bsk{sl}"), reads=[R_bd], writes=[R_bsk[sl]])
                else:
                    P.dma("sp", QTS[sl][0:64, :], QTd[h * 128:h * 128 + 64, :], ds(f"qts{sl}"), writes=[R_qts[sl]])
                    P.dma("sp", QT1[sl][64:128, :], QTd[h * 128 + 64:(h + 1) * 128, :], ds(f"qt1{sl}"), writes=[R_qt1[sl]])
                    P.dma("sp", KTS[sl], KTd[h * 128:(h + 1) * 128, :], ds(f"kts{sl}"), writes=[R_kts[sl]])
                    P.dma("sp", VSL[sl], Vd[h], ds(f"vsl{sl}"), writes=[R_vsl[sl]])
                    P.dma("sp", BSK[sl], Bd[8 + h], ds(f"bsk{sl}"), reads=[R_bd], writes=[R_bsk[sl]])

            def prep_head(hi):
                br, h = heads[hi]
                sl = hi % 2
                if br == "s":
                    P.op("pool", CP(BH[sl], BSK[sl]), reads=[R_bsk[sl]], writes=[R_bh[sl]])
                    P.op("pool", TT(BSK[sl], BSK[sl], BH[sl], ALU.subtract), reads=[R_bsk[sl], R_bh[sl]], writes=[R_bsk[sl]])
                    P.op("pool", CP(BL[sl], BSK[sl]), reads=[R_bsk[sl]], writes=[R_bh[sl]])

            tiles = []
            grp = 0
            for hi, (br, h) in enumerate(heads):
                for qc in range(NTC):
                    if br == "m":
                        kbs = list(range(NTB))
                    elif br == "s":
                        kbs = [kb for kb in range(4 * qc - 1, 4 * qc + 5) if 0 <= kb < NTB]
                    else:
                        kbs = list(range(NTB))
                    for comp in ((0, 1) if br == "d" else (0,)):
                        for ki, kb in enumerate(kbs):
                            o = kb - 4 * qc
                            if br == "m":
                                mode = ("none",)
                            elif -1 <= o <= 4:
                                mode = ("tile", (4 - o) * 128)
                            else:
                                mode = ("const", 0 if o < 0 else 1)
                            cr = (0, 512)
                            if br == "s":
                                cr = {-1: (0, 128), 0: (0, 256), 1: (0, 384), 2: (128, 512), 3: (256, 512), 4: (384, 512)}[o]
                            tiles.append(dict(hi=hi, br=br, h=h, qc=qc, comp=comp, kb=kb, first=(ki == 0),
                                              last=(ki == len(kbs) - 1), mode=mode, grp=grp, cr=cr))
                        grp += 1
            nt = len(tiles)
            SBK = [0, 1, 2, 3]
            MISC = 3
            ACCO = [4, 6, 5, 7]
            ACCS = [5, 7]

            def emit_qk(i):
                t = tiles[i]
                sl = t["hi"] % 2
                br = t["br"]
                kb, qc = t["kb"], t["qc"]
                bk = SBK[i % 4]
                if br == "m":
                    lhs, rhs, rq = KTS[sl][0:96, kb * 128:(kb + 1) * 128], QTS[sl][0:96, qc * 512:(qc + 1) * 512], R_qts[sl]
                elif t["comp"] == 1:
                    lhs, rhs, rq = KTS[sl][:, kb * 128:(kb + 1) * 128], QT1[sl][:, qc * 512:(qc + 1) * 512], R_qt1[sl]
                else:
                    lhs, rhs, rq = KTS[sl][:, kb * 128:(kb + 1) * 128], QTS[sl][:, qc * 512:(qc + 1) * 512], R_qts[sl]
                pb = i % 4
                if t["first"] and (br != "d" or t["comp"] == 1):
                    zz = t["grp"] % 2
                    tslz = slice(qc * 512, (qc + 1) * 512)
                    if br == "d":
                        P.dma("sp", ZS[zz], ZT[1024 + t["h"] * 128:1024 + (t["h"] + 1) * 128, tslz], ds(f"zs{zz}"), writes=[R_zs[zz]])
                    else:
                        bz = 0 if br == "m" else 512
                        P.dma("sp", ZS[zz][0:64, :], ZT[bz + t["h"] * 64:bz + (t["h"] + 1) * 64, tslz], ds(f"zs{zz}"),
                              writes=[R_zs[zz]])
                if br == "s":
                    off = t["mode"][1]
                    c0, c1 = t["cr"]
                    rhs = QTS[sl][:, qc * 512 + c0:qc * 512 + c1]
                    P.op("pe", MM(PS[bk][:, c0:c1], lhs, rhs, True, False), reads=[rq, R_kts[sl]], writes=[R_PS[bk]], signal=False)
                    P.op("pe", MM(PS[bk][:, c0:c1], IDB[:], BH[sl][:, off + c0:off + c1], False, False), reads=[R_bh[sl], R_const],
                         writes=[R_PS[bk]], signal=False)
                    P.op("pe", MM(PS[bk][:, c0:c1], IDB[:], BL[sl][:, off + c0:off + c1], False, True), reads=[R_bh[sl], R_const],
                         writes=[R_PS[bk]])
                    P.op("act", ACT(PB[pb][:, c0:c1], PS[bk][:, c0:c1], AF.Exp), reads=[R_PS[bk]], writes=[R_pb[pb]])
                    return
                P.op("pe", MM(PS[bk][:, :], lhs, rhs), reads=[rq, R_kts[sl]], writes=[R_PS[bk]])
                if t["mode"][0] == "none":
                    P.op("act", ACT(PB[pb], PS[bk][:, :], AF.Exp, scale=MLA_SCALE), reads=[R_PS[bk]], writes=[R_pb[pb]])
                elif t["mode"][0] == "const":
                    P.op("act", ACT(PB[pb], PS[bk][:, :], AF.Exp, bias=CB[:, t["mode"][1], 8 + t["h"]:9 + t["h"]], scale=1.0),
                         reads=[R_PS[bk], R_cb], writes=[R_pb[pb]])
                else:
                    off = t["mode"][1]
                    tm = i % 2
                    P.op("dve", TT(TMP[tm], PS[bk][:, :], BSK[sl][:, off:off + 512], ALU.add),
                         reads=[R_PS[bk], R_bsk[sl]], writes=[R_tmp[tm]])
                    P.op("act", ACT(PB[pb], TMP[tm], AF.Exp), reads=[R_tmp[tm]], writes=[R_pb[pb]])

            def emit_pv(i):
                t = tiles[i]
                sl = t["hi"] % 2
                br = t["br"]
                st_ = t["grp"] % 2 if br == "d" else t["grp"] % 4
                pb = i % 4
                kb = t["kb"]
                if t["first"] and t["qc"] == 0 and t["comp"] == 0 and t["hi"] + 1 < len(heads):
                    load_head(t["hi"] + 1)
                    if t["hi"] == 16 and l + 1 < depth:
                        issue_casts(l + 1)
                if t["first"] and t["qc"] == 4 and t["comp"] == 0 and t["hi"] + 1 < len(heads):
                    prep_head(t["hi"] + 1)
                c0, c1 = t["cr"]
                P.op("pe", lambda e, o_=PS[ACCO[st_]][:, c0:c1], l_=VSL[sl][:, kb, :], r_=PB[pb][:, c0:c1], a=t["first"], b=t["last"]:
                     e.matmul(o_, lhsT=l_, rhs=r_, start=a, stop=b, skip_group_check=True),
                     reads=[R_pb[pb], R_vsl[sl]], writes=[R_PS[ACCO[st_]]])
                if br == "d":
                    P.op("pe", MM(PS[ACCS[st_]][:, :], ONESB[:], PB[pb], t["first"], t["last"]),
                         reads=[R_pb[pb], R_const], writes=[R_PS[ACCS[st_]]])
                if t["last"]:
                    finalize(t, st_)

            def finalize(t, st_):
                br, h, qc = t["br"], t["h"], t["qc"]
                tsl = slice(qc * 512, (qc + 1) * 512)
                ao = PS[ACCO[st_]]
                as_ = PS[ACCS[st_]] if br == "d" else None
                z = t["grp"] % 2
                if br in ("m", "s"):
                    boff = 0 if br == "m" else 512
                    rows = slice(boff + h * 64, boff + (h + 1) * 64)
                    if br == "s":
                        P.op("act", ACT(RS[z][0:64, :], ao[64:128, :], AF.Ln, bias=SINKE[64:128, h:h + 1], scale=1.0),
                             reads=[R_PS[ACCO[st_]], R_small], writes=[R_rs[z]])
                        P.op("act", ACT(RS[z][0:64, :], RS[z][0:64, :], AF.Exp, scale=-1.0), reads=[R_rs[z]], writes=[R_rs[z]])
                    else:
                        P.op("dve", RCP(RS[z][0:64, :], ao[64:128, :]), reads=[R_PS[ACCO[st_]]], writes=[R_rs[z]])
                    P.op("dve", TT(OF[z][0:64, :], ao[0:64, :], RS[z][0:64, :], ALU.mult),
                         reads=[R_PS[ACCO[st_]], R_rs[z]], writes=[R_of[z]])
                    P.op("pool", TT(GO[z][0:64, :], OF[z][0:64, :], ZS[z][0:64, :], ALU.mult),
                         reads=[R_of[z], R_zs[z]], writes=[R_go[z]])
                    P.dma("pool", GT[rows, tsl], GO[z][0:64, :], ds(f"go{z}"), reads=[R_go[z]])
                elif t["comp"] == 0:
                    P.op("act", ACT(RS[z], as_[:, :], AF.Ln), reads=[R_PS[ACCS[st_]]], writes=[R_rs[z]])
                    P.op("act", ACT(RS[z], RS[z], AF.Exp, scale=-1.0), reads=[R_rs[z]], writes=[R_rs[z]])
                    P.op("dve", TT(O0, ao[:, :], RS[z], ALU.mult), reads=[R_PS[ACCO[st_]], R_rs[z]], writes=[R_o0])
                else:
                    rows = slice(1024 + h * 128, 1024 + (h + 1) * 128)
                    P.op("act", ACT(RS[z], as_[:, :], AF.Ln), reads=[R_PS[ACCS[st_]]], writes=[R_rs[z]])
                    P.op("act", ACT(RS[z], RS[z], AF.Exp, scale=-1.0), reads=[R_rs[z]], writes=[R_rs[z]])
                    P.op("dve", TT(OF[z], ao[:, :], RS[z], ALU.mult), reads=[R_PS[ACCO[st_]], R_rs[z]], writes=[R_of[z]])
                    P.op("dve", STT(DD, OF[z], NEGLAM[:, 0:1], O0, ALU.mult, ALU.add),
                         reads=[R_of[z], R_o0, R_small], writes=[R_dd])
                    P.op("dve", TT(SQ, DD, DD, ALU.mult), reads=[R_dd], writes=[R_sq])
                    P.op("pool", CP(GO[z], SQ), reads=[R_sq], writes=[R_go[z]])
                    P.op("pool", TT(SQ, SQ, GO[z], ALU.subtract), reads=[R_sq, R_go[z]], writes=[R_sq])
                    P.op("pool", CP(SQL, SQ), reads=[R_sq], writes=[R_sql])

                    def rest(z=z, rows=rows, tsl=tsl):
                        P.op("pe", MM(PS[MISC][:, :], ONESB[:], GO[z], True, False), reads=[R_go[z], R_const],
                             writes=[R_PS[MISC]], signal=False)
                        P.op("pe", MM(PS[MISC][:, :], ONESB[:], SQL, False, True), reads=[R_sql, R_const], writes=[R_PS[MISC]])
                        P.op("act", ACT(RTT, PS[MISC][:, :], AF.Ln, scale=1.0 / 128, bias=EPS), reads=[R_PS[MISC]], writes=[R_rtt])
                        P.op("act", ACT(RTT, RTT, AF.Exp, scale=-0.5), reads=[R_rtt], writes=[R_rtt])
                        P.op("dve", STT(DD, DD, GSUBC[:, 0:1], RTT, ALU.mult, ALU.mult),
                             reads=[R_dd, R_rtt, R_small], writes=[R_dd])
                        P.op("dve", TT(GO[z], DD, ZS[z], ALU.mult), reads=[R_dd, R_zs[z]], writes=[R_go[z]])
                        P.dma("pool", GT[rows, tsl], GO[z], ds(f"go{z}"), reads=[R_go[z]])
                    deferred.append([16, rest])

            load_head(0)
            LA = 3
            deferred = []
            for i in range(nt + LA):
                if i < nt:
                    emit_qk(i)
                if i >= LA:
                    emit_pv(i - LA)
                for dfr in list(deferred):
                    dfr[0] -= 1
                    if dfr[0] <= 0:
                        deferred.remove(dfr)
                        dfr[1]()
            for dfr in deferred:
                dfr[1]()
            P.barrier()
            if stop == "C":
                break

            P.dma("sp", WO, wo_b[l].rearrange("(k p) n -> p k n", p=128), ds("wo"), reads=[R_W[l]["o"]], writes=[R_wo])
            P.dma("sp", WOUT, wout_b[l].rearrange("(k p) n -> p k n", p=128), ds("wo"), reads=[R_W[l]["o"]], writes=[R_wo])
            gtv = GT.rearrange("(j p) t -> p j t", p=128)
            gav = GATES.rearrange("(i f p) t -> p i f t", i=3, f=8, p=128)
            fi = 0

            def d_mix(tc):
                nonlocal fi
                tsl = slice(tc * 512, (tc + 1) * 512)
                gp = tc % 2
                if tc == 0:
                    P.dma("sp", GTL[0], gtv[:, :, 0:512], ds("gtl0"), writes=[R_gtl[0]])
                if tc + 1 < NTC:
                    gn = (tc + 1) % 2
                    P.dma("sp", GTL[gn], gtv[:, :, (tc + 1) * 512:(tc + 2) * 512], ds(f"gtl{gn}"), writes=[R_gtl[gn]])
                for fc in range(8):
                    g2 = fi % 2
                    g4 = fi % 4
                    fi += 1
                    P.dma("sp", GL[g4], gav[:, :, fc, tsl], ds(f"gl{g4}"), writes=[R_gl[g4]])
                    bo = [g2 * 3 + i for i in range(3)]
                    for i in range(3):
                        for k in range(4):
                            P.op("pe", MM(PS[bo[i]][:, :], WO[:, i * 4 + k, fc * 128:(fc + 1) * 128], GTL[gp][:, i * 4 + k, :],
                                          k == 0, k == 3),
                                 reads=[R_wo, R_gtl[gp]], writes=[R_PS[bo[i]]], signal=(k == 3))
                    mm = MX[g2]
                    for i in range(3):
                        P.op("dve", TT(mm[i], PS[bo[i]][:, :], GL[g4][:, i, :], ALU.mult),
                             reads=[R_PS[bo[i]], R_gl[g4]], writes=[R_mx[g2][i]])
                    P.op("pool", TT(mm[0], mm[0], mm[1], ALU.add), reads=[R_mx[g2][0], R_mx[g2][1]], writes=[R_mx[g2][0]])
                    P.op("pool", TT(MIXT[gp][:, fc, :], mm[0], mm[2], ALU.add), reads=[R_mx[g2][0], R_mx[g2][2]],
                         writes=[R_mixt[gp]])

            def d_out(tc):
                gp = tc % 2
                for t4 in range(4):
                    tb = tc * 4 + t4
                    par = tb % 2
                    P.dma("sp", XB[par][:], xsrc[tb * 128:(tb + 1) * 128, :], ds(f"xb{par}"), writes=[R_xb[par]])
                    by = [6, 7]
                    for hf in range(2):
                        for fc in range(8):
                            P.op("pe", MM(PS[by[hf]][:, :], MIXT[gp][:, fc, t4 * 128:(t4 + 1) * 128], WOUT[:, fc, hf * 512:(hf + 1) * 512],
                                          fc == 0, fc == 7),
                                 reads=[R_mixt[gp], R_wo], writes=[R_PS[by[hf]]], signal=(fc == 7))
                    c = (tb % 4) * 3
                    rc = R_sm[tb % 4]
                    for hf in range(2):
                        P.op("act", ACT(JUNK[:, 0:512], PS[by[hf]][:, :], AF.Square, accum_out=SM[:, c + hf:c + hf + 1]),
                             reads=[R_PS[by[hf]]], writes=[R_junk, rc])
                    P.op("dve", TT(SM[:, c + 2:c + 3], SM[:, c:c + 1], SM[:, c + 1:c + 2], ALU.add), reads=[rc], writes=[rc])
                    P.op("act", ACT(SM[:, c + 2:c + 3], SM[:, c + 2:c + 3], AF.Sqrt, scale=1.0 / D, bias=EPS), reads=[rc], writes=[rc])
                    P.op("dve", RCP(SM[:, c + 2:c + 3], SM[:, c + 2:c + 3]), reads=[rc], writes=[rc])
                    for hf in range(2):
                        hs = slice(hf * 512, (hf + 1) * 512)
                        P.op("dve", STT(YT[par][:, hs], PS[by[hf]][:, :], SM[:, c + 2:c + 3], GP[:, hs], ALU.mult, ALU.mult),
                             reads=[R_PS[by[hf]], rc, R_mod], writes=[R_yt[par]])
                    P.op("pool", TT(XB[par][:], XB[par][:], YT[par], ALU.add), reads=[R_xb[par], R_yt[par]], writes=[R_xb[par]])
                    P.dma("pool", out[tb * 128:(tb + 1) * 128, :], XB[par][:], ds(f"xbst{par}"), reads=[R_xb[par]])

            pipeline([d_mix, d_out], NTC)
            P.barrier()

        P.final_wait("sp")
        P.emit()
    return nc


def _t5_bucket_np(rel):
    import jax
    import jax.numpy as jnp
    try:
        dev = jax.devices("cpu")[0]
    except Exception:
        dev = None

    def f(rel):
        nb = 16
        max_exact = 8
        base = jnp.where(rel > 0, nb, 0)
        n = jnp.abs(rel)
        nf = jnp.maximum(n, 1).astype(jnp.float32)
        large = max_exact + (jnp.log(nf / max_exact) / math.log(128 / max_exact) * (nb - max_exact)).astype(jnp.int32)
        large = jnp.minimum(large, nb - 1)
        return base + jnp.where(n < max_exact, n, large)
    if dev is not None:
        with jax.default_device(dev):
            return np.asarray(f(jnp.asarray(rel, dtype=jnp.int32)))
    rel = np.asarray(rel, dtype=np.int64)
    n = np.abs(rel)
    nf = np.maximum(n, 1).astype(np.float32)
    large = 8 + (np.log(nf / np.float32(8)) / np.float32(math.log(16.0)) * np.float32(8)).astype(np.int32)
    large = np.minimum(large, 15)
    return np.where(rel > 0, 16, 0) + np.where(n < 8, n, large)


_CONSTS = {}


def _consts():
    if _CONSTS:
        return _CONSTS
    i = np.arange(1280)
    delta = 639 - i
    bucket = _t5_bucket_np(delta.astype(np.int32))
    oh = np.zeros((33, 1280), np.float32)
    oh[bucket, i] = 1.0
    oh[:, 1279] = 0.0
    oh[32, :] = (np.abs(delta) > 128).astype(np.float32)
    pos = np.arange(S, dtype=np.float32)
    inv = (np.float32(10000.0) ** (-np.arange(0, 32, 2, dtype=np.float32) / np.float32(32))).astype(np.float32)
    ang = (pos[:, None] * inv[None, :]).astype(np.float32)
    cs = np.concatenate([np.cos(ang), np.sin(ang)], axis=1).astype(np.float32)
    _CONSTS.update(oh=oh, cs=cs, ident=np.eye(128, dtype=np.float32))
    return _CONSTS


_NC_CACHE = {}


def kernel(x, c, w_ada, b_ada, g_pre, g_post, w_in, g_q, w_uq, g_kv, w_ukv, sink,
           lam_q1, lam_k1, lam_q2, lam_k2, g_sub, w_o_mla, w_o_swa, w_o_diff, w_out, rel_bias, _depth=DEPTH):
    f = lambda a: np.ascontiguousarray(np.asarray(a, dtype=np.float32))
    x, c, w_in = f(x), f(c), f(w_in)
    k = _consts()
    o = np.cumsum([0, 256, 128, 32, 512, 512, 128, 128, 512, 512, 512, 512, 512, 3072])
    col = lambda i: np.arange(o[i], o[i + 1])
    ia = np.concatenate([col(0), col(1), col(2), col(6), col(10)])
    ib = np.concatenate([col(3), col(7), col(11), col(4), col(5), col(8), col(9), col(12)])
    w_a = np.ascontiguousarray(w_in[:, :, ia])
    w_b = np.ascontiguousarray(w_in[:, :, ib])
    wk = f(w_ukv).reshape(DEPTH, 128, 8, 128)
    w_ukv2 = np.ascontiguousarray(np.concatenate([wk[..., :64].reshape(DEPTH, 128, 512),
                                                  wk[..., 64:].reshape(DEPTH, 128, 512)], axis=2))
    g_qkv = np.ascontiguousarray(np.concatenate([f(g_q), f(g_kv)], axis=1))
    lam = np.ascontiguousarray(np.concatenate([f(lam_q1), f(lam_k1), f(lam_q2), f(lam_k2)], axis=1))
    w_o = np.ascontiguousarray(np.concatenate([f(w_o_mla), f(w_o_swa), f(w_o_diff)], axis=1))
    rbx = np.zeros((33, 12), np.float32)
    rbx[:32] = f(rel_bias)
    rbx[32, :8] = -30000.0
    shared = dict(w_ada=f(w_ada), b_ada=f(b_ada), g_pre=f(g_pre), g_post=f(g_post), w_a=w_a, w_b=w_b,
                  w_uq=f(w_uq), w_ukv=w_ukv2, g_qkv=g_qkv, sink=f(sink), lam=lam, g_sub=f(g_sub),
                  w_o=w_o, w_out=f(w_out), rbx=rbx, oh=k["oh"], cs=k["cs"], ident=k["ident"])
    if _depth not in _NC_CACHE:
        import os
        _NC_CACHE[_depth] = build_program(_depth, os.environ.get("KSTOP"))
    nc = _NC_CACHE[_depth]
    in_maps = []
    for b in range(NCORES):
        m = dict(shared)
        m["x"] = x[b]
        m["c"] = c[b:b + 1]
        in_maps.append(m)
    res = run_bass_kernel_spmd(nc, in_maps, core_ids=list(range(NCORES)))
    return np.stack([np.asarray(r["out"], dtype=np.float32) for r in res.results], axis=0)
```
